# Optimizing a Trainium2 kernel written in Bass

```python
import math
import jax, jax.numpy as jnp
from jax import lax
import numpy as np

D_MODEL = 1024
BATCH = 8
SEQ = 2048
DEPTH = 1
DEC_BATCH = 128
DEC_SEQ = 4
PAST_LEN = 16384
PAGE_SIZE = 128

MIX_WIDTH = D_MODEL
S5_WIDTH = MIX_WIDTH // 2
S5_GROUP = 16
S5_N_GROUPS = S5_WIDTH // S5_GROUP
S5_STATE = 64
GLA_WIDTH = MIX_WIDTH - S5_WIDTH
GLA_HEADS = 4
GLA_DV = GLA_WIDTH // GLA_HEADS
GLA_DK = GLA_DV // 2
GLA_QK = GLA_HEADS * GLA_DK
GLA_GATE_RANK = 16
GLA_GATE_NORM = 16.0
GLA_CHUNK = 64
D_FF = ((8 * D_MODEL // 3 + 127) // 128) * 128
CONV_W = 3
NORM_EPS = 1e-6
IN_COLS = S5_WIDTH + 2 * GLA_QK + 2 * GLA_WIDTH + GLA_GATE_RANK

kernel_name = 'hybrid_s5_gla_convffn_step'


def rms_norm(x, g):
    xf = x.astype(jnp.float32)
    out = xf * lax.rsqrt(jnp.mean(xf * xf, axis=-1, keepdims=True) + NORM_EPS)
    return (out * g.astype(jnp.float32)).astype(x.dtype)


def _s5_combine(e1, e2):
    a1r, a1i, b1r, b1i = e1
    a2r, a2i, b2r, b2i = e2
    ar = a2r * a1r - a2i * a1i
    ai = a2r * a1i + a2i * a1r
    br = a2r * b1r - a2i * b1i + b2r
    bi = a2r * b1i + a2i * b1r + b2i
    return (ar, ai, br, bi)


def s5_mixer(u, s0_re, s0_im, A_re, A_im, B_re, B_im, C_re, C_im, D, log_step):
    f32 = jnp.float32
    A_re = A_re.astype(f32); A_im = A_im.astype(f32)
    B_re = B_re.astype(f32); B_im = B_im.astype(f32)
    C_re = C_re.astype(f32); C_im = C_im.astype(f32)
    dt = jnp.exp(log_step.astype(f32))[:, None]
    mag = jnp.exp(A_re * dt)
    lam_r = mag * jnp.cos(A_im * dt)
    lam_i = mag * jnp.sin(A_im * dt)
    den = A_re * A_re + A_im * A_im
    nr = lam_r - 1.0
    ni = lam_i
    cr = (nr * A_re + ni * A_im) / den
    ci = (ni * A_re - nr * A_im) / den
    Bb_r = cr[..., None] * B_re - ci[..., None] * B_im
    Bb_i = cr[..., None] * B_im + ci[..., None] * B_re
    bu_r = jnp.einsum('ntgc,gpc->ntgp', u, Bb_r)
    bu_i = jnp.einsum('ntgc,gpc->ntgp', u, Bb_i)
    a_r = jnp.broadcast_to(lam_r, bu_r.shape)
    a_i = jnp.broadcast_to(lam_i, bu_i.shape)
    ar, ai, br, bi = lax.associative_scan(_s5_combine, (a_r, a_i, bu_r, bu_i), axis=1)
    s0r = s0_re.astype(f32)[:, None]
    s0i = s0_im.astype(f32)[:, None]
    s_r = ar * s0r - ai * s0i + br
    s_i = ar * s0i + ai * s0r + bi
    y = (jnp.einsum('ntgp,gcp->ntgc', s_r, C_re)
         - jnp.einsum('ntgp,gcp->ntgc', s_i, C_im)
         + D.astype(f32) * u)
    return y, s_r[:, -1], s_i[:, -1]


def gla_mixer(q, k, v, logf, S0):
    N, T, H, DK = q.shape
    DV = v.shape[-1]
    L = math.gcd(T, GLA_CHUNK)
    nC = T // L

    def to_chunks(a):
        return a.reshape(N, nC, L, H, a.shape[-1]).transpose(1, 0, 3, 2, 4)

    mask = jnp.tril(jnp.ones((L, L), dtype=bool))

    def step(S, c):
        qc, kc, vc, gc = c
        b = jnp.cumsum(gc, axis=-2)
        bL = b[..., -1:, :]
        qe = qc * jnp.exp(b)
        ke = kc * jnp.exp(-b)
        att = jnp.where(mask, jnp.einsum('nhik,nhjk->nhij', qe, ke), 0.0)
        o = jnp.einsum('nhij,nhjv->nhiv', att, vc) + jnp.einsum('nhik,nhkv->nhiv', qe, S)
        S_new = (jnp.exp(bL[..., 0, :])[..., None] * S
                 + jnp.einsum('nhjk,nhjv->nhkv', kc * jnp.exp(bL - b), vc))
        return S_new, o

    S, o = lax.scan(step, S0.astype(jnp.float32),
                    (to_chunks(q), to_chunks(k), to_chunks(v), to_chunks(logf)))
    o = o.transpose(1, 0, 3, 2, 4).reshape(N, T, H, DV)
    return o, S


def decoder_layer(x, s5_re0, s5_im0, gla_S0, conv0,
                  attn_norm, w_in, s5_A_re, s5_A_im, s5_B_re, s5_B_im, s5_C_re, s5_C_im,
                  s5_D, s5_log_step, w_glu, b_glu, s5_out_norm, w_gate_up, b_gate,
                  gla_out_norm, w_o, ffn_norm, w_up, conv_w, conv_b, w_down):
    f32 = jnp.float32
    N, T, _ = x.shape
    h = rms_norm(x, attn_norm)
    proj = h @ w_in
    c1 = S5_WIDTH
    c2 = c1 + GLA_QK
    c3 = c2 + GLA_QK
    c4 = c3 + GLA_WIDTH
    c5 = c4 + GLA_WIDTH
    u, q, k, v, g, z = jnp.split(proj, [c1, c2, c3, c4, c5], axis=-1)

    y5, s5_re, s5_im = s5_mixer(u.reshape(N, T, S5_N_GROUPS, S5_GROUP).astype(f32),
                                s5_re0, s5_im0, s5_A_re, s5_A_im, s5_B_re, s5_B_im,
                                s5_C_re, s5_C_im, s5_D, s5_log_step)
    y5 = jax.nn.gelu(y5.reshape(N, T, S5_WIDTH))
    y5 = y5 * jax.nn.sigmoid(y5 @ w_glu.astype(f32) + b_glu.astype(f32))
    y5 = rms_norm(y5, s5_out_norm).astype(x.dtype)

    logf = jax.nn.log_sigmoid((z @ w_gate_up + b_gate).astype(f32)) / GLA_GATE_NORM
    qh = q.reshape(N, T, GLA_HEADS, GLA_DK).astype(f32) * (GLA_DK ** -0.5)
    kh = k.reshape(N, T, GLA_HEADS, GLA_DK).astype(f32)
    vh = v.reshape(N, T, GLA_HEADS, GLA_DV).astype(f32)
    o, gla_S = gla_mixer(qh, kh, vh, logf.reshape(N, T, GLA_HEADS, GLA_DK), gla_S0)
    o = rms_norm(o, gla_out_norm).reshape(N, T, GLA_WIDTH)
    o = (o * jax.nn.silu(g.astype(f32))).astype(x.dtype)

    x = x + jnp.concatenate([y5, o], axis=-1) @ w_o

    h2 = rms_norm(x, ffn_norm)
    a, b = jnp.split(h2 @ w_up, 2, axis=-1)
    ext = jnp.concatenate([conv0.astype(a.dtype), a], axis=1)
    a_conv = conv_b
    for w in range(CONV_W):
        a_conv = a_conv + conv_w[w] * ext[:, w:w + T]
    x = x + (jax.nn.silu(a_conv) * b) @ w_down
    conv_new = ext[:, -(CONV_W - 1):]
    return x, s5_re, s5_im, gla_S, conv_new


def setup_inputs(seed: int = 0) -> dict:
    key = jax.random.key(seed)
    ks = jax.random.split(key, 32)
    f32 = jnp.float32

    def nrm(k, shape, scale):
        return jax.random.normal(k, shape, f32) * scale

    def gain(k, shape):
        return 1.0 + 0.02 * jax.random.normal(k, shape, f32)

    n_idx = jnp.arange(S5_STATE, dtype=f32)
    G, P, C = S5_N_GROUPS, S5_STATE, S5_GROUP
    return {
        'x_prompt': nrm(ks[0], (BATCH, SEQ, D_MODEL), 1.0),
        'x_sample': nrm(ks[1], (DEC_BATCH, DEC_SEQ, D_MODEL), 1.0),
        'state_s5_re': nrm(ks[2], (DEPTH, DEC_BATCH, G, P), 0.5),
        'state_s5_im': nrm(ks[3], (DEPTH, DEC_BATCH, G, P), 0.5),
        'state_gla': nrm(ks[4], (DEPTH, DEC_BATCH, GLA_HEADS, GLA_DK, GLA_DV), 0.5),
        'state_conv': nrm(ks[5], (DEPTH, DEC_BATCH, CONV_W - 1, D_FF), 1.0),
        'attn_norm': gain(ks[6], (DEPTH, D_MODEL)),
        'w_in': nrm(ks[7], (DEPTH, D_MODEL, IN_COLS), D_MODEL ** -0.5),
        's5_A_re': -0.5 + 0.01 * jax.random.normal(ks[8], (DEPTH, G, P), f32),
        's5_A_im': math.pi * n_idx + 0.01 * jax.random.normal(ks[9], (DEPTH, G, P), f32),
        's5_B_re': nrm(ks[10], (DEPTH, G, P, C), (2.0 * C) ** -0.5),
        's5_B_im': nrm(ks[11], (DEPTH, G, P, C), (2.0 * C) ** -0.5),
        's5_C_re': nrm(ks[12], (DEPTH, G, C, P), (2.0 * P) ** -0.5),
        's5_C_im': nrm(ks[13], (DEPTH, G, C, P), (2.0 * P) ** -0.5),
        's5_D': nrm(ks[14], (DEPTH, G, C), 1.0),
        's5_log_step': jax.random.uniform(ks[15], (DEPTH, G), f32, math.log(1e-3), math.log(1e-1)),
        'w_glu': nrm(ks[16], (DEPTH, S5_WIDTH, S5_WIDTH), S5_WIDTH ** -0.5),
        'b_glu': nrm(ks[17], (DEPTH, S5_WIDTH), 0.01),
        's5_out_norm': gain(ks[18], (DEPTH, S5_WIDTH)),
        'w_gate_up': nrm(ks[19], (DEPTH, GLA_GATE_RANK, GLA_QK), GLA_GATE_RANK ** -0.5),
        'b_gate': nrm(ks[20], (DEPTH, GLA_QK), 0.01),
        'gla_out_norm': gain(ks[21], (DEPTH, GLA_DV)),
        'w_o': nrm(ks[22], (DEPTH, MIX_WIDTH, D_MODEL), MIX_WIDTH ** -0.5),
        'ffn_norm': gain(ks[23], (DEPTH, D_MODEL)),
        'w_up': nrm(ks[24], (DEPTH, D_MODEL, 2 * D_FF), D_MODEL ** -0.5),
        'conv_w': nrm(ks[25], (DEPTH, CONV_W, D_FF), CONV_W ** -0.5),
        'conv_b': nrm(ks[26], (DEPTH, D_FF), 0.01),
        'w_down': nrm(ks[27], (DEPTH, D_FF, D_MODEL), D_FF ** -0.5),
        'final_norm': gain(ks[28], (D_MODEL,)),
    }


def reference(x_prompt, x_sample, state_s5_re, state_s5_im, state_gla, state_conv,
              attn_norm, w_in, s5_A_re, s5_A_im, s5_B_re, s5_B_im, s5_C_re, s5_C_im,
              s5_D, s5_log_step, w_glu, b_glu, s5_out_norm, w_gate_up, b_gate,
              gla_out_norm, w_o, ffn_norm, w_up, conv_w, conv_b, w_down, final_norm):
    f32 = jnp.float32
    xp, xs = x_prompt, x_sample
    Np = x_prompt.shape[0]
    p_re, p_im, p_gla, p_conv = [], [], [], []
    s_re, s_im, s_gla, s_conv = [], [], [], []
    for l in range(DEPTH):
        lw = (attn_norm[l], w_in[l], s5_A_re[l], s5_A_im[l], s5_B_re[l], s5_B_im[l],
              s5_C_re[l], s5_C_im[l], s5_D[l], s5_log_step[l], w_glu[l], b_glu[l],
              s5_out_norm[l], w_gate_up[l], b_gate[l], gla_out_norm[l], w_o[l],
              ffn_norm[l], w_up[l], conv_w[l], conv_b[l], w_down[l])
        z_re = jnp.zeros((Np, S5_N_GROUPS, S5_STATE), f32)
        z_gla = jnp.zeros((Np, GLA_HEADS, GLA_DK, GLA_DV), f32)
        z_conv = jnp.zeros((Np, CONV_W - 1, D_FF), x_prompt.dtype)
        xp, r1, i1, g1, c1 = decoder_layer(xp, z_re, z_re, z_gla, z_conv, *lw)
        xs, r2, i2, g2, c2 = decoder_layer(xs, state_s5_re[l], state_s5_im[l],
                                           state_gla[l], state_conv[l], *lw)
        p_re.append(r1); p_im.append(i1); p_gla.append(g1); p_conv.append(c1)
        s_re.append(r2); s_im.append(i2); s_gla.append(g2); s_conv.append(c2)
    y_prompt = rms_norm(xp, final_norm)
    y_sample = rms_norm(xs, final_norm)
    return (y_prompt, y_sample,
            jnp.stack(p_re), jnp.stack(p_im), jnp.stack(p_gla), jnp.stack(p_conv),
            jnp.stack(s_re), jnp.stack(s_im), jnp.stack(s_gla), jnp.stack(s_conv))
```

```python
import contextlib
import math
import numpy as np
import concourse.bass as bass
import concourse.mybir as mybir
from concourse.bass_utils import run_bass_kernel_spmd

F32 = mybir.dt.float32
BF16 = mybir.dt.bfloat16
I32 = mybir.dt.int32
U8 = mybir.dt.uint8
AF = mybir.ActivationFunctionType
ALU = mybir.AluOpType
ESZ = {F32: 4, BF16: 2, I32: 4, U8: 1}

NCORES = 8
D = 1024
TP_ = 2048
TS_ = 64
NSEQ = 16
INC = 2064
DFF = 2816
NF = 22
EPS = 1e-6
MCOL = 272
TWO_PI = 2.0 * math.pi
MAGIC = 12582912.0

DEBUG = {}
SAME_ENGINE_WAR = True


class Op:
    __slots__ = ("eng", "fn", "R", "W", "chan", "deps", "inc", "val", "waits", "barrier", "weak")

    def __init__(self, eng, fn, R, W, chan=None, barrier=False, weak=()):
        self.eng, self.fn, self.R, self.W, self.chan = eng, fn, tuple(R), tuple(W), chan
        self.weak = tuple(weak)
        self.deps = set()
        self.inc = False
        self.val = 0
        self.waits = []
        self.barrier = barrier


class FW:
    ENGS = ("pe", "act", "dve", "pool", "sp")

    def __init__(self):
        self.ops = []

    def add(self, eng, fn, R=(), W=(), chan=None, weak=()):
        self.ops.append(Op(eng, fn, R, W, chan, weak=weak))

    def barrier(self, engs=None):
        for e in (engs or self.ENGS):
            self.ops.append(Op(e, None, (), (), None, barrier=True))

    def finalize(self):
        ops = self.ops
        lastw = {}
        readers = {}
        last_on = {}
        bar_start = None
        i = 0
        n = len(ops)
        while i < n:
            op = ops[i]
            if op.barrier:
                j = i
                snap = dict(last_on)
                while j < n and ops[j].barrier:
                    ops[j].deps = set(snap.values())
                    j += 1
                for k in range(i, j):
                    last_on[("e", ops[k].eng)] = k
                i = j
                continue
            deps = set()
            for k in op.R:
                if k in lastw:
                    deps.add(lastw[k])
            for k in op.W:
                if k in lastw:
                    deps.add(lastw[k])
                for r in readers.get(k, {}).values():
                    deps.add(r)
            deps.discard(i)
            op.deps = deps
            rk = ("c", op.chan) if op.chan is not None else ("e", op.eng)
            for k in op.R:
                if k not in op.weak:
                    readers.setdefault(k, {})[rk] = i
            for k in op.W:
                lastw[k] = i
                readers[k] = {}
            if op.chan is not None:
                last_on[("c", op.chan)] = i
            else:
                last_on[("e", op.eng)] = i
            i += 1
        need = [set() for _ in ops]
        for i, op in enumerate(ops):
            for d in op.deps:
                dop = ops[d]
                if dop.chan is None and dop.eng == op.eng and op.chan is None:
                    if op.eng in ("pe", "sp"):
                        continue
                    if dop.fn is None:
                        continue
                    hazard = (set(dop.W) & (set(op.R) | set(op.W)))
                    if SAME_ENGINE_WAR:
                        hazard = hazard or (set(dop.R) & set(op.W))
                    if not hazard and not op.barrier:
                        continue
                if dop.chan is not None and dop.chan == op.chan:
                    continue
                need[i].add(d)
                dop.inc = True
        for op in ops:
            if op.chan is not None:
                op.inc = True
        cnt = {}
        for op in ops:
            if op.inc:
                key = ("c", op.chan) if op.chan is not None else ("e", op.eng)
                cnt[key] = cnt.get(key, 0) + (16 if op.chan is not None else 1)
                op.val = cnt[key]
        self.sem_keys = sorted(cnt.keys(), key=str)
        for op in ops:
            if op.inc and op.chan is not None and str(op.chan).startswith("G:"):
                op.val = cnt[("c", op.chan)]
        waited = {e: {} for e in self.ENGS}
        for i, op in enumerate(ops):
            best = {}
            for d in need[i]:
                dop = ops[d]
                key = ("c", dop.chan) if dop.chan is not None else ("e", dop.eng)
                if dop.val > best.get(key, 0):
                    best[key] = dop.val
            w = waited[op.eng]
            for key, v in best.items():
                if v > w.get(key, 0):
                    w[key] = v
                    op.waits.append((key, v))

    def simulate(self):
        per = {e: [op for op in self.ops if op.eng == e] for e in self.ENGS}
        pos = {e: 0 for e in self.ENGS}
        sem = {}
        progress = True
        while progress:
            progress = False
            for e in self.ENGS:
                lst = per[e]
                while pos[e] < len(lst):
                    op = lst[pos[e]]
                    if all(sem.get(k, 0) >= v for k, v in op.waits):
                        if op.inc:
                            key = ("c", op.chan) if op.chan is not None else ("e", op.eng)
                            sem[key] = sem.get(key, 0) + (16 if op.chan is not None else 1)
                        pos[e] += 1
                        progress = True
                    else:
                        break
        stuck = {e: pos[e] for e in self.ENGS if pos[e] < len(per[e])}
        if stuck:
            for e, p in stuck.items():
                op = per[e][p]
                print("DEADLOCK", e, p, "waits", op.waits, "R", op.R, "W", op.W,
                      "have", {k: sem.get(k, 0) for k, _ in op.waits})
            raise RuntimeError("deadlock in sync plan")
        return True

    def emit(self, nc, st, block):
        sems = {}
        for key in self.sem_keys:
            sems[key] = st.enter_context(nc.semaphore("s_" + "_".join(str(x) for x in key)))
        per = {e: [op for op in self.ops if op.eng == e] for e in self.ENGS}

        def run(engobj, lst):
            for op in lst:
                for key, v in op.waits:
                    engobj.wait_ge(sems[key], v)
                if op.fn is None:
                    if op.inc:
                        engobj.nop().then_inc(sems[("e", op.eng)], 1)
                    continue
                ins = op.fn(engobj)
                if op.inc:
                    key = ("c", op.chan) if op.chan is not None else ("e", op.eng)
                    ins.then_inc(sems[key], 16 if op.chan is not None else 1)

        @block.tensor
        def _(e):
            run(e, per["pe"])

        @block.scalar
        def _(e):
            run(e, per["act"])

        @block.vector
        def _(e):
            run(e, per["dve"])

        @block.gpsimd
        def _(e):
            run(e, per["pool"])

        @block.sync
        def _(e):
            run(e, per["sp"])


class Arena:
    def __init__(self, u8ap, lo, hi):
        self.ap, self.lo, self.hi, self.top = u8ap, lo, hi, lo

    def reset(self, top=None):
        self.top = self.lo if top is None else top

    def alloc(self, shape, dtype):
        free = 1
        for s in shape[1:]:
            free *= s
        nb = free * ESZ[dtype]
        off = (self.top + 63) // 64 * 64
        assert off + nb <= self.hi, f"arena overflow {off + nb} > {self.hi}"
        self.top = off + nb
        v = self.ap[:, off:off + nb].bitcast(dtype)
        if len(shape) > 2:
            names = " ".join(f"d{i}" for i in range(len(shape) - 1))
            kw = {f"d{i}": shape[i + 1] for i in range(len(shape) - 1)}
            v = v.rearrange(f"p ({names}) -> p {names}", **kw)
        if shape[0] < 128:
            v = v[0:shape[0]]
        return v


def bc(ap, axis, n):
    l = [list(t) for t in ap.ap]
    l.insert(axis, [0, n])
    return bass.AP(ap.tensor, ap.offset, l)


def dap(t, offset, dims):
    return bass.AP(t.tensor, offset, [list(d) for d in dims])


def build(stop_after=None):
    nc = bass.Bass("TRN2", target_bir_lowering=False)
    fw = FW()

    def din(name, shape, dt=F32):
        return nc.dram_tensor(name, list(shape), dt, kind="ExternalInput").ap()

    def dout(name, shape, dt=F32):
        return nc.dram_tensor(name, list(shape), dt, kind="ExternalOutput").ap()

    I = {}
    I["xp"] = din("xp", [TP_, D])
    I["xs"] = din("xs", [TS_, D])
    I["s5r"] = din("s5r", [NSEQ, 2048])
    I["s5i"] = din("s5i", [NSEQ, 2048])
    I["sgla"] = din("sgla", [NSEQ, 4, 64, 128])
    I["sconv"] = din("sconv", [2 * NSEQ, DFF])
    for nm, shp in [("attn_norm", [D]), ("w_in", [D, INC]), ("A_re", [32, 64]), ("A_im", [32, 64]),
                    ("B_re", [32, 64, 16]), ("B_im", [32, 64, 16]), ("C_re", [32, 16, 64]),
                    ("C_im", [32, 16, 64]), ("Dp", [32, 16]), ("log_step", [32]), ("w_glu", [512, 512]),
                    ("b_glu", [512]), ("s5_out_norm", [512]), ("w_gate_up", [16, 256]), ("b_gate", [256]),
                    ("gla_out_norm", [128]), ("w_o", [D, D]), ("ffn_norm", [D]), ("w_up", [D, 2 * DFF]),
                    ("conv_w", [3, DFF]), ("conv_b", [DFF]), ("w_down", [DFF, D]), ("final_norm", [D])]:
        I[nm] = din(nm, shp)
    O = {}
    O["yp"] = dout("yp", [TP_, D])
    O["ys"] = dout("ys", [TS_, D])
    O["s5o_r"] = dout("s5o_r", [17, 2048])
    O["s5o_i"] = dout("s5o_i", [17, 2048])
    O["glao"] = dout("glao", [17, 4, 64, 128])
    O["convo"] = dout("convo", [34, DFF])
    for k, (shp, dts) in DEBUG.items():
        O[k] = dout(k, shp, BF16 if dts == "bf16" else F32)
    if "d_mix" in DEBUG:
        mix_d = O["d_mix"]
    else:
        mix_d = nc.dram_tensor("mix_d", [128, 8, TP_ + TS_], BF16, kind="Internal").ap()

    st = contextlib.ExitStack()
    ARENA_BYTES = 206 * 1024
    arena_t = st.enter_context(nc.sbuf_tensor("arena", [128, ARENA_BYTES], U8))
    psum = [st.enter_context(nc.psum_tensor(f"ps{i}", [128, 512], F32)) for i in range(8)]
    CONST = Arena(arena_t, 0, 14 * 1024)
    TAB = Arena(arena_t, 14 * 1024, 46 * 1024)
    REST = Arena(arena_t, 46 * 1024, ARENA_BYTES)

    def psb(i, n=1024):
        return psum[i][:, 0:n // 2].bitcast(BF16)

    dma_ctr = [0]

    def dma(out, in_, R, W, chan, eng="sp", **kw):
        def fn(e, out=out, in_=in_, kw=kw):
            return e.dma_start(out=out, in_=in_, **kw)
        fw.add(eng, fn, R, W, chan=chan)

    def mm(out, lhsT, rhs, R, W, start=True, stop=True):
        def fn(e):
            return e.matmul(out, lhsT, rhs, start=start, stop=stop)
        fw.add("pe", fn, R, W)

    def tr(out, in_, ident, R, W):
        def fn(e):
            return e.transpose(out=out, in_=in_, identity=ident)
        fw.add("pe", fn, R, W)

    def act(out, in_, func, R, W, **kw):
        def fn(e):
            return e.activation(out=out, in_=in_, func=func, **kw)
        fw.add("act", fn, R, W)

    def tt(eng, out, in0, in1, op, R, W):
        def fn(e):
            return e.tensor_tensor(out=out, in0=in0, in1=in1, op=op)
        fw.add(eng, fn, R, W)

    def ts(eng, out, in0, s1, op0, R, W, s2=None, op1=None, accum_out=None):
        def fn(e):
            if op1 is None:
                return e.tensor_scalar(out=out, in0=in0, scalar1=s1, scalar2=None, op0=op0)
            return e.tensor_scalar(out=out, in0=in0, scalar1=s1, scalar2=s2, op0=op0, op1=op1)
        fw.add(eng, fn, R, W)

    def stt(out, in0, scalar, in1, op0, op1, R, W):
        def fn(e):
            return e.scalar_tensor_tensor(out=out, in0=in0, scalar=scalar, in1=in1, op0=op0, op1=op1)
        fw.add("dve", fn, R, W)

    def cp(eng, out, in_, R, W):
        if eng == "act":
            def fn(e):
                return e.copy(out=out, in_=in_)
        else:
            def fn(e):
                return e.tensor_copy(out=out, in_=in_)
        fw.add(eng, fn, R, W)

    def mset(eng, ap, val, W):
        def fn(e):
            return e.memset(ap, val)
        fw.add(eng, fn, (), W)

    def recip(out, in_, R, W):
        def fn(e):
            return e.reciprocal(out=out, in_=in_)
        fw.add("dve", fn, R, W)

    def iota(out, pattern, base, cm, W):
        def fn(e):
            return e.iota(out, pattern=pattern, base=base, channel_multiplier=cm)
        fw.add("pool", fn, (), W)

    out_chans = []

    def store(out, in_, R, name):
        ch = "o_" + name
        if ch not in out_chans:
            out_chans.append(ch)
        dma(out, in_, R, ("OUT_" + name,), ch)

    def dbg(name, ap, R):
        if name in DEBUG:
            store(O[name], ap, R, name)

    C = {}
    C["ident_f"] = CONST.alloc([128, 128], F32)
    C["ident_b"] = CONST.alloc([128, 128], BF16)
    C["tri_f"] = CONST.alloc([128, 128], F32)
    C["ones_b"] = CONST.alloc([128, 128], BF16)
    C["gnorm_row"] = CONST.alloc([128, 128], F32)
    C["fin_row"] = CONST.alloc([128, 1024], F32)
    C["attn_col"] = CONST.alloc([128, 8], F32)
    C["ffn_col"] = CONST.alloc([128, 8], F32)
    C["s5n_col"] = CONST.alloc([128, 4], F32)
    C["bglu_col"] = CONST.alloc([128, 4], F32)
    C["cw_col"] = CONST.alloc([128, 3, NF], F32)
    C["cb_col"] = CONST.alloc([128, NF], F32)
    C["wgate"] = CONST.alloc([16, 256], BF16)
    C["bgate"] = CONST.alloc([1, 256], BF16)
    C["ones_row"] = CONST.alloc([1, 128], BF16)
    C["padmask"] = CONST.alloc([128, 1], F32)
    C["eps_col"] = CONST.alloc([128, 1], F32)
    C["L8"] = CONST.alloc([128, 2, 16], F32)
    C["LK"] = CONST.alloc([128, 2, 4, 16], F32)
    TPW_OFF = (CONST.top + 63) // 64 * 64
    C["TPW"] = CONST.alloc([128, 2, 16, 16], F32)
    C["Lm4"] = CONST.alloc([128, 2, 16], F32)
    C["fin"] = CONST.alloc([128, 2, 16, 17], F32)
    C["itmp"] = st.enter_context(nc.sbuf_tensor("iota_i32", [128, 128], I32))[:, :]
    C["pmtmp"] = CONST.alloc([128, 8], F32)

    def load_consts():
        tmpi = C["itmp"]
        iota(tmpi, [[1, 128]], 0, -1, ("itmp",))
        cp("pool", C["ident_f"], tmpi, ("itmp",), ("c_tmpf",))
        ts("dve", C["ident_b"], C["ident_f"], 0.0, ALU.is_equal, ("c_tmpf",), ("ident_b",))
        ts("dve", C["tri_f"], C["ident_f"], 0.0, ALU.is_ge, ("c_tmpf", "ident_b"), ("tri_f",))
        ts("dve", C["ident_f"], C["ident_f"], 0.0, ALU.is_equal, ("tri_f", "ident_b", "c_tmpf"), ("ident_f",))
        mset("dve", C["tri_f"][0:64, 64:128], 0.0, ("tri_f",))
        mset("dve", C["ones_b"], 1.0, ("ones_b",))
        mset("dve", C["ones_row"], 1.0, ("ones_row",))
        mset("dve", C["padmask"], 0.0, ("padmask",))
        mset("dve", C["eps_col"], EPS, ("eps_col",))
        C["mhalf"] = C["pmtmp"][:, 4:5]
        mset("dve", C["mhalf"], -0.5, ("mhalf",))
        iota(tmpi[:, 0:1], [[0, 1]], 0, 1, ("itmp",))
        cp("pool", C["padmask"], tmpi[:, 0:1], ("itmp",), ("padmask",))
        pm = C["pmtmp"]
        ts("dve", pm[:, 1:2], C["padmask"], 60.0, ALU.is_ge, ("padmask",), ("pmtmp",))
        ts("dve", pm[:, 2:3], C["padmask"], 64.0, ALU.is_ge, ("padmask",), ("pmtmp",))
        ts("dve", pm[:, 3:4], C["padmask"], 124.0, ALU.is_ge, ("padmask",), ("pmtmp",))
        tt("dve", pm[:, 1:2], pm[:, 1:2], pm[:, 2:3], ALU.subtract, ("pmtmp",), ("pmtmp",))
        tt("dve", C["padmask"], pm[:, 1:2], pm[:, 3:4], ALU.add, ("pmtmp",), ("padmask",))
        dma(C["gnorm_row"], dap(I["gla_out_norm"], 0, [[0, 128], [1, 128]]), (), ("gnorm_row",), "G:cst")
        dma(C["fin_row"], dap(I["final_norm"], 0, [[0, 128], [1, 1024]]), (), ("fin_row",), "G:cst")
        dma(C["attn_col"], dap(I["attn_norm"], 0, [[1, 128], [128, 8]]), (), ("attn_col",), "G:cst",
            allow_slow_non_contiguous=True)
        dma(C["ffn_col"], dap(I["ffn_norm"], 0, [[1, 128], [128, 8]]), (), ("ffn_col",), "G:cst",
            allow_slow_non_contiguous=True)
        dma(C["s5n_col"], dap(I["s5_out_norm"], 0, [[1, 128], [128, 4]]), (), ("s5n_col",), "G:cst",
            allow_slow_non_contiguous=True)
        dma(C["bglu_col"], dap(I["b_glu"], 0, [[1, 128], [128, 4]]), (), ("bglu_col",), "G:cst",
            allow_slow_non_contiguous=True)
        dma(C["cw_col"], dap(I["conv_w"], 0, [[1, 128], [DFF, 3], [128, NF]]), (), ("cw_col",), "G:cst",
            allow_slow_non_contiguous=True)
        dma(C["cb_col"], dap(I["conv_b"], 0, [[1, 128], [128, NF]]), (), ("cb_col",), "G:cst",
            allow_slow_non_contiguous=True)
        dma(C["wgate"], I["w_gate_up"], (), ("wgate",), "G:cstp", eng="pool")
        dma(C["bgate"], dap(I["b_gate"], 0, [[0, 1], [1, 256]]), (), ("bgate",), "G:cstp", eng="pool")

    CK = ("ident_f", "ident_b", "tri_f")

    T = {}
    T["BL"] = TAB.alloc([128, 32, 2, 64], BF16)
    T["T0"] = TAB.alloc([128, 32, 128], BF16)
    T["CL"] = TAB.alloc([128, 32, 2, 128], BF16)

    def phase0():
        R0 = Arena(arena_t, REST.lo, REST.hi)

        def A(shape, dt=F32):
            return R0.alloc(shape, dt)
        LSrow = A([128, 32]); Bnr = A([128, 16, 16]); Bni = A([128, 16, 16])
        P0_KEEP = R0.top
        Acr = A([128, 16]); Aci = A([128, 16]); dtc = A([128, 16])
        arc = A([128, 16]); thc = A([128, 16])
        dma(Acr, dap(I["A_re"], 0, [[1, 128], [128, 16]]), (), ("Acr",), "G:p0a", allow_slow_non_contiguous=True)
        dma(Aci, dap(I["A_im"], 0, [[1, 128], [128, 16]]), (), ("Aci",), "G:p0a", allow_slow_non_contiguous=True)
        dma(LSrow, dap(I["log_step"], 0, [[0, 128], [1, 32]]), (), ("LSrow",), "G:p0a")
        act(LSrow, LSrow, AF.Exp, ("LSrow",), ("LSrow",))
        cp("dve", dtc[0:64, :], LSrow[0:64, 0:32:2], ("LSrow",), ("dtc",))
        cp("dve", dtc[64:128, :], LSrow[64:128, 1:32:2], ("LSrow",), ("dtc",))
        tt("dve", arc, Acr, dtc, ALU.mult, ("Acr", "dtc"), ("arc",))
        tt("dve", thc, Aci, dtc, ALU.mult, ("Aci", "dtc"), ("thc",))
        NN = 31
        nlist = list(range(-7, 9)) + [8 * k for k in range(2, 17)]

        def nidx(n):
            return n + 7 if n <= 8 else 15 + (n // 8 - 1)
        NV = A([128, NN, 16]); argE = A([128, NN, 16]); argS = A([128, NN, 16]); tmpk = A([128, NN, 16])
        rr = A([128, NN, 16]); LPr = A([128, NN, 16]); LPi = A([128, NN, 16]); Ecol = A([128, NN, 16])
        for j, n in enumerate(nlist):
            mset("dve", NV[:, j, :], float(n), ("NV",))
        tt("dve", argE, NV, bc(arc, 1, NN), ALU.mult, ("NV", "arc"), ("argE",))
        tt("dve", argS, NV, bc(thc, 1, NN), ALU.mult, ("NV", "thc"), ("argS",))
        act(Ecol, argE, AF.Exp, ("argE",), ("Ecol",))

        def sincos(eng, out_s, out_c, arg, tmp, rtile, etile, kin, kout_s, kout_c, tkey, rkey, ekey):
            for shift, outt, kout in ((0.0, out_s, kout_s), (math.pi / 2, out_c, kout_c)):
                C1 = 6.28125
                C2 = TWO_PI - C1
                ts(eng, outt, arg, shift, ALU.add, (kin,), (kout,))
                ts(eng, tmp, outt, 1.0 / TWO_PI, ALU.mult, (kout,), (tkey,), s2=MAGIC, op1=ALU.add)
                ts(eng, tmp, tmp, -MAGIC, ALU.add, (tkey,), (tkey,))
                ts(eng, rtile, tmp, -C1, ALU.mult, (tkey,), (rkey,))
                tt(eng, rtile, rtile, outt, ALU.add, (rkey, kout), (rkey,))
                ts(eng, tmp, tmp, -C2, ALU.mult, (tkey,), (tkey,))
                tt(eng, rtile, rtile, tmp, ALU.add, (rkey, tkey), (rkey,))
                ts(eng, rtile, rtile, 3.1415925, ALU.min, (rkey,), (rkey,), s2=-3.1415925, op1=ALU.max)
                act(outt, rtile, AF.Sin, (rkey,), (kout,))
                tt(eng, outt, outt, etile, ALU.mult, (kout, ekey), (kout,))

        sincos("dve", LPi, LPr, argS, tmpk, rr, Ecol, "argS", "LPi", "LPr", "tmpk", "rr", "Ecol")
        for ri, LP in ((0, LPr), (1, LPi)):
            key = "LPr" if ri == 0 else "LPi"
            cp("dve", C["L8"][:, ri, :], LP[:, nidx(8), :], (key,), ("L8",))
            cp("dve", C["Lm4"][:, ri, :], LP[:, nidx(-4), :], (key,), ("Lm4",))
            cp("dve", C["LK"][:, ri, 0, :], LP[:, nidx(128), :], (key,), ("LK",))
            cp("dve", C["TPW"][:, ri, :, 0], LP[:, nidx(0), :], (key,), ("TPW",))
            cp("dve", C["TPW"][:, ri, :, 1], LP[:, nidx(8), :], (key,), ("TPW",))
            for k in range(2, 16):
                cp("dve", C["TPW"][:, ri, :, k], LP[:, nidx(8 * k), :], (key,), ("TPW",))
        sq1 = A([128, 16]); sq2 = A([128, 16])
        for k in range(1, 4):
            pr = C["LK"][:, 0, k - 1, :]; pi_ = C["LK"][:, 1, k - 1, :]
            tt("dve", sq1, pr, pr, ALU.mult, ("LK",), ("sq1",))
            tt("dve", sq2, pi_, pi_, ALU.mult, ("LK",), ("sq2",))
            tt("dve", C["LK"][:, 0, k, :], sq1, sq2, ALU.subtract, ("sq1", "sq2"), ("LK",))
            tt("dve", sq1, pr, pi_, ALU.mult, ("LK",), ("sq1",))
            ts("dve", C["LK"][:, 1, k, :], sq1, 2.0, ALU.mult, ("sq1",), ("LK",))
        den = A([128, 16]); crc = A([128, 16]); cic = A([128, 16]); nrc = A([128, 16]); t16 = A([128, 16])
        l1r = LPr[:, nidx(1), :]; l1i = LPi[:, nidx(1), :]
        tt("dve", den, Acr, Acr, ALU.mult, ("Acr",), ("den",))
        tt("dve", t16, Aci, Aci, ALU.mult, ("Aci",), ("t16",))
        tt("dve", den, den, t16, ALU.add, ("den", "t16"), ("den",))
        recip(den, den, ("den",), ("den",))
        ts("dve", nrc, l1r, -1.0, ALU.add, ("LPr",), ("nrc",))
        tt("dve", crc, nrc, Acr, ALU.mult, ("nrc", "Acr"), ("crc",))
        tt("dve", t16, l1i, Aci, ALU.mult, ("LPi", "Aci"), ("t16",))
        tt("dve", crc, crc, t16, ALU.add, ("crc", "t16"), ("crc",))
        tt("dve", crc, crc, den, ALU.mult, ("crc", "den"), ("crc",))
        tt("dve", cic, l1i, Acr, ALU.mult, ("LPi", "Acr"), ("cic",))
        tt("dve", t16, nrc, Aci, ALU.mult, ("nrc", "Aci"), ("t16",))
        tt("dve", cic, cic, t16, ALU.subtract, ("cic", "t16"), ("cic",))
        tt("dve", cic, cic, den, ALU.mult, ("cic", "den"), ("cic",))
        if stop_after == "p0col":
            dbg("d_dtc", dtc, ("dtc",)); dbg("d_thc", thc, ("thc",)); dbg("d_arc", arc, ("arc",))
            dbg("d_argS", argS.rearrange("p n g -> p (n g)"), ("argS",))
            dbg("d_Ecol", Ecol.rearrange("p n g -> p (n g)"), ("Ecol",))
            dbg("d_LPr", LPr.rearrange("p n g -> p (n g)"), ("LPr",))
            dbg("d_LPi", LPi.rearrange("p n g -> p (n g)"), ("LPi",))
            dbg("d_rr", rr.rearrange("p n g -> p (n g)"), ("rr",))
            dbg("d_TPW", C["TPW"].rearrange("p r g k -> p (r g k)"), ("TPW",))
            dbg("d_LK", C["LK"].rearrange("p r k g -> p (r k g)"), ("LK",))
            return
        dma(Bnr, dap(I["B_re"], 0, [[16, 128], [2048, 16], [1, 16]]), (), ("Bnr",), "G:p0b")
        dma(Bni, dap(I["B_im"], 0, [[16, 128], [2048, 16], [1, 16]]), (), ("Bni",), "G:p0b")
        Bbr = A([128, 16, 16]); Bbi = A([128, 16, 16]); t256 = A([128, 16, 16])
        crb = bc(crc, 2, 16); cib = bc(cic, 2, 16)
        tt("dve", Bbr, Bnr, crb, ALU.mult, ("Bnr", "crc"), ("Bbr",))
        tt("dve", t256, Bni, cib, ALU.mult, ("Bni", "cic"), ("t256",))
        tt("dve", Bbr, Bbr, t256, ALU.subtract, ("Bbr", "t256"), ("Bbr",))
        tt("dve", Bbi, Bni, crb, ALU.mult, ("Bni", "crc"), ("Bbi",))
        tt("dve", t256, Bnr, cib, ALU.mult, ("Bnr", "cic"), ("t256",))
        tt("dve", Bbi, Bbi, t256, ALU.add, ("Bbi", "t256"), ("Bbi",))
        if stop_after == "p0Bb":
            dbg("d_x", Bbr.rearrange("p g c -> p (g c)"), ("Bbr", "Bbi"))
            return
        Cn2r = A([128, 4, 128]); Cn2i = A([128, 4, 128])
        for h in range(2):
            dma(Cn2r[:, :, 64 * h:64 * h + 64], dap(I["C_re"], 0, [[64, 128], [8192, 4], [1, 64]]), (), ("Cn2r",), "G:p0b")
            dma(Cn2i[:, :, 64 * h:64 * h + 64], dap(I["C_im"], 0, [[64, 128], [8192, 4], [1, 64]]), (), ("Cn2i",), "G:p0b")
        Ccr = A([128, 32, 16]); Cci = A([128, 32, 16])
        for (src, dstt, pk, sk, dk) in ((Cn2r, Ccr, 0, "Cn2r", "Ccr"), (Cn2i, Cci, 1, "Cn2i", "Cci")):
            for q in range(4):
                mm(psum[pk][:, q * 128:(q + 1) * 128], src[:, q, :], C["ident_f"], (sk, "ident_f"), (f"ps{pk}",))
            cp("dve", dstt.rearrange("p g c -> p (g c)"), psum[pk][:, 0:512], (f"ps{pk}",), (dk,))
        if stop_after == "p0Cc":
            dbg("d_x", Ccr.rearrange("p g c -> p (g c)")[:, 0:256], ("Ccr", "Cci"))
            return
        mset("dve", T["CL"], 0.0, ("CL",))
        X_r = A([128, 16, 8, 16]); X_i = A([128, 16, 8, 16]); Y_r = A([128, 16, 8, 16]); Y_i = A([128, 16, 8, 16])
        t1 = A([128, 16, 8, 16]); t2 = A([128, 16, 8, 16])

        def lp(LP, n0, step, half):
            v = LP[64 * half:64 * half + 64]
            l = [list(t) for t in v.ap]
            nstr, gstr = l[1][0], l[2][0]
            return bass.AP(v.tensor, v.offset + nidx(n0) * nstr, [l[0], [gstr, 16], [step * nstr, 8], [0, 16]])

        Cown_r = A([128, 16, 16]); Cown_i = A([128, 16, 16])
        for half in range(2):
            sl = slice(64 * half, 64 * half + 64)
            cp("dve", Cown_r[sl], Ccr[sl, half:32:2, :], ("Ccr",), ("Cown_r",))
            cp("dve", Cown_i[sl], Cci[sl, half:32:2, :], ("Cci",), ("Cown_i",))

        def lpf(LP, n0, step):
            nstr, gstr = LP.ap[1][0], LP.ap[2][0]
            return bass.AP(LP.tensor, LP.offset + nidx(n0) * nstr, [list(LP.ap[0]), [gstr, 16], [step * nstr, 8], [0, 16]])
        Cr_b = bc(Cown_r, 2, 8); Ci_b = bc(Cown_i, 2, 8)
        for (n0, dst_r, dst_i, kr, ki) in ((1, None, None, None, None), (0, Y_r, Y_i, "Y_r", "Y_i")):
            Lr = lpf(LPr, n0, 1); Li = lpf(LPi, n0, 1)
            tt("dve", t1, Cr_b, Lr, ALU.mult, ("Cown_r", "LPr"), ("t1",))
            tt("dve", t2, Ci_b, Li, ALU.mult, ("Cown_i", "LPi"), ("t2",))
            if dst_r is None:
                for half in range(2):
                    sl = slice(64 * half, 64 * half + 64)
                    clr = T["CL"][sl, half:32:2, 0, :].rearrange("p g (i c) -> p g i c", i=8)
                    tt("dve", clr, t1[sl], t2[sl], ALU.subtract, ("t1", "t2"), ("CL",))
            else:
                tt("dve", dst_r, t1, t2, ALU.subtract, ("t1", "t2"), (kr,))
            tt("dve", t1, Cr_b, Li, ALU.mult, ("Cown_r", "LPi"), ("t1",))
            tt("dve", t2, Ci_b, Lr, ALU.mult, ("Cown_i", "LPr"), ("t2",))
            tt("dve", t1, t1, t2, ALU.add, ("t1", "t2"), ("t1",))
            if dst_r is None:
                for half in range(2):
                    sl = slice(64 * half, 64 * half + 64)
                    cli = T["CL"][sl, half:32:2, 1, :].rearrange("p g (i c) -> p g i c", i=8)
                    ts("dve", cli, t1[sl], -1.0, ALU.mult, ("t1",), ("CL",))
            else:
                ts("dve", dst_i, t1, -1.0, ALU.mult, ("t1",), (ki,))
        if stop_after == "p0CL":
            dbg("d_CL", T["CL"].rearrange("p g r n -> p (g r n)"), ("CL", "Y_r", "Y_i"))
            return
        Lr = bass.AP(LPr.tensor, LPr.offset + nidx(0) * LPr.ap[1][0],
                     [list(LPr.ap[0]), [LPr.ap[2][0], 16], [-LPr.ap[1][0], 8], [0, 16]])
        Li = bass.AP(LPi.tensor, LPi.offset + nidx(0) * LPi.ap[1][0],
                     [list(LPi.ap[0]), [LPi.ap[2][0], 16], [-LPi.ap[1][0], 8], [0, 16]])
        Bbr_b = bc(Bbr, 2, 8); Bbi_b = bc(Bbi, 2, 8)
        tt("dve", t1, Lr, Bbr_b, ALU.mult, ("LPr", "Bbr"), ("t1",))
        tt("dve", t2, Li, Bbi_b, ALU.mult, ("LPi", "Bbi"), ("t2",))
        tt("dve", X_r, t1, t2, ALU.subtract, ("t1", "t2"), ("X_r",))
        tt("dve", t1, Lr, Bbi_b, ALU.mult, ("LPr", "Bbi"), ("t1",))
        tt("dve", t2, Li, Bbr_b, ALU.mult, ("LPi", "Bbr"), ("t2",))
        tt("dve", X_i, t1, t2, ALU.add, ("t1", "t2"), ("X_i",))
        if stop_after == "p0X":
            dbg("d_x", X_r.rearrange("p g j c -> p (g j c)")[:, 0:256], ("X_r", "X_i"))
            return
        mask8 = A([128, 128]); Drow = A([128, 32, 16]); Dterm = A([128, 16, 8, 16]); tmpT = A([128, 16, 128])
        iota(C["itmp"], [[16, 8], [0, 16]], 15, -1, ("itmp",))
        cp("dve", mask8, C["itmp"], ("itmp",), ("mask8",))
        ts("dve", mask8, mask8, 0.0, ALU.is_ge, ("mask8",), ("mask8",))
        dma(Drow, dap(I["Dp"], 0, [[0, 128], [16, 32], [1, 16]]), (), ("Drow",), "G:p0b")
        Xm_r = A([128, 16, 128]); Xm_i = A([128, 16, 128])
        for rnd in range(2):
            tt("dve", Dterm, bc(C["ident_f"].rearrange("p (i c) -> p i c", i=8), 1, 16),
               bc(Drow[:, rnd * 16:(rnd + 1) * 16, :], 2, 8), ALU.mult, ("ident_f", "Drow"), ("Dterm",))
            for (Xm, Xs, kx, kxm) in ((Xm_r, X_r, "X_r", "Xm_r"), (Xm_i, X_i, "X_i", "Xm_i")):
                mset("dve", Xm, 0.0, (kxm,))
                for half in range(2):
                    sl = slice(64 * half, 64 * half + 64)
                    cp("dve", Xm[sl, half:16:2, :],
                       Xs[sl, rnd * 8:(rnd + 1) * 8].rearrange("p g j c -> p g (j c)"), (kx,), (kxm,))
            for gg in range(16):
                g = rnd * 16 + gg
                gp = g // 2
                bank = psum[gg // 4]
                o = bank[:, (gg % 4) * 128:(gg % 4 + 1) * 128]
                mm(o, Xm_r[:, gg, :], Y_r[:, gp].rearrange("p i c -> p (i c)"),
                   ("Xm_r", "Y_r"), (f"ps{gg // 4}",), start=True, stop=False)
                mm(o, Xm_i[:, gg, :], Y_i[:, gp].rearrange("p i c -> p (i c)"),
                   ("Xm_i", "Y_i"), (f"ps{gg // 4}",), start=False, stop=True)
            for b4 in range(4):
                tt("dve", tmpT[:, b4 * 4:(b4 + 1) * 4, :], psum[b4][:, 0:512].rearrange("p (g n) -> p g n", g=4),
                   bc(mask8, 1, 4), ALU.mult, (f"ps{b4}", "mask8"), ("tmpT",))
            tt("dve", T["T0"][:, rnd * 16:(rnd + 1) * 16, :], tmpT,
               Dterm.rearrange("p g i c -> p g (i c)"), ALU.add,
               ("tmpT", "Dterm"), ("T0",))
        BLc = [X_r, X_i]
        nstr, gstr = LPr.ap[1][0], LPr.ap[2][0]
        LrB = bass.AP(LPr.tensor, LPr.offset + nidx(7) * nstr, [list(LPr.ap[0]), [gstr, 16], [-nstr, 8], [0, 16]])
        LiB = bass.AP(LPi.tensor, LPi.offset + nidx(7) * nstr, [list(LPi.ap[0]), [gstr, 16], [-nstr, 8], [0, 16]])
        tt("dve", t1, LrB, Bbr_b, ALU.mult, ("LPr", "Bbr"), ("t1",))
        tt("dve", t2, LiB, Bbi_b, ALU.mult, ("LPi", "Bbi"), ("t2",))
        tt("dve", BLc[0], t1, t2, ALU.subtract, ("t1", "t2"), ("X_r",))
        tt("dve", t1, LrB, Bbi_b, ALU.mult, ("LPr", "Bbi"), ("t1",))
        tt("dve", t2, LiB, Bbr_b, ALU.mult, ("LPi", "Bbr"), ("t2",))
        tt("dve", BLc[1], t1, t2, ALU.add, ("t1", "t2"), ("X_i",))
        cnt = 0
        for ri in range(2):
            for q4 in range(4):
                bk = 4 + cnt % 4
                cnt += 1
                for gq in range(4):
                    gp = q4 * 4 + gq
                    tr(psum[bk][:, gq * 128:(gq + 1) * 128], BLc[ri][:, gp].rearrange("p j c -> p (j c)"), C["ident_f"],
                       ("X_r" if ri == 0 else "X_i", "ident_f"), (f"ps{bk}",))
                cp("act" if cnt % 2 == 0 else "dve", T["BL"][:, q4 * 8:(q4 + 1) * 8, ri, :],
                   psum[bk][:, 0:512].rearrange("p (g q) -> p g q", g=8), (f"ps{bk}",), ("BL",))
        if "d_BL" in DEBUG:
            dbg("d_BL", T["BL"].rearrange("p g r q -> p (g r q)"), ("BL",))
            dbg("d_T0", T["T0"].rearrange("p g n -> p (g n)"), ("T0",))
            dbg("d_CL", T["CL"].rearrange("p g r n -> p (g r n)"), ("CL",))
            dbg("d_TPW", C["TPW"].rearrange("p r g k -> p (r g k)"), ("TPW",))
            dbg("d_LK", C["LK"].rearrange("p r k g -> p (r k g)"), ("LK",))


    W_IN_BYTES = 8 * INC * 2
    w_in_sb = arena_t[:, ARENA_BYTES - W_IN_BYTES:ARENA_BYTES].bitcast(BF16).rearrange("p (k n) -> p k n", k=8)
    REST.hi = ARENA_BYTES - W_IN_BYTES - 64

    def load_w_in():
        for kc in range(8):
            dma(w_in_sb[:, kc, :], I["w_in"][kc * 128:(kc + 1) * 128, :], (), ("w_in",), "G:w_in", eng="pool",
                max_dma_last_dim=4096)

    P1 = {}

    def alloc_p1():
        REST.reset()
        A = REST.alloc
        P1["Y"] = A([128, 2, 32, 8, 16], BF16)
        P1["Ys"] = A([16, 32, 8, 16], BF16)
        P1["P1_KEEP"] = REST.top
        P1["hT"] = A([128, 8, 1024], BF16)
        P1["hTs"] = A([128, 8, 64], BF16)
        P1["hTpad"] = A([128, 8, 128], BF16)
        P1["x"] = [A([128, 1024], F32) for _ in range(2)]
        P1["hb"] = [A([128, 1024], BF16) for _ in range(2)]
        P1["st"] = A([128, 16], F32)
        P1["qf"] = A([128, 2, 512], F32)
        P1["kf"] = A([128, 2, 512], F32)
        P1["zT"] = A([16, 512], BF16)
        P1["vtok"] = [A([128, 512], BF16) for _ in range(6)]
        P1["gsil"] = [A([128, 512], BF16) for _ in range(6)]
        P1["lnv"] = A([128, 256], F32)
        P1["Epl"] = [A([128, 2, 128], F32) for _ in range(2)]
        P1["Emi"] = A([128, 2, 128], F32)
        P1["qeT"] = A([128, 2, 128], BF16)
        P1["QA"] = [A([128, 2, 128], BF16) for _ in range(2)]
        P1["QB"] = [A([128, 2, 128], BF16) for _ in range(2)]
        P1["keT"] = A([128, 2, 128], BF16)
        P1["ketok"] = A([128, 256], BF16)
        P1["attm"] = [A([128, 4, 128], BF16) for _ in range(2)]
        P1["Psb"] = [A([128, 2, 2, 128], F32) for _ in range(2)]
        P1["Sf"] = [A([128, 2, 128], F32) for _ in range(2)]
        P1["Sblk"] = [A([128, 2, 2, 128], BF16) for _ in range(2)]
        P1["Fst"] = [A([128, 2, 128], F32) for _ in range(2)]
        P1["stmp"] = A([128, 2, 128], F32)
        P1["osq"] = A([128, 4, 128], F32)
        P1["on"] = [A([128, 512], BF16) for _ in range(2)]
        P1["mst"] = A([128, 4, 128], BF16)
        P1["P1_TOP"] = REST.top

    xctr = [0]

    def norm_tile(x_dram_rows, nrows, hT_dst, gain_col, xkey_pref="x"):
        slot = xctr[0] % 2
        xctr[0] += 1
        xb = P1["x"][slot][0:nrows]
        hb = P1["hb"][slot][0:nrows]
        xk, hk, sk = f"x{slot}", f"hb{slot}", f"st{slot}"
        ss = P1["st"][0:nrows, slot * 4:slot * 4 + 1]
        rs = P1["st"][0:nrows, slot * 4 + 1:slot * 4 + 2]
        dma(xb, x_dram_rows, (), (xk,), f"x{slot}")
        fw.add("act", lambda e: e.activation(out=hb, in_=xb, func=AF.Square, accum_out=ss), (xk,), (hk, sk))
        act(rs, ss, AF.Ln, (sk,), (sk,), scale=1.0 / D, bias=C["eps_col"][0:nrows])
        act(rs, rs, AF.Exp, (sk,), (sk,), scale=-0.5)
        fw.add("act", lambda e: e.activation(out=hb, in_=xb, func=AF.Copy, scale=rs), (xk, sk), (hk,))
        pb = psb(0)
        for kc in range(8):
            tr(pb[0:128, kc * 128:kc * 128 + nrows], hb[:, kc * 128:(kc + 1) * 128], C["ident_b"][0:nrows, 0:nrows],
               (hk, "ident_b"), ("ps0a", "ps0b"))
        pv = pb.rearrange("p (k n) -> p k n", k=8)[:, :, 0:nrows]
        tt("dve", hT_dst, pv, bc(gain_col, 2, nrows), ALU.mult, ("ps0a", "ps0b", "attn_col"), ("hT",))

    def proj_qkz(hT_ap, n, tagR=("hT",)):
        idx = 0
        for (dst, col0, key) in ((P1["qf"], 512, "qf"), (P1["kf"], 768, "kf")):
            for c2 in range(2):
                b = 1 + idx % 2
                idx += 1
                for kc in range(8):
                    mm(psum[b][:, 0:n], w_in_sb[:, kc, col0 + c2 * 128:col0 + (c2 + 1) * 128], hT_ap[:, kc, :],
                       ("w_in",) + tagR, (f"ps{b}",), start=(kc == 0), stop=(kc == 7))
                cp("act", dst[:, c2, 0:n], psum[b][:, 0:n], (f"ps{b}",), (key,))
        b = 1 + idx % 2
        for kc in range(8):
            mm(psum[b][0:16, 0:n], w_in_sb[:, kc, 2048:2064], hT_ap[:, kc, :], ("w_in",) + tagR, (f"ps{b}",),
               start=(kc == 0), stop=(kc == 7))
        cp("dve", P1["zT"][:, 0:n], psum[b][0:16, 0:n], (f"ps{b}",), ("zT",))

    def proj_vg(hT_ap, slot):
        vtok, gsil = P1["vtok"][slot], P1["gsil"][slot]
        kv, kg = f"vtok{slot}", f"gsil{slot}"
        for (b, col0) in ((3, 1024), (4, 1536)):
            for kc in range(8):
                mm(psum[b][:, 0:512], hT_ap[:, kc, :], w_in_sb[:, kc, col0:col0 + 512], ("hT", "w_in"), (f"ps{b}",),
                   start=(kc == 0), stop=(kc == 7))
        cp("dve", vtok, psum[3][:, 0:512], ("ps3",), (kv,))
        act(gsil, psum[4][:, 0:512], AF.Silu, ("ps4",), (kg,))
        tt("dve", gsil.rearrange("p (h v) -> p h v", h=4), gsil.rearrange("p (h v) -> p h v", h=4),
           bc(C["gnorm_row"], 1, 4), ALU.mult, (kg, "gnorm_row"), (kg,))

    def gla_front(hT_ap, c0, mode, par, slot, part):
        vtok, gsil, Epl = P1["vtok"][slot], P1["gsil"][slot], P1["Epl"][par]
        QA, QB, attm, Psb = P1["QA"][par], P1["QB"][par], P1["attm"][par], P1["Psb"][par]
        kv, kg, ke, kqa, kqb, kat, kp = (f"vtok{slot}", f"gsil{slot}", f"Epl{par}", f"QA{par}", f"QB{par}", f"attm{par}",
                                         f"Psb{par}")
        if part == 1:
            return gla_front2(c0, mode, par, slot)
        mm(psum[5][:, 0:256], P1["zT"][:, c0:c0 + 128], C["wgate"], ("zT", "wgate"), ("ps5a",), start=True, stop=False)
        mm(psum[5][:, 0:256], C["ones_row"], C["bgate"], ("ones_row", "bgate"), ("ps5a",), start=False, stop=True)
        act(P1["lnv"], psum[5][:, 0:256], AF.Exp, ("ps5a",), ("lnv",), scale=-1.0)
        act(P1["lnv"], P1["lnv"], AF.Ln, ("lnv",), ("lnv",), bias=1.0)
        if mode == "s":
            ts("dve", P1["lnv"], P1["lnv"], C["padmask"], ALU.mult, ("lnv", "padmask"), ("lnv",))
        for hp in range(2):
            mm(psum[5][:, 256 + hp * 128:256 + (hp + 1) * 128], P1["lnv"][:, hp * 128:(hp + 1) * 128], C["tri_f"],
               ("lnv", "tri_f"), ("ps5b",))
        cT = psum[5][:, 256:512].rearrange("p (h t) -> p h t", h=2)
        act(Epl, cT, AF.Exp, ("ps5b",), (ke,), scale=-1.0 / 16)
        act(P1["Emi"], cT, AF.Exp, ("ps5b",), ("Emi",), scale=1.0 / 16)

    def gla_front2(c0, mode, par, slot):
        vtok, gsil, Epl = P1["vtok"][slot], P1["gsil"][slot], P1["Epl"][par]
        QA, QB, attm, Psb = P1["QA"][par], P1["QB"][par], P1["attm"][par], P1["Psb"][par]
        kv, kg, ke, kqa, kqb, kat, kp = (f"vtok{slot}", f"gsil{slot}", f"Epl{par}", f"QA{par}", f"QB{par}", f"attm{par}",
                                         f"Psb{par}")
        stt(P1["qeT"], P1["qf"][:, :, c0:c0 + 128], 0.125, Epl, ALU.mult, ALU.mult, ("qf", ke), ("qeT",))
        tt("dve", P1["keT"], P1["kf"][:, :, c0:c0 + 128], P1["Emi"], ALU.mult, ("kf", "Emi"), ("keT",))
        cp("act", QA[:, :, 0:64], P1["qeT"][:, :, 0:64], ("qeT",), (kqa,))
        cp("act", QB[:, :, 64:128], P1["qeT"][:, :, 64:128], ("qeT",), (kqb,))
        pb = psb(0)
        for hp in range(2):
            tr(pb[:, hp * 128:(hp + 1) * 128], P1["keT"][:, hp, :], C["ident_b"], ("keT", "ident_b"), ("ps0a",))
        cp("act", P1["ketok"], pb[:, 0:256], ("ps0a",), ("ketok",))
        for h in range(4):
            rows = slice(64 * (h % 2), 64 * (h % 2) + 64)
            bk = 6 if h % 2 == 0 else 3
            mm(psum[bk][:, (h // 2) * 128:(h // 2 + 1) * 128], P1["keT"][rows, h // 2, :], P1["qeT"][rows, h // 2, :],
               ("keT", "qeT"), (f"ps{bk}",))
        for h2 in range(2):
            bk = 6 if h2 == 0 else 3
            tt("dve", attm[:, h2:4:2, :], psum[bk][:, 0:256].rearrange("p (h i) -> p h i", h=2),
               bc(C["tri_f"], 1, 2), ALU.mult, (f"ps{bk}", "tri_f"), (kat,))
        for X in range(2):
            trow = slice(64 * X, 64 * X + 64)
            for hp in range(2):
                mm(psum[1 + X][:, hp * 256:(hp + 1) * 256], P1["ketok"][trow, hp * 128:(hp + 1) * 128],
                   vtok[trow, hp * 256:(hp + 1) * 256], ("ketok", kv), (f"ps{1 + X}",))
            for h2 in range(2):
                rows = slice(64 * h2, 64 * h2 + 64)
                pv = psum[1 + X][rows, 0:512].rearrange("p (hp b v) -> p hp b v", hp=2, b=2)[:, :, h2, :]
                cp("act", Psb[rows, X], pv, (f"ps{1 + X}",), (kp,))

    def gla_back(mode, par, slot, tok0=None, seqA=None):
        vtok, gsil, Epl = P1["vtok"][slot], P1["gsil"][slot], P1["Epl"][par]
        QA, QB, attm, Psb = P1["QA"][par], P1["QB"][par], P1["attm"][par], P1["Psb"][par]
        kv, kg, ke, kqa, kqb, kat, kp = (f"vtok{slot}", f"gsil{slot}", f"Epl{par}", f"QA{par}", f"QB{par}", f"attm{par}",
                                         f"Psb{par}")
        on, kon = P1["on"][par], f"on{par}"
        Sf, Sblk = P1["Sf"], P1["Sblk"]
        if mode == "s":
            for X in range(2):
                dma(Sf[X], dap(I["sgla"], (seqA + X) * 32768, [[128, 128], [16384, 2], [1, 128]]), (), (f"Sf{X}",),
                    f"sf{X}")
                for h2 in range(2):
                    rows = slice(64 * h2, 64 * h2 + 64)
                    cp("act", Sblk[X][rows, :, h2, :], Sf[X][rows], (f"Sf{X}",), (f"Sblk{X}",))

        def upd(src_f, X, eL_col, dst_f, dst_blk, dkey_f, dkey_b, skey):
            tt("dve", P1["stmp"], Psb[:, X], src_f, ALU.add, (kp, skey), ("stmp",))
            tt("dve", dst_f, P1["stmp"], bc(Epl[:, :, eL_col], 2, 128), ALU.mult, ("stmp", ke), (dkey_f,))
            if dst_blk is not None:
                for h2 in range(2):
                    rows = slice(64 * h2, 64 * h2 + 64)
                    cp("act", dst_blk[rows, :, h2, :], dst_f[rows], (dkey_f,), (dkey_b,))
        if mode == "p":
            upd(Sf[0], 0, 63, Sf[1], Sblk[1], "Sf1", "Sblk1", "Sf0")
        else:
            upd(Sf[0], 0, 63, P1["Fst"][0], None, "Fst0", None, "Sf0")
            upd(Sf[1], 1, 127, P1["Fst"][1], None, "Fst1", None, "Sf1")
        for hp in range(2):
            o_pair = psum[7][:, hp * 256:(hp + 1) * 256]
            mm(o_pair, QA[:, hp, :], Sblk[0][:, hp].rearrange("p a v -> p (a v)"), (kqa, "Sblk0"), ("ps7",),
               start=True, stop=False)
            mm(o_pair, QB[:, hp, :], Sblk[1][:, hp].rearrange("p a v -> p (a v)"), (kqb, "Sblk1"), ("ps7",),
               start=False, stop=False)
            for h in (2 * hp, 2 * hp + 1):
                mm(psum[7][:, h * 128:(h + 1) * 128], attm[:, h, :], vtok[:, h * 128:(h + 1) * 128],
                   (kat, kv), ("ps7",), start=False, stop=(h == 2 * hp + 1))
        if mode == "p":
            upd(Sf[1], 1, 127, Sf[0], Sblk[0], "Sf0", "Sblk0", "Sf1")
        o4 = psum[7][:, 0:512].rearrange("p (h v) -> p h v", h=4)
        act(P1["osq"], o4, AF.Square, ("ps7",), ("osq",))
        ost = P1["st"][:, 8:12]
        fw.add("dve", lambda e: e.tensor_reduce(out=ost, in_=P1["osq"], axis=mybir.AxisListType.X, op=ALU.add),
               ("osq",), ("ost",))
        act(ost, ost, AF.Ln, ("ost", "eps_col"), ("ost",), scale=1.0 / 128, bias=C["eps_col"])
        act(ost, ost, AF.Exp, ("ost",), ("ost",), scale=-0.5)
        tt("dve", P1["osq"], o4, bc(ost, 2, 128), ALU.mult, ("ps7", "ost"), ("osq",))
        tt("dve", on, P1["osq"].rearrange("p h v -> p (h v)"), gsil, ALU.mult, ("osq", kg), (kon,))
        if mode == "s":
            for X in range(2):
                store(dap(O["glao"], (1 + seqA + X) * 32768, [[128, 128], [16384, 2], [1, 128]]), P1["Fst"][X],
                      (f"Fst{X}",), f"glao{X}")

    def gla_tail(mode, par, tok0=None, seqA=None):
        on, kon = P1["on"][par], f"on{par}"
        pb = psb(0)
        for h in range(4):
            tr(pb[:, 512 + h * 128:512 + (h + 1) * 128], on[:, h * 128:(h + 1) * 128], C["ident_b"],
               (kon, "ident_b"), ("ps0b",))
        cp("act", P1["mst"].rearrange("p h t -> p (h t)"), pb[:, 512:1024], ("ps0b",), ("mst",))
        if mode == "p":
            dma(mix_d[:, 4:8, tok0:tok0 + 128], P1["mst"], ("mst",), ("mix_d",), "mixw")
        else:
            for X in range(2):
                t0 = TP_ + (seqA + X) * 4
                dma(mix_d[:, 4:8, t0:t0 + 4], P1["mst"][:, :, 64 * X + 60:64 * X + 64], ("mst",), ("mix_d",), "mixw")

    def phase1a():
        fw.barrier()
        alloc_p1()
        for par in range(2):
            mset("pool", P1["QA"][par], 0.0, (f"QA{par}",))
            mset("pool", P1["QB"][par], 0.0, (f"QB{par}",))
        for X in range(2):
            mset("pool", P1["Sblk"][X], 0.0, (f"Sblk{X}",))
        mset("pool", P1["Sf"][0], 0.0, ("Sf0",))
        mset("pool", P1["Ys"][:, :, 0:4, :], 0.0, ("Ys",))
        mset("pool", P1["hTpad"], 0.0, ("hTpad",))
        Y = P1["Y"]
        pendB = []
        pendC = []
        tcount = [0]

        def capture(fn):
            n0 = len(fw.ops)
            fn()
            lst = fw.ops[n0:]
            del fw.ops[n0:]
            return lst

        def step(front=None, back=None, tail=None):
            lists = []
            if pendC:
                c = pendC.pop(0)
                lists.append(capture(c))
            if pendB:
                b, c = pendB.pop(0)
                lists.append(capture(b))
                pendC.append(c)
            if front is not None:
                lists.append(capture(lambda: (front(0), front(1))))
                pendB.append((back, tail))
            keyed = []
            for li, lst in enumerate(lists):
                n = len(lst)
                for i, op in enumerate(lst):
                    keyed.append(((i + 0.5) / n, li, i, op))
            keyed.sort(key=lambda t: (t[0], t[1], t[2]))
            fw.ops.extend(op for _, _, _, op in keyed)

        def drain():
            while pendB or pendC:
                step()

        def do_tile(hT_ap, c0, mode, slot, tok0=None, seqA=None):
            par = tcount[0] % 2
            tcount[0] += 1
            step(lambda part: gla_front(hT_ap, c0, mode, par, slot, part),
                 lambda: gla_back(mode, par, slot, tok0=tok0, seqA=seqA),
                 lambda: gla_tail(mode, par, tok0=tok0, seqA=seqA))
        vslot = [0]
        for a in range(2):
            for sti in range(2):
                for i in range(4):
                    lt = sti * 4 + i
                    pt = a * 8 + lt
                    norm_tile(I["xp"][pt * 128:(pt + 1) * 128, :], 128, P1["hT"][:, :, lt * 128:(lt + 1) * 128],
                              C["attn_col"])
                hT512 = P1["hT"][:, :, sti * 512:(sti + 1) * 512]
                proj_qkz(hT512, 512)
                slots = []
                for i in range(4):
                    lt = sti * 4 + i
                    sl_ = vslot[0] % 6
                    vslot[0] += 1
                    slots.append(sl_)
                    proj_vg(P1["hT"][:, :, lt * 128:(lt + 1) * 128], sl_)
                for i in range(4):
                    lt = sti * 4 + i
                    pt = a * 8 + lt
                    do_tile(P1["hT"][:, :, lt * 128:(lt + 1) * 128], i * 128, "p", slots[i], tok0=pt * 128)
            for jl in range(8):
                b = 1 + jl % 2
                for kc in range(8):
                    mm(psum[b][:, 0:512], P1["hT"][:, kc, jl:1024:8], w_in_sb[:, kc, 0:512], ("hT", "w_in"), (f"ps{b}",),
                       start=(kc == 0), stop=(kc == 7))
                cp("act", Y[:, a, :, jl, :], psum[b][:, 0:512].rearrange("p (g c) -> p g c", g=32), (f"ps{b}",), ("Y",))
        drain()
        store(dap(O["glao"], 0, [[128, 128], [16384, 2], [1, 128]]), P1["Sf"][0], ("Sf0",), "glaoP")
        norm_tile(I["xs"], 64, P1["hTs"], C["attn_col"])
        for tq in range(4):
            b = 1 + tq % 2
            for kc in range(8):
                mm(psum[b][0:16, 0:512], P1["hTs"][:, kc, tq:64:4], w_in_sb[:, kc, 0:512], ("hT", "w_in"), (f"ps{b}",),
                   start=(kc == 0), stop=(kc == 7))
            cp("act", P1["Ys"][:, :, 4 + tq, :], psum[b][0:16, 0:512].rearrange("p (g c) -> p g c", g=32), (f"ps{b}",),
               ("Ys",))
        hTp = P1["hT"]
        mset("dve", hTp, 0.0, ("hT",))
        cp("dve", hTp.rearrange("p k (s t) -> p k s t", s=16)[:, :, :, 60:64],
           P1["hTs"].rearrange("p k (s t) -> p k s t", s=16), ("hT",), ("hT",))
        for st_ in range(2):
            proj_qkz(hTp[:, :, st_ * 512:(st_ + 1) * 512], 512)
            slots = []
            for i in range(4):
                sl_ = vslot[0] % 6
                vslot[0] += 1
                slots.append(sl_)
                proj_vg(hTp[:, :, (st_ * 4 + i) * 128:(st_ * 4 + i + 1) * 128], sl_)
            for i in range(4):
                pst = st_ * 4 + i
                do_tile(hTp[:, :, pst * 128:(pst + 1) * 128], i * 128, "s", slots[i], seqA=2 * pst)
        drain()

    P2 = {}

    def phase1b():
        fw.barrier()
        REST.hi = ARENA_BYTES
        REST.reset(P1["P1_KEEP"])
        A = REST.alloc
        Y, Ys = P1["Y"], P1["Ys"]
        Ublk = A([128, 32, MCOL], BF16)
        Wreg_off = REST.top
        W = [A([128, 16, MCOL], F32) for _ in range(2)]
        Sprev = [A([128, 16, MCOL], BF16) for _ in range(2)]
        Sa = A([128, 2, 16, 16], F32); Sb = A([128, 2, 16, 16], F32); Sst = A([128, 2, 16, 16], F32)
        s0 = A([128, 2, 16, 16], F32); s0p = A([128, 2, 16, 16], F32); s0full = A([128, 2, 2048], F32)
        s0tok = s0full[0:16]
        tA = [A([128, 4, 16, 16], F32) for _ in range(4)]
        hA = [A([128, 16, 16], F32) for _ in range(4)]
        wglu = A([128, 4, 512], BF16)
        for kc in range(4):
            dma(wglu[:, kc, :], I["w_glu"][kc * 128:(kc + 1) * 128, :], (), ("wglu",), "G:wglu", eng="pool")
        dma(s0tok[:, 0, :], I["s5r"], (), ("s0tok",), "G:s0")
        dma(s0tok[:, 1, :], I["s5i"], (), ("s0tok",), "G:s0")
        cnt = 0
        for a in range(2):
            for gb in range(4):
                bk = cnt % 2
                pb = psb(bk)
                for gi in range(8):
                    g = gb * 8 + gi
                    tr(pb[:, gi * 128:(gi + 1) * 128], Y[:, a, g].rearrange("p j c -> p (j c)"), C["ident_b"],
                       ("Y", "ident_b"), (f"ps{bk}",))
                cp("act" if cnt % 2 == 0 else "dve", Ublk[:, gb * 8:(gb + 1) * 8, a * 128:(a + 1) * 128],
                   pb.rearrange("p (g m) -> p g m", g=8), (f"ps{bk}",), (f"Ub_{a}_{gb}",))
                cnt += 1
        pb = psb(0)
        for g in range(32):
            tr(pb[:, g * 16:(g + 1) * 16], Ys[:, g].rearrange("p j c -> p (j c)"), C["ident_b"][0:16, 0:16],
               ("Ys", "ident_b"), ("ps0",))
        cp("act", Ublk[:, :, 256:MCOL], pb[:, 0:512].rearrange("p (g m) -> p g m", g=32), ("ps0",), ("Ub_s",))
        fw.add("dve", None, tuple(f"Ub_{a}_{gb}" for a in range(2) for gb in range(4)) + ("Ub_s",), ("Ublk",))
        for ri in range(2):
            for gp in range(16):
                tr(psum[1][:, ri * 256 + gp * 16:ri * 256 + (gp + 1) * 16], s0tok[:, ri, gp * 128:(gp + 1) * 128],
                   C["ident_f"][0:16, 0:16], ("s0tok", "ident_f"), ("ps1",))
        cp("dve", s0.rearrange("p r g s -> p (r g s)"), psum[1][:, 0:512], ("ps1",), ("s0",))
        l4r = bc(C["Lm4"][:, 0, :], 2, 16); l4i = bc(C["Lm4"][:, 1, :], 2, 16)
        tt("dve", hA[0], s0[:, 0], l4r, ALU.mult, ("s0", "Lm4"), ("hA0",))
        tt("dve", hA[1], s0[:, 1], l4i, ALU.mult, ("s0", "Lm4"), ("hA1",))
        tt("dve", s0p[:, 0], hA[0], hA[1], ALU.subtract, ("hA0", "hA1"), ("s0p",))
        tt("dve", hA[0], s0[:, 0], l4i, ALU.mult, ("s0", "Lm4"), ("hA0",))
        tt("dve", hA[1], s0[:, 1], l4r, ALU.mult, ("s0", "Lm4"), ("hA1",))
        tt("dve", s0p[:, 1], hA[0], hA[1], ALU.add, ("hA0", "hA1"), ("s0p",))
        cnt = 0
        for gp in range(16):
            for ri in range(2):
                bk = 2 + cnt % 6
                cnt += 1
                for g2 in range(2):
                    g = 2 * gp + g2
                    mm(psum[bk][64 * g2:64 * g2 + 64, 0:MCOL], T["BL"][:, g, ri, :], Ublk[:, g, :], ("BL", "Ublk"),
                       (f"ps{bk}",))
                cp("act" if cnt % 2 == 0 else "dve", W[ri][:, gp, :], psum[bk][:, 0:MCOL], (f"ps{bk}",),
                   (f"Wc{ri}_{gp}",))
        for ri in range(2):
            fw.add("dve", None, tuple(f"Wc{ri}_{gp}" for gp in range(16)), (f"W{ri}_0", f"W{ri}s"))
        if stop_after == "p1b_W":
            dbg("d_W", W[0].rearrange("p g m -> p (g m)"), ("W0_0", "W0s"))
            dbg("d_Ublk", Ublk.rearrange("p g m -> p (g m)"), ("Ublk",))
            return
        Wv = [W[ri][:, :, 0:256].rearrange("p g (c k) -> p g c k", c=16) for ri in range(2)]
        l8r = bc(C["L8"][:, 0, :], 2, 16); l8i = bc(C["L8"][:, 1, :], 2, 16)

        def wk(ri, k):
            return f"W{ri}_{k}" if k > 0 else f"W{ri}_0"
        for ri in range(2):
            fw.add("dve", None, (f"W{ri}_0",), tuple(f"W{ri}_{k}" for k in range(1, 16)))
        for k in range(1, 16):
            pr, pi_ = Wv[0][:, :, :, k - 1], Wv[1][:, :, :, k - 1]
            kr = (wk(0, k - 1), wk(1, k - 1), "L8")
            tt("dve", hA[0], pr, l8r, ALU.mult, kr, ("hA0",))
            tt("dve", hA[1], pi_, l8i, ALU.mult, kr, ("hA1",))
            tt("dve", hA[2], pi_, l8r, ALU.mult, kr, ("hA2",))
            tt("dve", hA[3], pr, l8i, ALU.mult, kr, ("hA3",))
            tt("dve", Wv[0][:, :, :, k], Wv[0][:, :, :, k], hA[0], ALU.add, ("hA0", wk(0, k)), (wk(0, k),))
            tt("dve", Wv[1][:, :, :, k], Wv[1][:, :, :, k], hA[2], ALU.add, ("hA2", wk(1, k)), (wk(1, k),))
            tt("dve", Wv[0][:, :, :, k], Wv[0][:, :, :, k], hA[1], ALU.subtract, ("hA1", wk(0, k)), (wk(0, k),))
            tt("dve", Wv[1][:, :, :, k], Wv[1][:, :, :, k], hA[3], ALU.add, ("hA3", wk(1, k)), (wk(1, k),))
        allW = tuple(f"W{ri}_{k}" for ri in range(2) for k in range(16))
        cp("dve", Sa[:, 0], Wv[0][:, :, :, 15], (wk(0, 15),), ("Sa",))
        cp("dve", Sa[:, 1], Wv[1][:, :, :, 15], (wk(1, 15),), ("Sa",))
        cur, nxt, kc_, kn_ = Sa, Sb, "Sa", "Sb"
        for lev, d in enumerate((1, 2, 4, 8)):
            lr = bc(C["LK"][:, 0, lev, :], 2, 16 - d); li = bc(C["LK"][:, 1, lev, :], 2, 16 - d)
            cp("dve", nxt[:, 0], cur[:, 0], (kc_,), (kn_,))
            cp("dve", nxt[:, 1], cur[:, 1], (kc_,), (kn_,))
            sr, si = cur[:, 0, :, 0:16 - d], cur[:, 1, :, 0:16 - d]
            h0, h1, h2_, h3 = (hA[i][:, :, 0:16 - d] for i in range(4))
            tt("dve", h0, sr, lr, ALU.mult, (kc_, "LK"), ("hA0",))
            tt("dve", nxt[:, 0, :, d:16], nxt[:, 0, :, d:16], h0, ALU.add, ("hA0", kn_), (kn_,))
            tt("dve", h1, si, li, ALU.mult, (kc_, "LK"), ("hA1",))
            tt("dve", nxt[:, 0, :, d:16], nxt[:, 0, :, d:16], h1, ALU.subtract, ("hA1", kn_), (kn_,))
            tt("dve", h2_, si, lr, ALU.mult, (kc_, "LK"), ("hA2",))
            tt("dve", nxt[:, 1, :, d:16], nxt[:, 1, :, d:16], h2_, ALU.add, ("hA2", kn_), (kn_,))
            tt("dve", h3, sr, li, ALU.mult, (kc_, "LK"), ("hA3",))
            tt("dve", nxt[:, 1, :, d:16], nxt[:, 1, :, d:16], h3, ALU.add, ("hA3", kn_), (kn_,))
            cur, nxt, kc_, kn_ = nxt, cur, kn_, kc_
        Send, ke_ = cur, kc_
        mset("dve", Sst, 0.0, ("Sst",))
        cp("dve", Sst[:, :, :, 1:16], Send[:, :, :, 0:15], (ke_,), ("Sst",))
        cp("dve", C["fin"][:, :, :, 0], Send[:, :, :, 15], (ke_,), ("fin",))
        l8rs = bc(C["L8"][:, 0, :], 2, 16); l8is = bc(C["L8"][:, 1, :], 2, 16)
        tt("dve", hA[0], s0p[:, 0], l8rs, ALU.mult, ("s0p", "L8"), ("hA0",))
        tt("dve", hA[1], s0p[:, 1], l8is, ALU.mult, ("s0p", "L8"), ("hA1",))
        tt("dve", hA[0], hA[0], hA[1], ALU.subtract, ("hA0", "hA1"), ("hA0",))
        tt("dve", C["fin"][:, 0, :, 1:17], hA[0], W[0][:, :, 256:MCOL], ALU.add, ("hA0", "W0s"), ("fin",))
        tt("dve", hA[0], s0p[:, 0], l8is, ALU.mult, ("s0p", "L8"), ("hA0",))
        tt("dve", hA[1], s0p[:, 1], l8rs, ALU.mult, ("s0p", "L8"), ("hA1",))
        tt("dve", hA[0], hA[0], hA[1], ALU.add, ("hA0", "hA1"), ("hA0",))
        tt("dve", C["fin"][:, 1, :, 1:17], hA[0], W[1][:, :, 256:MCOL], ALU.add, ("hA0", "W1s"), ("fin",))
        for q in range(4):
            gs = slice(4 * q, 4 * q + 4)
            TPr = bc(C["TPW"][:, 0, gs, :], 2, 16); TPi = bc(C["TPW"][:, 1, gs, :], 2, 16)
            SR = bc(Sst[:, 0, gs, :], 3, 16); SI = bc(Sst[:, 1, gs, :], 3, 16)
            spr = Sprev[0][:, gs, 0:256].rearrange("p g (c k) -> p g c k", c=16)
            spi = Sprev[1][:, gs, 0:256].rearrange("p g (c k) -> p g c k", c=16)
            tt("dve", tA[0], TPr, SR, ALU.mult, ("TPW", "Sst"), ("tA0",))
            tt("dve", tA[1], TPi, SI, ALU.mult, ("TPW", "Sst"), ("tA1",))
            tt("dve", tA[0], tA[0], tA[1], ALU.subtract, ("tA0", "tA1"), ("tA0",))
            tt("dve", spr[:, :, :, 1:16], tA[0][:, :, :, 1:16], Wv[0][:, gs, :, 0:15], ALU.add, ("tA0",) + allW, ("Sp0",))
            cp("dve", spr[:, :, :, 0], tA[0][:, :, :, 0], ("tA0",), ("Sp0",))
            tt("dve", tA[2], TPr, SI, ALU.mult, ("TPW", "Sst"), ("tA2",))
            tt("dve", tA[3], TPi, SR, ALU.mult, ("TPW", "Sst"), ("tA3",))
            tt("dve", tA[2], tA[2], tA[3], ALU.add, ("tA2", "tA3"), ("tA2",))
            tt("dve", spi[:, :, :, 1:16], tA[2][:, :, :, 1:16], Wv[1][:, gs, :, 0:15], ALU.add, ("tA2",) + allW, ("Sp1",))
            cp("dve", spi[:, :, :, 0], tA[2][:, :, :, 0], ("tA2",), ("Sp1",))
        cp("dve", Sprev[0][:, :, 256:MCOL], s0p[:, 0], ("s0p",), ("Sp0",))
        cp("dve", Sprev[1][:, :, 256:MCOL], s0p[:, 1], ("s0p",), ("Sp1",))
        finT = s0full[0:17]
        for ri in range(2):
            for q4 in range(4):
                bk = 2 + q4
                for gq in range(4):
                    gp = q4 * 4 + gq
                    tr(psum[bk][0:17, gq * 128:(gq + 1) * 128], C["fin"][:, ri, gp, :], C["ident_f"], ("fin", "ident_f"),
                       (f"ps{bk}",))
                cp("act", finT[:, ri, q4 * 512:(q4 + 1) * 512], psum[bk][0:17, 0:512], (f"ps{bk}",), ("s0tok",))
            store(O["s5o_r" if ri == 0 else "s5o_i"], finT[:, ri, :], ("s0tok",), "s5o")
        if stop_after == "p1b_S":
            dbg("d_Sp", Sprev[0].rearrange("p g m -> p (g m)"), ("Sp0", "Sp1"))
            return
        Yg = arena_t[:, REST.lo:REST.lo + 32 * MCOL * 2].bitcast(BF16).rearrange("p (g m) -> p g m", g=32)
        fw.add("dve", None, (), ("Y", "Ys") + tuple(f"Yg{g}" for g in range(32)))
        for g in range(32):
            gp = g // 2
            bk = 2 + g % 6
            o = psum[bk][:, 0:MCOL]
            mm(o, T["T0"][:, g, :], Ublk[:, g, :], ("T0", "Ublk"), (f"ps{bk}",), start=True, stop=False)
            mm(o, T["CL"][:, g, 0, :], Sprev[0][:, gp, :], ("CL", "Sp0"), (f"ps{bk}",), start=False, stop=False)
            mm(o, T["CL"][:, g, 1, :], Sprev[1][:, gp, :], ("CL", "Sp1"), (f"ps{bk}",), start=False, stop=True)
            cp("act" if g % 2 == 0 else "dve", Yg[:, g, :], o, (f"ps{bk}",), (f"Yg{g}",))
        fw.add("dve", None, tuple(f"Yg{g}" for g in range(32)), ("Y", "Ys"))
        if stop_after == "p1b_Y":
            dbg("d_Yg", Yg.rearrange("p g m -> p (g m)"), ("Y",))
            return
        Z = arena_t[:, Wreg_off:Wreg_off + 2 * 8 * 512 * 2].bitcast(BF16).rearrange("p (a i n) -> p a i n", a=2, i=8)
        Zs = arena_t[:, Wreg_off + 16384:Wreg_off + 16384 + 8 * 512 * 2].bitcast(BF16).rearrange(
            "p (i n) -> p i n", i=8)[0:16]
        ZK = allW + ("W0s", "W1s")
        ZF = tuple(f"Zc_{a}_{gb}" for a in range(2) for gb in range(4)) + tuple(f"Zs_{gb}" for gb in range(4))
        fw.add("dve", None, (), ZK + ZF)
        cnt = 0
        for a in range(2):
            for gb in range(4):
                bk = cnt % 2
                pb = psb(bk)
                for gi in range(8):
                    g = gb * 8 + gi
                    tr(pb[:, gi * 128:(gi + 1) * 128], Yg[:, g, a * 128:(a + 1) * 128], C["ident_b"], ("Y", "ident_b"),
                       (f"ps{bk}",))
                cp("act" if cnt % 2 == 0 else "dve",
                   Z[:, a, :, gb * 128:(gb + 1) * 128].rearrange("p i (g c) -> p i g c", g=8),
                   pb.rearrange("p (g i c) -> p i g c", g=8, i=8), (f"ps{bk}",), (f"Zc_{a}_{gb}",))
                cnt += 1
        for gb in range(4):
            bk = cnt % 2
            pb = psb(bk)
            for gi in range(8):
                g = gb * 8 + gi
                tr(pb[0:16, gi * 128:(gi + 1) * 128], Yg[:, g, 256:MCOL], C["ident_b"], ("Y", "ident_b"), (f"ps{bk}",))
            cp("act" if cnt % 2 == 0 else "dve", Zs[:, :, gb * 128:(gb + 1) * 128].rearrange("p i (g c) -> p i g c", g=8),
               pb[0:16].rearrange("p (g i c) -> p i g c", g=8, i=8), (f"ps{bk}",), (f"Zs_{gb}",))
            cnt += 1
        fw.add("dve", None, ZF, ZK)
        y5T = Ublk.rearrange("p g m -> p (g m)")[:, 0:4 * (TP_ + TS_)].rearrange("p (q t) -> p q t", q=4)
        fw.add("dve", None, (), ("Ublk",) + tuple(f"y5c_{a}_{q}" for a in range(2) for q in range(4)) + ("y5c_s",))
        for a in range(2):
            for q in range(4):
                bk = cnt % 2
                pb = psb(bk)
                for il in range(8):
                    tr(pb[:, il * 128:(il + 1) * 128], Z[:, a, il, q * 128:(q + 1) * 128], C["ident_b"], ZK + ("ident_b",),
                       (f"ps{bk}",))
                cp("act" if cnt % 2 == 0 else "dve", y5T[:, q, a * 1024:(a + 1) * 1024].rearrange("p (m i) -> p i m", i=8),
                   pb.rearrange("p (i m) -> p i m", i=8), (f"ps{bk}",), (f"y5c_{a}_{q}",))
                cnt += 1
        bk = cnt % 2
        pb = psb(bk)
        for q in range(4):
            for tq in range(4):
                tr(pb[:, (q * 4 + tq) * 16:(q * 4 + tq + 1) * 16], Zs[:, 4 + tq, q * 128:(q + 1) * 128],
                   C["ident_b"][0:16, 0:16], ZK + ("ident_b",), (f"ps{bk}",))
        cp("act", y5T[:, :, TP_:TP_ + TS_].rearrange("p q (s t) -> p q t s", t=4),
           pb[:, 0:256].rearrange("p (q t s) -> p q t s", q=4, t=4), (f"ps{bk}",), ("y5c_s",))
        fw.add("dve", None, tuple(f"y5c_{a}_{q}" for a in range(2) for q in range(4)) + ("y5c_s",), ("Ublk",))
        if stop_after == "p1b_T":
            dbg("d_y5T", y5T.rearrange("p q t -> p (q t)"), ("Ublk",))
            return
        prefetch_p2_weights()
        G_ = {}
        GB = 256
        G_["y5"] = A([128, 4, GB], BF16); G_["sg"] = A([128, GB], F32); G_["y5g"] = A([128, 4, GB], F32)
        G_["sq"] = A([128, 4, GB], BF16); G_["rstd"] = A([128, GB], F32); G_["mixS"] = A([128, 4, GB], BF16)
        blocks = [(c0, GB) for c0 in range(0, TP_, GB)] + [(TP_, TS_)]
        for (c0, n) in blocks:
            act(G_["y5"][:, :, 0:n], y5T[:, :, c0:c0 + n], AF.Gelu_apprx_tanh, ("Ublk",), ("y5",))
            for qo in range(4):
                bk = 2 + qo % 2
                for qi in range(4):
                    mm(psum[bk][:, 0:n], wglu[:, qi, qo * 128:(qo + 1) * 128], G_["y5"][:, qi, 0:n], ("wglu", "y5"),
                       (f"ps{bk}",), start=(qi == 0), stop=(qi == 3))
                act(G_["sg"][:, 0:n], psum[bk][:, 0:n], AF.Sigmoid, (f"ps{bk}", "bglu_col"), ("sg",),
                    bias=C["bglu_col"][:, qo:qo + 1])
                tt("dve", G_["y5g"][:, qo, 0:n], G_["y5"][:, qo, 0:n], G_["sg"][:, 0:n], ALU.mult, ("y5", "sg"),
                   (f"y5g{qo}",))
                act(G_["sq"][:, qo, 0:n], G_["y5g"][:, qo, 0:n], AF.Square, (f"y5g{qo}",), (f"sq{qo}",))
            for qo in range(4):
                mm(psum[4][:, 0:n], C["ones_b"], G_["sq"][:, qo, 0:n], ("ones_b", f"sq{qo}"), ("ps4",), start=(qo == 0),
                   stop=(qo == 3))
            ts("dve", G_["rstd"][:, 0:n], psum[4][:, 0:n], 1.0 / 512, ALU.mult, ("ps4",), ("rstd",), s2=EPS, op1=ALU.add)
            act(G_["rstd"][:, 0:n], G_["rstd"][:, 0:n], AF.Sqrt, ("rstd",), ("rstd",))
            recip(G_["rstd"][:, 0:n], G_["rstd"][:, 0:n], ("rstd",), ("rstd",))
            for qo in range(4):
                stt(G_["mixS"][:, qo, 0:n], G_["y5g"][:, qo, 0:n], C["s5n_col"][:, qo:qo + 1], G_["rstd"][:, 0:n], ALU.mult,
                    ALU.mult, (f"y5g{qo}", "rstd", "s5n_col"), ("mixS",))
            dma(mix_d[:, 0:4, c0:c0 + n], G_["mixS"][:, :, 0:n], ("mixS",), ("mix_d",), "mixw")

    PH = Arena(arena_t, 14 * 1024, ARENA_BYTES)
    w_o = PH.alloc([128, 8, D], BF16)
    w_up = PH.alloc([128, 8, 2 * DFF], BF16)
    w_dn = PH.alloc([128, NF, D], BF16)
    LATE_KC = (3, 4, 5)

    def p2_weight_dmas(late):
        if not late:
            for kc in range(8):
                dma(w_o[:, kc, :], I["w_o"][kc * 128:(kc + 1) * 128, :], (), ("w_o",), "G:w_o", eng="pool",
                    max_dma_last_dim=4096)
        for kc in range(8):
            if (kc in LATE_KC) != late:
                continue
            dma(w_up[:, kc, :], I["w_up"][kc * 128:(kc + 1) * 128, :], (), (f"wuk{kc}",), f"G:wuk{kc}", eng="pool",
                max_dma_last_dim=4096)
        if not late:
            for f in range(NF):
                dma(w_dn[:, f, :], I["w_down"][f * 128:(f + 1) * 128, :], (), ("w_dn",), "G:w_dn", eng="pool",
                    max_dma_last_dim=4096)

    def prefetch_p2_weights():
        fw.barrier(engs=("pool",))
        p2_weight_dmas(late=False)

    def phase2():
        fw.barrier()
        A = PH.alloc
        NTT = 256
        xbs = [A([128, D], F32) for _ in range(3)]
        mt = A([128, 8, NTT], BF16)
        hb = A([128, D], BF16)
        h2T = A([128, 8, NTT], BF16)
        aext = [A([128, NTT + 8], F32) for _ in range(2)]
        ACC_OFF = (PH.top + 63) // 64 * 64
        acc = [A([128, NTT], F32) for _ in range(2)]
        junk2 = arena_t[:, ACC_OFF:ACC_OFF + 2 * NTT * 4].bitcast(BF16)
        sil = [A([128, NTT], BF16) for _ in range(2)]
        gT = A([128, NF, NTT], BF16)
        halo2 = [A([128, NF, 2], F32) for _ in range(2)]
        atail = arena_t[:, TPW_OFF:TPW_OFF + NF * 34 * 4].bitcast(F32).rearrange("p (f n) -> p f n", f=NF)
        fragA = arena_t[:, TPW_OFF + 2992:TPW_OFF + 2992 + 10 * 128].bitcast(F32).rearrange("p (f n) -> p f n", f=10)
        ctop = (CONST.top + 63) // 64 * 64
        fragB = arena_t[:, ctop:ctop + 8 * 128].bitcast(F32).rearrange("p (f n) -> p f n", f=8)
        assert ctop + 8 * 128 <= 14 * 1024 and 2992 + 10 * 128 <= 4352
        fragC = A([128, 4, 32], F32)
        conv0T = [fragA[:, f, :] for f in range(10)] + [fragB[:, f, :] for f in range(8)] + [fragC[:, f, :] for f in range(4)]
        xbs.append(A([128, D], F32))
        st2 = A([128, 16], F32)
        gT_off_bytes = None
        c0tok = gT.rearrange("p f n -> p (f n)").bitcast(F32)[0:32, 0:DFF]
        tailT = gT.rearrange("p f n -> p (f n)").bitcast(F32)[0:34, 0:DFF]
        p2_weight_dmas(late=True)
        GTK = tuple(f"gT{f}" for f in range(NF))
        dma(c0tok, I["sconv"], (), GTK, "c0")
        for f in range(NF):
            bk = 5 + f // 16
            tr(psum[bk][:, (f % 16) * 32:(f % 16 + 1) * 32], c0tok[:, f * 128:(f + 1) * 128], C["ident_f"][0:32, 0:32],
               GTK + ("ident_f",), (f"ps{bk}",))
        cp("dve", fragA, psum[5][:, 0:320].rearrange("p (f n) -> p f n", f=10), ("ps5",), ("conv0T",))
        cp("dve", fragB[:, 0:6, :], psum[5][:, 320:512].rearrange("p (f n) -> p f n", f=6), ("ps5",), ("conv0T",))
        cp("dve", fragB[:, 6:8, :], psum[6][:, 0:64].rearrange("p (f n) -> p f n", f=2), ("ps6",), ("conv0T",))
        cp("dve", fragC, psum[6][:, 64:192].rearrange("p (f n) -> p f n", f=4), ("ps6",), ("conv0T",))
        for hp_ in range(2):
            mset("dve", halo2[hp_], 0.0, tuple(f"halo{hp_}_{f}" for f in range(NF)))

        def rms_rstd(src, n, slot, xk):
            ss = st2[0:n, slot * 2:slot * 2 + 1]
            rs = st2[0:n, slot * 2 + 1:slot * 2 + 2]
            k = f"st2_{slot}"
            if slot == 0:
                junk, jk = hb[0:n], "hb"
            else:
                junk, jk = junk2[0:n, :], "acc0"
            fw.add("act", lambda e: e.activation(out=junk, in_=src, func=AF.Square, accum_out=ss), (xk,),
                   (jk, k) if slot == 0 else ("acc0", "acc1", k))
            ts("dve", rs, ss, 1.0 / D, ALU.mult, (k,), (k,), s2=EPS, op1=ALU.add)
            tt("pool", rs, rs, C["mhalf"][0:n], ALU.pow, (k, "mhalf"), (k,))
            return rs, k

        tiles = [(t * NTT, NTT, "p") for t in range(TP_ // NTT)] + [(TP_, TS_, "s")]
        xslot = {}
        xctr2 = [0]

        def tinfo(ti):
            tok0, NT, kind = tiles[ti]
            nsub = (NT + 127) // 128
            return tok0, NT, kind, [(sb, min(128, NT - sb * 128)) for sb in range(nsub)]

        def pro(ti, sb):
            tok0, NT, kind, subs = tinfo(ti)
            n = subs[sb][1]
            k_ = xctr2[0] % 4
            xctr2[0] += 1
            xslot[(ti, sb)] = k_
            xb, xk = xbs[k_], f"xb{k_}"
            src = I["xp"][tok0 + sb * 128:tok0 + sb * 128 + n, :] if kind == "p" else I["xs"]
            dma(xb[0:n, :], src, (), (xk,), f"xb{k_}")
            if sb == 0:
                dma(mt[:, :, 0:NT], mix_d[:, :, tok0:tok0 + NT], ("mix_d",), ("mt",), "mt")
            cs = slice(sb * 128, sb * 128 + n)
            for half in range(2):
                for kc in range(8):
                    mm(psum[half][0:n, 0:512], mt[:, kc, cs], w_o[:, kc, half * 512:(half + 1) * 512], ("mt", "w_o"),
                       (f"ps{half}",), start=(kc == 0), stop=(kc == 7))
                tt("dve", xb[0:n, half * 512:(half + 1) * 512], psum[half][0:n, 0:512],
                   xb[0:n, half * 512:(half + 1) * 512], ALU.add, (f"ps{half}", xk), (xk,))
            rs, k = rms_rstd(xb[0:n, :], n, 0, xk)
            xsrc = xb[0:n, :]
            fw.add("act", lambda e, xsrc=xsrc, rs=rs, n=n: e.activation(out=hb[0:n], in_=xsrc, func=AF.Copy, scale=rs),
                   (xk, k), ("hb",))

            def part_b():
                pb = psb(7)
                for kc in range(8):
                    tr(pb[:, kc * 128:kc * 128 + n], hb[0:n, kc * 128:(kc + 1) * 128], C["ident_b"][0:n, 0:n],
                       ("hb", "ident_b"), ("ps7",))
                tt("dve", h2T[:, :, cs], pb.rearrange("p (k n) -> p k n", k=8)[:, :, 0:n], bc(C["ffn_col"], 2, n),
                   ALU.mult, ("ps7", "ffn_col"), ("h2T",))
            return part_b

        def epi(ti, sb):
            tok0, NT, kind, subs = tinfo(ti)
            n = subs[sb][1]
            k_ = xslot[(ti, sb)]
            xb, xk = xbs[k_], f"xb{k_}"
            cs = slice(sb * 128, sb * 128 + n)
            for half in range(2):
                for f in range(NF):
                    mm(psum[half][0:n, 0:512], gT[:, f, cs], w_dn[:, f, half * 512:(half + 1) * 512], (f"gT{f}", "w_dn"),
                       (f"ps{half}",), start=(f == 0), stop=(f == NF - 1))
                tt("dve", xb[0:n, half * 512:(half + 1) * 512], psum[half][0:n, 0:512],
                   xb[0:n, half * 512:(half + 1) * 512], ALU.add, (f"ps{half}", xk), (xk,))
            rs, k = rms_rstd(xb[0:n, :], n, 1, xk)
            xsrc = xb[0:n, :]
            fw.add("act", lambda e, xsrc=xsrc, rs=rs: e.activation(out=xsrc, in_=xsrc, func=AF.Copy, scale=rs),
                   (xk, k), (xk,))
            tt("dve", xsrc, xsrc, C["fin_row"][0:n], ALU.mult, (xk, "fin_row"), (xk,))
            dst = O["yp"][tok0 + sb * 128:tok0 + sb * 128 + n, :] if kind == "p" else O["ys"]
            store(dst, xsrc, (xk,), f"y{k_}")

        def ffn_up(ti):
            tok0, NT, kind, subs = tinfo(ti)
            cw = C["cw_col"]
            last_prompt = (ti == len(tiles) - 2)

            def views(f):
                sl = f % 2
                bank = 2 + f % 5
                ae, ac, si = aext[sl], acc[sl], sil[sl]
                psA = psum[bank][:, 0:NT]
                psB = psum[bank][:, 256:256 + NT]
                if kind == "p":
                    v2, v1, v0 = ae[:, 2:2 + NT], ae[:, 1:1 + NT], ae[:, 0:NT]
                    acv, pav = ac[:, 0:NT], psA
                else:
                    ae3 = ae[:, 0:96].rearrange("p (s w) -> p s w", w=6)
                    v2, v1, v0 = ae3[:, :, 2:6], ae3[:, :, 1:5], ae3[:, :, 0:4]
                    acv = ac[:, 0:NT].rearrange("p (s t) -> p s t", t=4)
                    pav = psA.rearrange("p (s t) -> p s t", t=4)
                return sl, bank, ae, ac, si, psA, psB, v2, v1, v0, acv, pav

            def stA(f):
                sl, bank, ae, ac, si, psA, psB, v2, v1, v0, acv, pav = views(f)
                ak, ck, pk = f"aext{sl}", f"acc{sl}", f"ps{bank}"
                for (c0p, off) in ((0, 0), (256, DFF)):
                    for kc in range(8):
                        mm(psum[bank][:, c0p:c0p + NT], w_up[:, kc, off + f * 128:off + (f + 1) * 128], h2T[:, kc, 0:NT],
                           (f"wuk{kc}", "h2T"), (pk,), start=(kc == 0), stop=(kc == 7))
                if kind == "p":
                    cp("dve", ae[:, 0:2], halo2[ti % 2][:, f, :], (f"halo{ti % 2}_{f}",), (ak + "h",))
                    fw.add("act", lambda e, o_=ae[:, 2:2 + NT], i_=psA: e.copy(out=o_, in_=i_), (pk,), (ak,), weak=(pk,))
                    cp("act", halo2[(ti + 1) % 2][:, f, :], ae[:, NT:NT + 2], (ak,), (f"halo{(ti + 1) % 2}_{f}",))
                    if last_prompt:
                        cp("act", atail[:, f, 0:2], ae[:, NT:NT + 2], (ak,), (f"atail{f}",))
                else:
                    ae3 = ae[:, 0:96].rearrange("p (s w) -> p s w", w=6)
                    cp("dve", ae3[:, :, 0:2], conv0T[f].rearrange("p (s w) -> p s w", w=2), ("conv0T",), (ak + "h",))
                    cp("act", ae3[:, :, 2:6], pav, (pk,), (ak,))
                    cp("act", atail[:, f, 2:34].rearrange("p (s w) -> p s w", w=2), ae3[:, :, 4:6], (ak,), (f"atail{f}",))
                fw.add("act", lambda e, acv=acv, pav=pav, f=f: e.activation(
                    out=acv, in_=pav, func=AF.Identity, scale=cw[:, 2, f:f + 1], bias=C["cb_col"][:, f:f + 1]),
                    (pk, "cw_col", "cb_col"), (ck,), weak=(pk,))

            def stB(f):
                sl, bank, ae, ac, si, psA, psB, v2, v1, v0, acv, pav = views(f)
                ak, ck = f"aext{sl}", f"acc{sl}"
                stt(acv, v1, cw[:, 1, f:f + 1], acv, ALU.mult, ALU.add, (ak, ak + "h", ck, "cw_col"), (ck,))
                stt(acv, v0, cw[:, 0, f:f + 1], acv, ALU.mult, ALU.add, (ak, ak + "h", ck, "cw_col"), (ck,))

            def stC(f):
                sl, bank, ae, ac, si, psA, psB, v2, v1, v0, acv, pav = views(f)
                act(si[:, 0:NT], ac[:, 0:NT], AF.Silu, (f"acc{sl}",), (f"sil{sl}",))

            def stD(f):
                sl, bank, ae, ac, si, psA, psB, v2, v1, v0, acv, pav = views(f)
                tt("dve", gT[:, f, 0:NT], si[:, 0:NT], psB, ALU.mult, (f"sil{sl}", f"ps{bank}"), (f"gT{f}",))

            for step in range(NF + 3):
                for (stg, lag) in ((stD, 3), (stC, 2), (stB, 1), (stA, 0)):
                    f = step - lag
                    if 0 <= f < NF:
                        stg(f)

        for sb, _ in tinfo(0)[3]:
            pro(0, sb)()
        for ti in range(len(tiles)):
            ffn_up(ti)
            cur = tinfo(ti)[3]
            nxt = tinfo(ti + 1)[3] if ti + 1 < len(tiles) else []
            for j in range(max(len(cur), len(nxt))):
                pbf = pro(ti + 1, j) if j < len(nxt) else None
                if j < len(cur):
                    epi(ti, j)
                if pbf is not None:
                    pbf()
        for f in range(NF):
            bk = 3 + (f // 4) % 4
            tr(psum[bk][0:34, (f % 4) * 128:(f % 4 + 1) * 128], atail[:, f, :], C["ident_f"], (f"atail{f}", "ident_f"),
               (f"ps{bk}",))
            if f % 4 == 3 or f == NF - 1:
                f0 = f - f % 4
                nn = (f - f0 + 1) * 128
                cp("act", tailT[:, f0 * 128:f0 * 128 + nn], psum[bk][0:34, 0:nn], (f"ps{bk}",), GTK)
        store(O["convo"], tailT, GTK, "convo")

    def dbg_rows(name, ap, r0, n):
        store(O[name][r0:r0 + n, :], ap, ("xb",), name)

    load_consts()
    if stop_after != "consts":
        load_w_in()
        phase0()
    if stop_after is None or stop_after.startswith("p1") or stop_after.startswith("g"):
        phase1a()
        if "d_Y" in DEBUG:
            dbg("d_Y", P1["Y"].rearrange("p a g j c -> p (a g j c)"), ("Y",))
            dbg("d_Ys", P1["Ys"].rearrange("p g j c -> p (g j c)"), ("Ys",))
        if stop_after is None or stop_after.startswith("p1b") or stop_after.startswith("p2"):
            phase1b()
        if "d_mix" in DEBUG:
            fw.add("sp", None, ("mix_d",), ("OUT_mixd",))
            out_chans.append("o_mixd")
        if stop_after is None or stop_after.startswith("p2"):
            phase2()

    fw.add("sp", None, tuple("OUT_" + c[2:] for c in out_chans), ())
    fw.finalize()
    fw.simulate()
    block = st.enter_context(nc.Block())
    fw.emit(nc, st, block)
    st.close()
    return nc


def make_in_maps(inputs):
    f = lambda a: np.ascontiguousarray(np.asarray(a, dtype=np.float32))
    shared = {
        "attn_norm": f(inputs["attn_norm"][0]), "w_in": f(inputs["w_in"][0]),
        "A_re": f(inputs["s5_A_re"][0]), "A_im": f(inputs["s5_A_im"][0]),
        "B_re": f(inputs["s5_B_re"][0]), "B_im": f(inputs["s5_B_im"][0]),
        "C_re": f(inputs["s5_C_re"][0]), "C_im": f(inputs["s5_C_im"][0]),
        "Dp": f(inputs["s5_D"][0]), "log_step": f(inputs["s5_log_step"][0]),
        "w_glu": f(inputs["w_glu"][0]), "b_glu": f(inputs["b_glu"][0]),
        "s5_out_norm": f(inputs["s5_out_norm"][0]), "w_gate_up": f(inputs["w_gate_up"][0]),
        "b_gate": f(inputs["b_gate"][0]), "gla_out_norm": f(inputs["gla_out_norm"][0]),
        "w_o": f(inputs["w_o"][0]), "ffn_norm": f(inputs["ffn_norm"][0]), "w_up": f(inputs["w_up"][0]),
        "conv_w": f(inputs["conv_w"][0]), "conv_b": f(inputs["conv_b"][0]), "w_down": f(inputs["w_down"][0]),
        "final_norm": f(inputs["final_norm"]),
    }
    maps = []
    for c in range(NCORES):
        m = dict(shared)
        sl = slice(NSEQ * c, NSEQ * (c + 1))
        m["xp"] = f(inputs["x_prompt"][c])
        m["xs"] = f(inputs["x_sample"][sl]).reshape(TS_, D)
        m["s5r"] = f(inputs["state_s5_re"][0, sl]).reshape(NSEQ, 2048)
        m["s5i"] = f(inputs["state_s5_im"][0, sl]).reshape(NSEQ, 2048)
        m["sgla"] = f(inputs["state_gla"][0, sl])
        m["sconv"] = f(inputs["state_conv"][0, sl]).reshape(2 * NSEQ, DFF)
        maps.append(m)
    return maps


_NC_CACHE = {}


def kernel(**inputs):
    if "nc" not in _NC_CACHE:
        _NC_CACHE["nc"] = build()
    nc = _NC_CACHE["nc"]
    in_maps = make_in_maps(inputs)
    res = run_bass_kernel_spmd(nc, in_maps, core_ids=list(range(NCORES)))
    R = res.results
    yp = np.stack([R[c]["yp"] for c in range(NCORES)]).reshape(8, 2048, D)
    ys = np.concatenate([R[c]["ys"].reshape(NSEQ, 4, D) for c in range(NCORES)], 0)
    p_re = np.stack([R[c]["s5o_r"][0].reshape(32, 64) for c in range(NCORES)])[None]
    p_im = np.stack([R[c]["s5o_i"][0].reshape(32, 64) for c in range(NCORES)])[None]
    s_re = np.concatenate([R[c]["s5o_r"][1:].reshape(NSEQ, 32, 64) for c in range(NCORES)], 0)[None]
    s_im = np.concatenate([R[c]["s5o_i"][1:].reshape(NSEQ, 32, 64) for c in range(NCORES)], 0)[None]
    p_gla = np.stack([R[c]["glao"][0] for c in range(NCORES)])[None]
    s_gla = np.concatenate([R[c]["glao"][1:] for c in range(NCORES)], 0)[None]
    p_conv = np.stack([R[c]["convo"][0:2] for c in range(NCORES)])[None]
    s_conv = np.concatenate([R[c]["convo"][2:].reshape(NSEQ, 2, DFF) for c in range(NCORES)], 0)[None]
    out = (yp, ys, p_re, p_im, p_gla, p_conv, s_re, s_im, s_gla, s_conv)
    return tuple(np.ascontiguousarray(o, dtype=np.float32) for o in out)
```

```python
import contextlib
import math
import numpy as np
import concourse.bass as bass
import concourse.mybir as mybir
from concourse.bass_utils import run_bass_kernel_spmd

F32 = mybir.dt.float32
BF16 = mybir.dt.bfloat16
I32 = mybir.dt.int32
U8 = mybir.dt.uint8
AF = mybir.ActivationFunctionType
ALU = mybir.AluOpType
ESZ = {F32: 4, BF16: 2, I32: 4, U8: 1}

NCORES = 8
D = 1024
TP_ = 2048
TS_ = 64
NSEQ = 16
INC = 2064
DFF = 2816
NF = 22
EPS = 1e-6
MCOL = 272
TWO_PI = 2.0 * math.pi
MAGIC = 12582912.0

DEBUG = {}
SAME_ENGINE_WAR = True


class Op:
    __slots__ = ("eng", "fn", "R", "W", "chan", "deps", "inc", "val", "waits", "barrier", "weak")

    def __init__(self, eng, fn, R, W, chan=None, barrier=False, weak=()):
        self.eng, self.fn, self.R, self.W, self.chan = eng, fn, tuple(R), tuple(W), chan
        self.weak = tuple(weak)
        self.deps = set()
        self.inc = False
        self.val = 0
        self.waits = []
        self.barrier = barrier


class FW:
    ENGS = ("pe", "act", "dve", "pool", "sp")

    def __init__(self):
        self.ops = []

    def add(self, eng, fn, R=(), W=(), chan=None, weak=()):
        self.ops.append(Op(eng, fn, R, W, chan, weak=weak))

    def barrier(self, engs=None):
        for e in (engs or self.ENGS):
            self.ops.append(Op(e, None, (), (), None, barrier=True))

    def finalize(self):
        ops = self.ops
        lastw = {}
        readers = {}
        last_on = {}
        bar_start = None
        i = 0
        n = len(ops)
        while i < n:
            op = ops[i]
            if op.barrier:
                j = i
                snap = dict(last_on)
                while j < n and ops[j].barrier:
                    ops[j].deps = set(snap.values())
                    j += 1
                for k in range(i, j):
                    last_on[("e", ops[k].eng)] = k
                i = j
                continue
            deps = set()
            for k in op.R:
                if k in lastw:
                    deps.add(lastw[k])
            for k in op.W:
                if k in lastw:
                    deps.add(lastw[k])
                for r in readers.get(k, {}).values():
                    deps.add(r)
            deps.discard(i)
            op.deps = deps
            rk = ("c", op.chan) if op.chan is not None else ("e", op.eng)
            for k in op.R:
                if k not in op.weak:
                    readers.setdefault(k, {})[rk] = i
            for k in op.W:
                lastw[k] = i
                readers[k] = {}
            if op.chan is not None:
                last_on[("c", op.chan)] = i
            else:
                last_on[("e", op.eng)] = i
            i += 1
        need = [set() for _ in ops]
        for i, op in enumerate(ops):
            for d in op.deps:
                dop = ops[d]
                if dop.chan is None and dop.eng == op.eng and op.chan is None:
                    if op.eng in ("pe", "sp"):
                        continue
                    if dop.fn is None:
                        continue
                    hazard = (set(dop.W) & (set(op.R) | set(op.W)))
                    if SAME_ENGINE_WAR:
                        hazard = hazard or (set(dop.R) & set(op.W))
                    if not hazard and not op.barrier:
                        continue
                if dop.chan is not None and dop.chan == op.chan:
                    continue
                need[i].add(d)
                dop.inc = True
        for op in ops:
            if op.chan is not None:
                op.inc = True
        cnt = {}
        for op in ops:
            if op.inc:
                key = ("c", op.chan) if op.chan is not None else ("e", op.eng)
                cnt[key] = cnt.get(key, 0) + (16 if op.chan is not None else 1)
                op.val = cnt[key]
        self.sem_keys = sorted(cnt.keys(), key=str)
        for op in ops:
            if op.inc and op.chan is not None and str(op.chan).startswith("G:"):
                op.val = cnt[("c", op.chan)]
        waited = {e: {} for e in self.ENGS}
        for i, op in enumerate(ops):
            best = {}
            for d in need[i]:
                dop = ops[d]
                key = ("c", dop.chan) if dop.chan is not None else ("e", dop.eng)
                if dop.val > best.get(key, 0):
                    best[key] = dop.val
            w = waited[op.eng]
            for key, v in best.items():
                if v > w.get(key, 0):
                    w[key] = v
                    op.waits.append((key, v))

    def simulate(self):
        per = {e: [op for op in self.ops if op.eng == e] for e in self.ENGS}
        pos = {e: 0 for e in self.ENGS}
        sem = {}
        progress = True
        while progress:
            progress = False
            for e in self.ENGS:
                lst = per[e]
                while pos[e] < len(lst):
                    op = lst[pos[e]]
                    if all(sem.get(k, 0) >= v for k, v in op.waits):
                        if op.inc:
                            key = ("c", op.chan) if op.chan is not None else ("e", op.eng)
                            sem[key] = sem.get(key, 0) + (16 if op.chan is not None else 1)
                        pos[e] += 1
                        progress = True
                    else:
                        break
        stuck = {e: pos[e] for e in self.ENGS if pos[e] < len(per[e])}
        if stuck:
            for e, p in stuck.items():
                op = per[e][p]
                print("DEADLOCK", e, p, "waits", op.waits, "R", op.R, "W", op.W,
                      "have", {k: sem.get(k, 0) for k, _ in op.waits})
            raise RuntimeError("deadlock in sync plan")
        return True

    def emit(self, nc, st, block):
        sems = {}
        for key in self.sem_keys:
            sems[key] = st.enter_context(nc.semaphore("s_" + "_".join(str(x) for x in key)))
        per = {e: [op for op in self.ops if op.eng == e] for e in self.ENGS}

        def run(engobj, lst):
            for op in lst:
                for key, v in op.waits:
                    engobj.wait_ge(sems[key], v)
                if op.fn is None:
                    if op.inc:
                        engobj.nop().then_inc(sems[("e", op.eng)], 1)
                    continue
                ins = op.fn(engobj)
                if op.inc:
                    key = ("c", op.chan) if op.chan is not None else ("e", op.eng)
                    ins.then_inc(sems[key], 16 if op.chan is not None else 1)

        @block.tensor
        def _(e):
            run(e, per["pe"])

        @block.scalar
        def _(e):
            run(e, per["act"])

        @block.vector
        def _(e):
            run(e, per["dve"])

        @block.gpsimd
        def _(e):
            run(e, per["pool"])

        @block.sync
        def _(e):
            run(e, per["sp"])


class Arena:
    def __init__(self, u8ap, lo, hi):
        self.ap, self.lo, self.hi, self.top = u8ap, lo, hi, lo

    def reset(self, top=None):
        self.top = self.lo if top is None else top

    def alloc(self, shape, dtype):
        free = 1
        for s in shape[1:]:
            free *= s
        nb = free * ESZ[dtype]
        off = (self.top + 63) // 64 * 64
        assert off + nb <= self.hi, f"arena overflow {off + nb} > {self.hi}"
        self.top = off + nb
        v = self.ap[:, off:off + nb].bitcast(dtype)
        if len(shape) > 2:
            names = " ".join(f"d{i}" for i in range(len(shape) - 1))
            kw = {f"d{i}": shape[i + 1] for i in range(len(shape) - 1)}
            v = v.rearrange(f"p ({names}) -> p {names}", **kw)
        if shape[0] < 128:
            v = v[0:shape[0]]
        return v


def bc(ap, axis, n):
    l = [list(t) for t in ap.ap]
    l.insert(axis, [0, n])
    return bass.AP(ap.tensor, ap.offset, l)


def dap(t, offset, dims):
    return bass.AP(t.tensor, offset, [list(d) for d in dims])


def build(stop_after=None):
    nc = bass.Bass("TRN2", target_bir_lowering=False)
    fw = FW()

    def din(name, shape, dt=F32):
        return nc.dram_tensor(name, list(shape), dt, kind="ExternalInput").ap()

    def dout(name, shape, dt=F32):
        return nc.dram_tensor(name, list(shape), dt, kind="ExternalOutput").ap()

    I = {}
    I["xp"] = din("xp", [TP_, D])
    I["xs"] = din("xs", [TS_, D])
    I["s5r"] = din("s5r", [NSEQ, 2048])
    I["s5i"] = din("s5i", [NSEQ, 2048])
    I["sgla"] = din("sgla", [NSEQ, 4, 64, 128])
    I["sconv"] = din("sconv", [2 * NSEQ, DFF])
    for nm, shp in [("attn_norm", [D]), ("w_in", [D, INC]), ("A_re", [32, 64]), ("A_im", [32, 64]),
                    ("B_re", [32, 64, 16]), ("B_im", [32, 64, 16]), ("C_re", [32, 16, 64]),
                    ("C_im", [32, 16, 64]), ("Dp", [32, 16]), ("log_step", [32]), ("w_glu", [512, 512]),
                    ("b_glu", [512]), ("s5_out_norm", [512]), ("w_gate_up", [16, 256]), ("b_gate", [256]),
                    ("gla_out_norm", [128]), ("w_o", [D, D]), ("ffn_norm", [D]), ("w_up", [D, 2 * DFF]),
                    ("conv_w", [3, DFF]), ("conv_b", [DFF]), ("w_down", [DFF, D]), ("final_norm", [D])]:
        I[nm] = din(nm, shp)
    O = {}
    O["yp"] = dout("yp", [TP_, D])
    O["ys"] = dout("ys", [TS_, D])
    O["s5o_r"] = dout("s5o_r", [17, 2048])
    O["s5o_i"] = dout("s5o_i", [17, 2048])
    O["glao"] = dout("glao", [17, 4, 64, 128])
    O["convo"] = dout("convo", [34, DFF])
    for k, (shp, dts) in DEBUG.items():
        O[k] = dout(k, shp, BF16 if dts == "bf16" else F32)
    if "d_mix" in DEBUG:
        mix_d = O["d_mix"]
    else:
        mix_d = nc.dram_tensor("mix_d", [128, 8, TP_ + TS_], BF16, kind="Internal").ap()

    st = contextlib.ExitStack()
    ARENA_BYTES = 206 * 1024
    arena_t = st.enter_context(nc.sbuf_tensor("arena", [128, ARENA_BYTES], U8))
    psum = [st.enter_context(nc.psum_tensor(f"ps{i}", [128, 512], F32)) for i in range(8)]
    CONST = Arena(arena_t, 0, 14 * 1024)
    TAB = Arena(arena_t, 14 * 1024, 46 * 1024)
    REST = Arena(arena_t, 46 * 1024, ARENA_BYTES)

    def psb(i, n=1024):
        return psum[i][:, 0:n // 2].bitcast(BF16)

    dma_ctr = [0]

    def dma(out, in_, R, W, chan, eng="sp", **kw):
        def fn(e, out=out, in_=in_, kw=kw):
            return e.dma_start(out=out, in_=in_, **kw)
        fw.add(eng, fn, R, W, chan=chan)

    def mm(out, lhsT, rhs, R, W, start=True, stop=True):
        def fn(e):
            return e.matmul(out, lhsT, rhs, start=start, stop=stop)
        fw.add("pe", fn, R, W)

    def tr(out, in_, ident, R, W):
        def fn(e):
            return e.transpose(out=out, in_=in_, identity=ident)
        fw.add("pe", fn, R, W)

    def act(out, in_, func, R, W, **kw):
        def fn(e):
            return e.activation(out=out, in_=in_, func=func, **kw)
        fw.add("act", fn, R, W)

    def tt(eng, out, in0, in1, op, R, W):
        def fn(e):
            return e.tensor_tensor(out=out, in0=in0, in1=in1, op=op)
        fw.add(eng, fn, R, W)

    def ts(eng, out, in0, s1, op0, R, W, s2=None, op1=None, accum_out=None):
        def fn(e):
            if op1 is None:
                return e.tensor_scalar(out=out, in0=in0, scalar1=s1, scalar2=None, op0=op0)
            return e.tensor_scalar(out=out, in0=in0, scalar1=s1, scalar2=s2, op0=op0, op1=op1)
        fw.add(eng, fn, R, W)

    def stt(out, in0, scalar, in1, op0, op1, R, W):
        def fn(e):
            return e.scalar_tensor_tensor(out=out, in0=in0, scalar=scalar, in1=in1, op0=op0, op1=op1)
        fw.add("dve", fn, R, W)

    def cp(eng, out, in_, R, W):
        if eng == "act":
            def fn(e):
                return e.copy(out=out, in_=in_)
        else:
            def fn(e):
                return e.tensor_copy(out=out, in_=in_)
        fw.add(eng, fn, R, W)

    def mset(eng, ap, val, W):
        def fn(e):
            return e.memset(ap, val)
        fw.add(eng, fn, (), W)

    def recip(out, in_, R, W):
        def fn(e):
            return e.reciprocal(out=out, in_=in_)
        fw.add("dve", fn, R, W)

    def iota(out, pattern, base, cm, W):
        def fn(e):
            return e.iota(out, pattern=pattern, base=base, channel_multiplier=cm)
        fw.add("pool", fn, (), W)

    out_chans = []

    def store(out, in_, R, name):
        ch = "o_" + name
        if ch not in out_chans:
            out_chans.append(ch)
        dma(out, in_, R, ("OUT_" + name,), ch)

    def dbg(name, ap, R):
        if name in DEBUG:
            store(O[name], ap, R, name)

    C = {}
    C["ident_f"] = CONST.alloc([128, 128], F32)
    C["ident_b"] = CONST.alloc([128, 128], BF16)
    C["tri_f"] = CONST.alloc([128, 128], F32)
    C["ones_b"] = CONST.alloc([128, 128], BF16)
    C["gnorm_row"] = CONST.alloc([128, 128], F32)
    C["fin_row"] = CONST.alloc([128, 1024], F32)
    C["attn_col"] = CONST.alloc([128, 8], F32)
    C["ffn_col"] = CONST.alloc([128, 8], F32)
    C["s5n_col"] = CONST.alloc([128, 4], F32)
    C["bglu_col"] = CONST.alloc([128, 4], F32)
    C["cw_col"] = CONST.alloc([128, 3, NF], F32)
    C["cb_col"] = CONST.alloc([128, NF], F32)
    C["wgate"] = CONST.alloc([16, 256], BF16)
    C["bgate"] = CONST.alloc([1, 256], BF16)
    C["ones_row"] = CONST.alloc([1, 128], BF16)
    C["padmask"] = CONST.alloc([128, 1], F32)
    C["eps_col"] = CONST.alloc([128, 1], F32)
    C["L8"] = CONST.alloc([128, 2, 16], F32)
    C["LK"] = CONST.alloc([128, 2, 4, 16], F32)
    TPW_OFF = (CONST.top + 63) // 64 * 64
    C["TPW"] = CONST.alloc([128, 2, 16, 16], F32)
    C["Lm4"] = CONST.alloc([128, 2, 16], F32)
    C["fin"] = CONST.alloc([128, 2, 16, 17], F32)
    C["itmp"] = st.enter_context(nc.sbuf_tensor("iota_i32", [128, 128], I32))[:, :]
    C["pmtmp"] = CONST.alloc([128, 8], F32)

    def load_consts():
        tmpi = C["itmp"]
        iota(tmpi, [[1, 128]], 0, -1, ("itmp",))
        cp("pool", C["ident_f"], tmpi, ("itmp",), ("c_tmpf",))
        ts("dve", C["ident_b"], C["ident_f"], 0.0, ALU.is_equal, ("c_tmpf",), ("ident_b",))
        ts("dve", C["tri_f"], C["ident_f"], 0.0, ALU.is_ge, ("c_tmpf", "ident_b"), ("tri_f",))
        ts("dve", C["ident_f"], C["ident_f"], 0.0, ALU.is_equal, ("tri_f", "ident_b", "c_tmpf"), ("ident_f",))
        mset("dve", C["tri_f"][0:64, 64:128], 0.0, ("tri_f",))
        mset("dve", C["ones_b"], 1.0, ("ones_b",))
        mset("dve", C["ones_row"], 1.0, ("ones_row",))
        mset("dve", C["padmask"], 0.0, ("padmask",))
        mset("dve", C["eps_col"], EPS, ("eps_col",))
        C["mhalf"] = C["pmtmp"][:, 4:5]
        mset("dve", C["mhalf"], -0.5, ("mhalf",))
        iota(tmpi[:, 0:1], [[0, 1]], 0, 1, ("itmp",))
        cp("pool", C["padmask"], tmpi[:, 0:1], ("itmp",), ("padmask",))
        pm = C["pmtmp"]
        ts("dve", pm[:, 1:2], C["padmask"], 60.0, ALU.is_ge, ("padmask",), ("pmtmp",))
        ts("dve", pm[:, 2:3], C["padmask"], 64.0, ALU.is_ge, ("padmask",), ("pmtmp",))
        ts("dve", pm[:, 3:4], C["padmask"], 124.0, ALU.is_ge, ("padmask",), ("pmtmp",))
        tt("dve", pm[:, 1:2], pm[:, 1:2], pm[:, 2:3], ALU.subtract, ("pmtmp",), ("pmtmp",))
        tt("dve", C["padmask"], pm[:, 1:2], pm[:, 3:4], ALU.add, ("pmtmp",), ("padmask",))
        dma(C["gnorm_row"], dap(I["gla_out_norm"], 0, [[0, 128], [1, 128]]), (), ("gnorm_row",), "G:cst")
        dma(C["fin_row"], dap(I["final_norm"], 0, [[0, 128], [1, 1024]]), (), ("fin_row",), "G:cst")
        dma(C["attn_col"], dap(I["attn_norm"], 0, [[1, 128], [128, 8]]), (), ("attn_col",), "G:cst",
            allow_slow_non_contiguous=True)
        dma(C["ffn_col"], dap(I["ffn_norm"], 0, [[1, 128], [128, 8]]), (), ("ffn_col",), "G:cst",
            allow_slow_non_contiguous=True)
        dma(C["s5n_col"], dap(I["s5_out_norm"], 0, [[1, 128], [128, 4]]), (), ("s5n_col",), "G:cst",
            allow_slow_non_contiguous=True)
        dma(C["bglu_col"], dap(I["b_glu"], 0, [[1, 128], [128, 4]]), (), ("bglu_col",), "G:cst",
            allow_slow_non_contiguous=True)
        dma(C["cw_col"], dap(I["conv_w"], 0, [[1, 128], [DFF, 3], [128, NF]]), (), ("cw_col",), "G:cst",
            allow_slow_non_contiguous=True)
        dma(C["cb_col"], dap(I["conv_b"], 0, [[1, 128], [128, NF]]), (), ("cb_col",), "G:cst",
            allow_slow_non_contiguous=True)
        dma(C["wgate"], I["w_gate_up"], (), ("wgate",), "G:cstp", eng="pool")
        dma(C["bgate"], dap(I["b_gate"], 0, [[0, 1], [1, 256]]), (), ("bgate",), "G:cstp", eng="pool")

    CK = ("ident_f", "ident_b", "tri_f")

    T = {}
    T["BL"] = TAB.alloc([128, 32, 2, 64], BF16)
    T["T0"] = TAB.alloc([128, 32, 128], BF16)
    T["CL"] = TAB.alloc([128, 32, 2, 128], BF16)

    def phase0():
        R0 = Arena(arena_t, REST.lo, REST.hi)

        def A(shape, dt=F32):
            return R0.alloc(shape, dt)
        LSrow = A([128, 32]); Bnr = A([128, 16, 16]); Bni = A([128, 16, 16])
        P0_KEEP = R0.top
        Acr = A([128, 16]); Aci = A([128, 16]); dtc = A([128, 16])
        arc = A([128, 16]); thc = A([128, 16])
        dma(Acr, dap(I["A_re"], 0, [[1, 128], [128, 16]]), (), ("Acr",), "G:p0a", allow_slow_non_contiguous=True)
        dma(Aci, dap(I["A_im"], 0, [[1, 128], [128, 16]]), (), ("Aci",), "G:p0a", allow_slow_non_contiguous=True)
        dma(LSrow, dap(I["log_step"], 0, [[0, 128], [1, 32]]), (), ("LSrow",), "G:p0a")
        act(LSrow, LSrow, AF.Exp, ("LSrow",), ("LSrow",))
        cp("dve", dtc[0:64, :], LSrow[0:64, 0:32:2], ("LSrow",), ("dtc",))
        cp("dve", dtc[64:128, :], LSrow[64:128, 1:32:2], ("LSrow",), ("dtc",))
        tt("dve", arc, Acr, dtc, ALU.mult, ("Acr", "dtc"), ("arc",))
        tt("dve", thc, Aci, dtc, ALU.mult, ("Aci", "dtc"), ("thc",))
        NN = 31
        nlist = list(range(-7, 9)) + [8 * k for k in range(2, 17)]

        def nidx(n):
            return n + 7 if n <= 8 else 15 + (n // 8 - 1)
        NV = A([128, NN, 16]); argE = A([128, NN, 16]); argS = A([128, NN, 16]); tmpk = A([128, NN, 16])
        rr = A([128, NN, 16]); LPr = A([128, NN, 16]); LPi = A([128, NN, 16]); Ecol = A([128, NN, 16])
        for j, n in enumerate(nlist):
            mset("dve", NV[:, j, :], float(n), ("NV",))
        tt("dve", argE, NV, bc(arc, 1, NN), ALU.mult, ("NV", "arc"), ("argE",))
        tt("dve", argS, NV, bc(thc, 1, NN), ALU.mult, ("NV", "thc"), ("argS",))
        act(Ecol, argE, AF.Exp, ("argE",), ("Ecol",))

        def sincos(eng, out_s, out_c, arg, tmp, rtile, etile, kin, kout_s, kout_c, tkey, rkey, ekey):
            for shift, outt, kout in ((0.0, out_s, kout_s), (math.pi / 2, out_c, kout_c)):
                C1 = 6.28125
                C2 = TWO_PI - C1
                ts(eng, outt, arg, shift, ALU.add, (kin,), (kout,))
                ts(eng, tmp, outt, 1.0 / TWO_PI, ALU.mult, (kout,), (tkey,), s2=MAGIC, op1=ALU.add)
                ts(eng, tmp, tmp, -MAGIC, ALU.add, (tkey,), (tkey,))
                ts(eng, rtile, tmp, -C1, ALU.mult, (tkey,), (rkey,))
                tt(eng, rtile, rtile, outt, ALU.add, (rkey, kout), (rkey,))
                ts(eng, tmp, tmp, -C2, ALU.mult, (tkey,), (tkey,))
                tt(eng, rtile, rtile, tmp, ALU.add, (rkey, tkey), (rkey,))
                ts(eng, rtile, rtile, 3.1415925, ALU.min, (rkey,), (rkey,), s2=-3.1415925, op1=ALU.max)
                act(outt, rtile, AF.Sin, (rkey,), (kout,))
                tt(eng, outt, outt, etile, ALU.mult, (kout, ekey), (kout,))

        sincos("dve", LPi, LPr, argS, tmpk, rr, Ecol, "argS", "LPi", "LPr", "tmpk", "rr", "Ecol")
        for ri, LP in ((0, LPr), (1, LPi)):
            key = "LPr" if ri == 0 else "LPi"
            cp("dve", C["L8"][:, ri, :], LP[:, nidx(8), :], (key,), ("L8",))
            cp("dve", C["Lm4"][:, ri, :], LP[:, nidx(-4), :], (key,), ("Lm4",))
            cp("dve", C["LK"][:, ri, 0, :], LP[:, nidx(128), :], (key,), ("LK",))
            cp("dve", C["TPW"][:, ri, :, 0], LP[:, nidx(0), :], (key,), ("TPW",))
            cp("dve", C["TPW"][:, ri, :, 1], LP[:, nidx(8), :], (key,), ("TPW",))
            for k in range(2, 16):
                cp("dve", C["TPW"][:, ri, :, k], LP[:, nidx(8 * k), :], (key,), ("TPW",))
        sq1 = A([128, 16]); sq2 = A([128, 16])
        for k in range(1, 4):
            pr = C["LK"][:, 0, k - 1, :]; pi_ = C["LK"][:, 1, k - 1, :]
            tt("dve", sq1, pr, pr, ALU.mult, ("LK",), ("sq1",))
            tt("dve", sq2, pi_, pi_, ALU.mult, ("LK",), ("sq2",))
            tt("dve", C["LK"][:, 0, k, :], sq1, sq2, ALU.subtract, ("sq1", "sq2"), ("LK",))
            tt("dve", sq1, pr, pi_, ALU.mult, ("LK",), ("sq1",))
            ts("dve", C["LK"][:, 1, k, :], sq1, 2.0, ALU.mult, ("sq1",), ("LK",))
        den = A([128, 16]); crc = A([128, 16]); cic = A([128, 16]); nrc = A([128, 16]); t16 = A([128, 16])
        l1r = LPr[:, nidx(1), :]; l1i = LPi[:, nidx(1), :]
        tt("dve", den, Acr, Acr, ALU.mult, ("Acr",), ("den",))
        tt("dve", t16, Aci, Aci, ALU.mult, ("Aci",), ("t16",))
        tt("dve", den, den, t16, ALU.add, ("den", "t16"), ("den",))
        recip(den, den, ("den",), ("den",))
        ts("dve", nrc, l1r, -1.0, ALU.add, ("LPr",), ("nrc",))
        tt("dve", crc, nrc, Acr, ALU.mult, ("nrc", "Acr"), ("crc",))
        tt("dve", t16, l1i, Aci, ALU.mult, ("LPi", "Aci"), ("t16",))
        tt("dve", crc, crc, t16, ALU.add, ("crc", "t16"), ("crc",))
        tt("dve", crc, crc, den, ALU.mult, ("crc", "den"), ("crc",))
        tt("dve", cic, l1i, Acr, ALU.mult, ("LPi", "Acr"), ("cic",))
        tt("dve", t16, nrc, Aci, ALU.mult, ("nrc", "Aci"), ("t16",))
        tt("dve", cic, cic, t16, ALU.subtract, ("cic", "t16"), ("cic",))
        tt("dve", cic, cic, den, ALU.mult, ("cic", "den"), ("cic",))
        if stop_after == "p0col":
            dbg("d_dtc", dtc, ("dtc",)); dbg("d_thc", thc, ("thc",)); dbg("d_arc", arc, ("arc",))
            dbg("d_argS", argS.rearrange("p n g -> p (n g)"), ("argS",))
            dbg("d_Ecol", Ecol.rearrange("p n g -> p (n g)"), ("Ecol",))
            dbg("d_LPr", LPr.rearrange("p n g -> p (n g)"), ("LPr",))
            dbg("d_LPi", LPi.rearrange("p n g -> p (n g)"), ("LPi",))
            dbg("d_rr", rr.rearrange("p n g -> p (n g)"), ("rr",))
            dbg("d_TPW", C["TPW"].rearrange("p r g k -> p (r g k)"), ("TPW",))
            dbg("d_LK", C["LK"].rearrange("p r k g -> p (r k g)"), ("LK",))
            return
        dma(Bnr, dap(I["B_re"], 0, [[16, 128], [2048, 16], [1, 16]]), (), ("Bnr",), "G:p0b")
        dma(Bni, dap(I["B_im"], 0, [[16, 128], [2048, 16], [1, 16]]), (), ("Bni",), "G:p0b")
        Bbr = A([128, 16, 16]); Bbi = A([128, 16, 16]); t256 = A([128, 16, 16])
        crb = bc(crc, 2, 16); cib = bc(cic, 2, 16)
        tt("dve", Bbr, Bnr, crb, ALU.mult, ("Bnr", "crc"), ("Bbr",))
        tt("dve", t256, Bni, cib, ALU.mult, ("Bni", "cic"), ("t256",))
        tt("dve", Bbr, Bbr, t256, ALU.subtract, ("Bbr", "t256"), ("Bbr",))
        tt("dve", Bbi, Bni, crb, ALU.mult, ("Bni", "crc"), ("Bbi",))
        tt("dve", t256, Bnr, cib, ALU.mult, ("Bnr", "cic"), ("t256",))
        tt("dve", Bbi, Bbi, t256, ALU.add, ("Bbi", "t256"), ("Bbi",))
        if stop_after == "p0Bb":
            dbg("d_x", Bbr.rearrange("p g c -> p (g c)"), ("Bbr", "Bbi"))
            return
        Cn2r = A([128, 4, 128]); Cn2i = A([128, 4, 128])
        for h in range(2):
            dma(Cn2r[:, :, 64 * h:64 * h + 64], dap(I["C_re"], 0, [[64, 128], [8192, 4], [1, 64]]), (), ("Cn2r",), "G:p0b")
            dma(Cn2i[:, :, 64 * h:64 * h + 64], dap(I["C_im"], 0, [[64, 128], [8192, 4], [1, 64]]), (), ("Cn2i",), "G:p0b")
        Ccr = A([128, 32, 16]); Cci = A([128, 32, 16])
        for (src, dstt, pk, sk, dk) in ((Cn2r, Ccr, 0, "Cn2r", "Ccr"), (Cn2i, Cci, 1, "Cn2i", "Cci")):
            for q in range(4):
                mm(psum[pk][:, q * 128:(q + 1) * 128], src[:, q, :], C["ident_f"], (sk, "ident_f"), (f"ps{pk}",))
            cp("dve", dstt.rearrange("p g c -> p (g c)"), psum[pk][:, 0:512], (f"ps{pk}",), (dk,))
        if stop_after == "p0Cc":
            dbg("d_x", Ccr.rearrange("p g c -> p (g c)")[:, 0:256], ("Ccr", "Cci"))
            return
        mset("dve", T["CL"], 0.0, ("CL",))
        X_r = A([128, 16, 8, 16]); X_i = A([128, 16, 8, 16]); Y_r = A([128, 16, 8, 16]); Y_i = A([128, 16, 8, 16])
        t1 = A([128, 16, 8, 16]); t2 = A([128, 16, 8, 16])

        def lp(LP, n0, step, half):
            v = LP[64 * half:64 * half + 64]
            l = [list(t) for t in v.ap]
            nstr, gstr = l[1][0], l[2][0]
            return bass.AP(v.tensor, v.offset + nidx(n0) * nstr, [l[0], [gstr, 16], [step * nstr, 8], [0, 16]])

        Cown_r = A([128, 16, 16]); Cown_i = A([128, 16, 16])
        for half in range(2):
            sl = slice(64 * half, 64 * half + 64)
            cp("dve", Cown_r[sl], Ccr[sl, half:32:2, :], ("Ccr",), ("Cown_r",))
            cp("dve", Cown_i[sl], Cci[sl, half:32:2, :], ("Cci",), ("Cown_i",))

        def lpf(LP, n0, step):
            nstr, gstr = LP.ap[1][0], LP.ap[2][0]
            return bass.AP(LP.tensor, LP.offset + nidx(n0) * nstr, [list(LP.ap[0]), [gstr, 16], [step * nstr, 8], [0, 16]])
        Cr_b = bc(Cown_r, 2, 8); Ci_b = bc(Cown_i, 2, 8)
        for (n0, dst_r, dst_i, kr, ki) in ((1, None, None, None, None), (0, Y_r, Y_i, "Y_r", "Y_i")):
            Lr = lpf(LPr, n0, 1); Li = lpf(LPi, n0, 1)
            tt("dve", t1, Cr_b, Lr, ALU.mult, ("Cown_r", "LPr"), ("t1",))
            tt("dve", t2, Ci_b, Li, ALU.mult, ("Cown_i", "LPi"), ("t2",))
            if dst_r is None:
                for half in range(2):
                    sl = slice(64 * half, 64 * half + 64)
                    clr = T["CL"][sl, half:32:2, 0, :].rearrange("p g (i c) -> p g i c", i=8)
                    tt("dve", clr, t1[sl], t2[sl], ALU.subtract, ("t1", "t2"), ("CL",))
            else:
                tt("dve", dst_r, t1, t2, ALU.subtract, ("t1", "t2"), (kr,))
            tt("dve", t1, Cr_b, Li, ALU.mult, ("Cown_r", "LPi"), ("t1",))
            tt("dve", t2, Ci_b, Lr, ALU.mult, ("Cown_i", "LPr"), ("t2",))
            tt("dve", t1, t1, t2, ALU.add, ("t1", "t2"), ("t1",))
            if dst_r is None:
                for half in range(2):
                    sl = slice(64 * half, 64 * half + 64)
                    cli = T["CL"][sl, half:32:2, 1, :].rearrange("p g (i c) -> p g i c", i=8)
                    ts("dve", cli, t1[sl], -1.0, ALU.mult, ("t1",), ("CL",))
            else:
                ts("dve", dst_i, t1, -1.0, ALU.mult, ("t1",), (ki,))
        if stop_after == "p0CL":
            dbg("d_CL", T["CL"].rearrange("p g r n -> p (g r n)"), ("CL", "Y_r", "Y_i"))
            return
        Lr = bass.AP(LPr.tensor, LPr.offset + nidx(0) * LPr.ap[1][0],
                     [list(LPr.ap[0]), [LPr.ap[2][0], 16], [-LPr.ap[1][0], 8], [0, 16]])
        Li = bass.AP(LPi.tensor, LPi.offset + nidx(0) * LPi.ap[1][0],
                     [list(LPi.ap[0]), [LPi.ap[2][0], 16], [-LPi.ap[1][0], 8], [0, 16]])
        Bbr_b = bc(Bbr, 2, 8); Bbi_b = bc(Bbi, 2, 8)
        tt("dve", t1, Lr, Bbr_b, ALU.mult, ("LPr", "Bbr"), ("t1",))
        tt("dve", t2, Li, Bbi_b, ALU.mult, ("LPi", "Bbi"), ("t2",))
        tt("dve", X_r, t1, t2, ALU.subtract, ("t1", "t2"), ("X_r",))
        tt("dve", t1, Lr, Bbi_b, ALU.mult, ("LPr", "Bbi"), ("t1",))
        tt("dve", t2, Li, Bbr_b, ALU.mult, ("LPi", "Bbr"), ("t2",))
        tt("dve", X_i, t1, t2, ALU.add, ("t1", "t2"), ("X_i",))
        if stop_after == "p0X":
            dbg("d_x", X_r.rearrange("p g j c -> p (g j c)")[:, 0:256], ("X_r", "X_i"))
            return
        mask8 = A([128, 128]); Drow = A([128, 32, 16]); Dterm = A([128, 16, 8, 16]); tmpT = A([128, 16, 128])
        iota(C["itmp"], [[16, 8], [0, 16]], 15, -1, ("itmp",))
        cp("dve", mask8, C["itmp"], ("itmp",), ("mask8",))
        ts("dve", mask8, mask8, 0.0, ALU.is_ge, ("mask8",), ("mask8",))
        dma(Drow, dap(I["Dp"], 0, [[0, 128], [16, 32], [1, 16]]), (), ("Drow",), "G:p0b")
        Xm_r = A([128, 16, 128]); Xm_i = A([128, 16, 128])
        for rnd in range(2):
            tt("dve", Dterm, bc(C["ident_f"].rearrange("p (i c) -> p i c", i=8), 1, 16),
               bc(Drow[:, rnd * 16:(rnd + 1) * 16, :], 2, 8), ALU.mult, ("ident_f", "Drow"), ("Dterm",))
            for (Xm, Xs, kx, kxm) in ((Xm_r, X_r, "X_r", "Xm_r"), (Xm_i, X_i, "X_i", "Xm_i")):
                mset("dve", Xm, 0.0, (kxm,))
                for half in range(2):
                    sl = slice(64 * half, 64 * half + 64)
                    cp("dve", Xm[sl, half:16:2, :],
                       Xs[sl, rnd * 8:(rnd + 1) * 8].rearrange("p g j c -> p g (j c)"), (kx,), (kxm,))
            for gg in range(16):
                g = rnd * 16 + gg
                gp = g // 2
                bank = psum[gg // 4]
                o = bank[:, (gg % 4) * 128:(gg % 4 + 1) * 128]
                mm(o, Xm_r[:, gg, :], Y_r[:, gp].rearrange("p i c -> p (i c)"),
                   ("Xm_r", "Y_r"), (f"ps{gg // 4}",), start=True, stop=False)
                mm(o, Xm_i[:, gg, :], Y_i[:, gp].rearrange("p i c -> p (i c)"),
                   ("Xm_i", "Y_i"), (f"ps{gg // 4}",), start=False, stop=True)
            for b4 in range(4):
                tt("dve", tmpT[:, b4 * 4:(b4 + 1) * 4, :], psum[b4][:, 0:512].rearrange("p (g n) -> p g n", g=4),
                   bc(mask8, 1, 4), ALU.mult, (f"ps{b4}", "mask8"), ("tmpT",))
            tt("dve", T["T0"][:, rnd * 16:(rnd + 1) * 16, :], tmpT,
               Dterm.rearrange("p g i c -> p g (i c)"), ALU.add,
               ("tmpT", "Dterm"), ("T0",))
        BLc = [X_r, X_i]
        nstr, gstr = LPr.ap[1][0], LPr.ap[2][0]
        LrB = bass.AP(LPr.tensor, LPr.offset + nidx(7) * nstr, [list(LPr.ap[0]), [gstr, 16], [-nstr, 8], [0, 16]])
        LiB = bass.AP(LPi.tensor, LPi.offset + nidx(7) * nstr, [list(LPi.ap[0]), [gstr, 16], [-nstr, 8], [0, 16]])
        tt("dve", t1, LrB, Bbr_b, ALU.mult, ("LPr", "Bbr"), ("t1",))
        tt("dve", t2, LiB, Bbi_b, ALU.mult, ("LPi", "Bbi"), ("t2",))
        tt("dve", BLc[0], t1, t2, ALU.subtract, ("t1", "t2"), ("X_r",))
        tt("dve", t1, LrB, Bbi_b, ALU.mult, ("LPr", "Bbi"), ("t1",))
        tt("dve", t2, LiB, Bbr_b, ALU.mult, ("LPi", "Bbr"), ("t2",))
        tt("dve", BLc[1], t1, t2, ALU.add, ("t1", "t2"), ("X_i",))
        cnt = 0
        for ri in range(2):
            for q4 in range(4):
                bk = 4 + cnt % 4
                cnt += 1
                for gq in range(4):
                    gp = q4 * 4 + gq
                    tr(psum[bk][:, gq * 128:(gq + 1) * 128], BLc[ri][:, gp].rearrange("p j c -> p (j c)"), C["ident_f"],
                       ("X_r" if ri == 0 else "X_i", "ident_f"), (f"ps{bk}",))
                cp("act" if cnt % 2 == 0 else "dve", T["BL"][:, q4 * 8:(q4 + 1) * 8, ri, :],
                   psum[bk][:, 0:512].rearrange("p (g q) -> p g q", g=8), (f"ps{bk}",), ("BL",))
        if "d_BL" in DEBUG:
            dbg("d_BL", T["BL"].rearrange("p g r q -> p (g r q)"), ("BL",))
            dbg("d_T0", T["T0"].rearrange("p g n -> p (g n)"), ("T0",))
            dbg("d_CL", T["CL"].rearrange("p g r n -> p (g r n)"), ("CL",))
            dbg("d_TPW", C["TPW"].rearrange("p r g k -> p (r g k)"), ("TPW",))
            dbg("d_LK", C["LK"].rearrange("p r k g -> p (r k g)"), ("LK",))


    W_IN_BYTES = 8 * INC * 2
    w_in_sb = arena_t[:, ARENA_BYTES - W_IN_BYTES:ARENA_BYTES].bitcast(BF16).rearrange("p (k n) -> p k n", k=8)
    REST.hi = ARENA_BYTES - W_IN_BYTES - 64

    def load_w_in():
        for kc in range(8):
            dma(w_in_sb[:, kc, :], I["w_in"][kc * 128:(kc + 1) * 128, :], (), ("w_in",), "G:w_in", eng="pool",
                max_dma_last_dim=4096)

    P1 = {}

    def alloc_p1():
        REST.reset()
        A = REST.alloc
        P1["Y"] = A([128, 2, 32, 8, 16], BF16)
        P1["Ys"] = A([16, 32, 8, 16], BF16)
        P1["P1_KEEP"] = REST.top
        P1["hT"] = A([128, 8, 1024], BF16)
        P1["hTs"] = A([128, 8, 64], BF16)
        P1["hTpad"] = A([128, 8, 128], BF16)
        P1["x"] = [A([128, 1024], F32) for _ in range(2)]
        P1["hb"] = [A([128, 1024], BF16) for _ in range(2)]
        P1["st"] = A([128, 16], F32)
        P1["qf"] = A([128, 2, 512], F32)
        P1["kf"] = A([128, 2, 512], F32)
        P1["zT"] = A([16, 512], BF16)
        P1["vtok"] = [A([128, 512], BF16) for _ in range(6)]
        P1["gsil"] = [A([128, 512], BF16) for _ in range(6)]
        P1["lnv"] = A([128, 256], F32)
        P1["Epl"] = [A([128, 2, 128], F32) for _ in range(2)]
        P1["Emi"] = A([128, 2, 128], F32)
        P1["qeT"] = A([128, 2, 128], BF16)
        P1["QA"] = [A([128, 2, 128], BF16) for _ in range(2)]
        P1["QB"] = [A([128, 2, 128], BF16) for _ in range(2)]
        P1["keT"] = A([128, 2, 128], BF16)
        P1["ketok"] = A([128, 256], BF16)
        P1["attm"] = [A([128, 4, 128], BF16) for _ in range(2)]
        P1["Psb"] = [A([128, 2, 2, 128], F32) for _ in range(2)]
        P1["Sf"] = [A([128, 2, 128], F32) for _ in range(2)]
        P1["Sblk"] = [A([128, 2, 2, 128], BF16) for _ in range(2)]
        P1["Fst"] = [A([128, 2, 128], F32) for _ in range(2)]
        P1["stmp"] = A([128, 2, 128], F32)
        P1["osq"] = A([128, 4, 128], F32)
        P1["on"] = [A([128, 512], BF16) for _ in range(2)]
        P1["mst"] = A([128, 4, 128], BF16)
        P1["P1_TOP"] = REST.top

    xctr = [0]

    def norm_tile(x_dram_rows, nrows, hT_dst, gain_col, xkey_pref="x"):
        slot = xctr[0] % 2
        xctr[0] += 1
        xb = P1["x"][slot][0:nrows]
        hb = P1["hb"][slot][0:nrows]
        xk, hk, sk = f"x{slot}", f"hb{slot}", f"st{slot}"
        ss = P1["st"][0:nrows, slot * 4:slot * 4 + 1]
        rs = P1["st"][0:nrows, slot * 4 + 1:slot * 4 + 2]
        dma(xb, x_dram_rows, (), (xk,), f"x{slot}")
        fw.add("act", lambda e: e.activation(out=hb, in_=xb, func=AF.Square, accum_out=ss), (xk,), (hk, sk))
        act(rs, ss, AF.Ln, (sk,), (sk,), scale=1.0 / D, bias=C["eps_col"][0:nrows])
        act(rs, rs, AF.Exp, (sk,), (sk,), scale=-0.5)
        fw.add("act", lambda e: e.activation(out=hb, in_=xb, func=AF.Copy, scale=rs), (xk, sk), (hk,))
        pb = psb(0)
        for kc in range(8):
            tr(pb[0:128, kc * 128:kc * 128 + nrows], hb[:, kc * 128:(kc + 1) * 128], C["ident_b"][0:nrows, 0:nrows],
               (hk, "ident_b"), ("ps0a", "ps0b"))
        pv = pb.rearrange("p (k n) -> p k n", k=8)[:, :, 0:nrows]
        tt("dve", hT_dst, pv, bc(gain_col, 2, nrows), ALU.mult, ("ps0a", "ps0b", "attn_col"), ("hT",))

    def proj_qkz(hT_ap, n, tagR=("hT",)):
        idx = 0
        for (dst, col0, key) in ((P1["qf"], 512, "qf"), (P1["kf"], 768, "kf")):
            for c2 in range(2):
                b = 1 + idx % 2
                idx += 1
                for kc in range(8):
                    mm(psum[b][:, 0:n], w_in_sb[:, kc, col0 + c2 * 128:col0 + (c2 + 1) * 128], hT_ap[:, kc, :],
                       ("w_in",) + tagR, (f"ps{b}",), start=(kc == 0), stop=(kc == 7))
                cp("act", dst[:, c2, 0:n], psum[b][:, 0:n], (f"ps{b}",), (key,))
        b = 1 + idx % 2
        for kc in range(8):
            mm(psum[b][0:16, 0:n], w_in_sb[:, kc, 2048:2064], hT_ap[:, kc, :], ("w_in",) + tagR, (f"ps{b}",),
               start=(kc == 0), stop=(kc == 7))
        cp("dve", P1["zT"][:, 0:n], psum[b][0:16, 0:n], (f"ps{b}",), ("zT",))

    def proj_vg(hT_ap, slot):
        vtok, gsil = P1["vtok"][slot], P1["gsil"][slot]
        kv, kg = f"vtok{slot}", f"gsil{slot}"
        for (b, col0) in ((3, 1024), (4, 1536)):
            for kc in range(8):
                mm(psum[b][:, 0:512], hT_ap[:, kc, :], w_in_sb[:, kc, col0:col0 + 512], ("hT", "w_in"), (f"ps{b}",),
                   start=(kc == 0), stop=(kc == 7))
        cp("dve", vtok, psum[3][:, 0:512], ("ps3",), (kv,))
        act(gsil, psum[4][:, 0:512], AF.Silu, ("ps4",), (kg,))
        tt("dve", gsil.rearrange("p (h v) -> p h v", h=4), gsil.rearrange("p (h v) -> p h v", h=4),
           bc(C["gnorm_row"], 1, 4), ALU.mult, (kg, "gnorm_row"), (kg,))

    def gla_front(hT_ap, c0, mode, par, slot, part):
        vtok, gsil, Epl = P1["vtok"][slot], P1["gsil"][slot], P1["Epl"][par]
        QA, QB, attm, Psb = P1["QA"][par], P1["QB"][par], P1["attm"][par], P1["Psb"][par]
        kv, kg, ke, kqa, kqb, kat, kp = (f"vtok{slot}", f"gsil{slot}", f"Epl{par}", f"QA{par}", f"QB{par}", f"attm{par}",
                                         f"Psb{par}")
        if part == 1:
            return gla_front2(c0, mode, par, slot)
        mm(psum[5][:, 0:256], P1["zT"][:, c0:c0 + 128], C["wgate"], ("zT", "wgate"), ("ps5a",), start=True, stop=False)
        mm(psum[5][:, 0:256], C["ones_row"], C["bgate"], ("ones_row", "bgate"), ("ps5a",), start=False, stop=True)
        act(P1["lnv"], psum[5][:, 0:256], AF.Exp, ("ps5a",), ("lnv",), scale=-1.0)
        act(P1["lnv"], P1["lnv"], AF.Ln, ("lnv",), ("lnv",), bias=1.0)
        if mode == "s":
            ts("dve", P1["lnv"], P1["lnv"], C["padmask"], ALU.mult, ("lnv", "padmask"), ("lnv",))
        for hp in range(2):
            mm(psum[5][:, 256 + hp * 128:256 + (hp + 1) * 128], P1["lnv"][:, hp * 128:(hp + 1) * 128], C["tri_f"],
               ("lnv", "tri_f"), ("ps5b",))
        cT = psum[5][:, 256:512].rearrange("p (h t) -> p h t", h=2)
        act(Epl, cT, AF.Exp, ("ps5b",), (ke,), scale=-1.0 / 16)
        act(P1["Emi"], cT, AF.Exp, ("ps5b",), ("Emi",), scale=1.0 / 16)

    def gla_front2(c0, mode, par, slot):
        vtok, gsil, Epl = P1["vtok"][slot], P1["gsil"][slot], P1["Epl"][par]
        QA, QB, attm, Psb = P1["QA"][par], P1["QB"][par], P1["attm"][par], P1["Psb"][par]
        kv, kg, ke, kqa, kqb, kat, kp = (f"vtok{slot}", f"gsil{slot}", f"Epl{par}", f"QA{par}", f"QB{par}", f"attm{par}",
                                         f"Psb{par}")
        stt(P1["qeT"], P1["qf"][:, :, c0:c0 + 128], 0.125, Epl, ALU.mult, ALU.mult, ("qf", ke), ("qeT",))
        tt("dve", P1["keT"], P1["kf"][:, :, c0:c0 + 128], P1["Emi"], ALU.mult, ("kf", "Emi"), ("keT",))
        cp("act", QA[:, :, 0:64], P1["qeT"][:, :, 0:64], ("qeT",), (kqa,))
        cp("act", QB[:, :, 64:128], P1["qeT"][:, :, 64:128], ("qeT",), (kqb,))
        pb = psb(0)
        for hp in range(2):
            tr(pb[:, hp * 128:(hp + 1) * 128], P1["keT"][:, hp, :], C["ident_b"], ("keT", "ident_b"), ("ps0a",))
        cp("act", P1["ketok"], pb[:, 0:256], ("ps0a",), ("ketok",))
        for h in range(4):
            rows = slice(64 * (h % 2), 64 * (h % 2) + 64)
            bk = 6 if h % 2 == 0 else 3
            mm(psum[bk][:, (h // 2) * 128:(h // 2 + 1) * 128], P1["keT"][rows, h // 2, :], P1["qeT"][rows, h // 2, :],
               ("keT", "qeT"), (f"ps{bk}",))
        for h2 in range(2):
            bk = 6 if h2 == 0 else 3
            tt("dve", attm[:, h2:4:2, :], psum[bk][:, 0:256].rearrange("p (h i) -> p h i", h=2),
               bc(C["tri_f"], 1, 2), ALU.mult, (f"ps{bk}", "tri_f"), (kat,))
        for X in range(2):
            trow = slice(64 * X, 64 * X + 64)
            for hp in range(2):
                mm(psum[1 + X][:, hp * 256:(hp + 1) * 256], P1["ketok"][trow, hp * 128:(hp + 1) * 128],
                   vtok[trow, hp * 256:(hp + 1) * 256], ("ketok", kv), (f"ps{1 + X}",))
            for h2 in range(2):
                rows = slice(64 * h2, 64 * h2 + 64)
                pv = psum[1 + X][rows, 0:512].rearrange("p (hp b v) -> p hp b v", hp=2, b=2)[:, :, h2, :]
                cp("act", Psb[rows, X], pv, (f"ps{1 + X}",), (kp,))

    def gla_back(mode, par, slot, tok0=None, seqA=None):
        vtok, gsil, Epl = P1["vtok"][slot], P1["gsil"][slot], P1["Epl"][par]
        QA, QB, attm, Psb = P1["QA"][par], P1["QB"][par], P1["attm"][par], P1["Psb"][par]
        kv, kg, ke, kqa, kqb, kat, kp = (f"vtok{slot}", f"gsil{slot}", f"Epl{par}", f"QA{par}", f"QB{par}", f"attm{par}",
                                         f"Psb{par}")
        on, kon = P1["on"][par], f"on{par}"
        Sf, Sblk = P1["Sf"], P1["Sblk"]
        if mode == "s":
            for X in range(2):
                dma(Sf[X], dap(I["sgla"], (seqA + X) * 32768, [[128, 128], [16384, 2], [1, 128]]), (), (f"Sf{X}",),
                    f"sf{X}")
                for h2 in range(2):
                    rows = slice(64 * h2, 64 * h2 + 64)
                    cp("act", Sblk[X][rows, :, h2, :], Sf[X][rows], (f"Sf{X}",), (f"Sblk{X}",))

        def upd(src_f, X, eL_col, dst_f, dst_blk, dkey_f, dkey_b, skey):
            tt("dve", P1["stmp"], Psb[:, X], src_f, ALU.add, (kp, skey), ("stmp",))
            tt("dve", dst_f, P1["stmp"], bc(Epl[:, :, eL_col], 2, 128), ALU.mult, ("stmp", ke), (dkey_f,))
            if dst_blk is not None:
                for h2 in range(2):
                    rows = slice(64 * h2, 64 * h2 + 64)
                    cp("act", dst_blk[rows, :, h2, :], dst_f[rows], (dkey_f,), (dkey_b,))
        if mode == "p":
            upd(Sf[0], 0, 63, Sf[1], Sblk[1], "Sf1", "Sblk1", "Sf0")
        else:
            upd(Sf[0], 0, 63, P1["Fst"][0], None, "Fst0", None, "Sf0")
            upd(Sf[1], 1, 127, P1["Fst"][1], None, "Fst1", None, "Sf1")
        for hp in range(2):
            o_pair = psum[7][:, hp * 256:(hp + 1) * 256]
            mm(o_pair, QA[:, hp, :], Sblk[0][:, hp].rearrange("p a v -> p (a v)"), (kqa, "Sblk0"), ("ps7",),
               start=True, stop=False)
            mm(o_pair, QB[:, hp, :], Sblk[1][:, hp].rearrange("p a v -> p (a v)"), (kqb, "Sblk1"), ("ps7",),
               start=False, stop=False)
            for h in (2 * hp, 2 * hp + 1):
                mm(psum[7][:, h * 128:(h + 1) * 128], attm[:, h, :], vtok[:, h * 128:(h + 1) * 128],
                   (kat, kv), ("ps7",), start=False, stop=(h == 2 * hp + 1))
        if mode == "p":
            upd(Sf[1], 1, 127, Sf[0], Sblk[0], "Sf0", "Sblk0", "Sf1")
        o4 = psum[7][:, 0:512].rearrange("p (h v) -> p h v", h=4)
        act(P1["osq"], o4, AF.Square, ("ps7",), ("osq",))
        ost = P1["st"][:, 8:12]
        fw.add("dve", lambda e: e.tensor_reduce(out=ost, in_=P1["osq"], axis=mybir.AxisListType.X, op=ALU.add),
               ("osq",), ("ost",))
        act(ost, ost, AF.Ln, ("ost", "eps_col"), ("ost",), scale=1.0 / 128, bias=C["eps_col"])
        act(ost, ost, AF.Exp, ("ost",), ("ost",), scale=-0.5)
        tt("dve", P1["osq"], o4, bc(ost, 2, 128), ALU.mult, ("ps7", "ost"), ("osq",))
        tt("dve", on, P1["osq"].rearrange("p h v -> p (h v)"), gsil, ALU.mult, ("osq", kg), (kon,))
        if mode == "s":
            for X in range(2):
                store(dap(O["glao"], (1 + seqA + X) * 32768, [[128, 128], [16384, 2], [1, 128]]), P1["Fst"][X],
                      (f"Fst{X}",), f"glao{X}")

    def gla_tail(mode, par, tok0=None, seqA=None):
        on, kon = P1["on"][par], f"on{par}"
        pb = psb(0)
        for h in range(4):
            tr(pb[:, 512 + h * 128:512 + (h + 1) * 128], on[:, h * 128:(h + 1) * 128], C["ident_b"],
               (kon, "ident_b"), ("ps0b",))
        cp("act", P1["mst"].rearrange("p h t -> p (h t)"), pb[:, 512:1024], ("ps0b",), ("mst",))
        if mode == "p":
            dma(mix_d[:, 4:8, tok0:tok0 + 128], P1["mst"], ("mst",), ("mix_d",), "mixw")
        else:
            for X in range(2):
                t0 = TP_ + (seqA + X) * 4
                dma(mix_d[:, 4:8, t0:t0 + 4], P1["mst"][:, :, 64 * X + 60:64 * X + 64], ("mst",), ("mix_d",), "mixw")

    def phase1a():
        fw.barrier()
        alloc_p1()
        for par in range(2):
            mset("pool", P1["QA"][par], 0.0, (f"QA{par}",))
            mset("pool", P1["QB"][par], 0.0, (f"QB{par}",))
        for X in range(2):
            mset("pool", P1["Sblk"][X], 0.0, (f"Sblk{X}",))
        mset("pool", P1["Sf"][0], 0.0, ("Sf0",))
        mset("pool", P1["Ys"][:, :, 0:4, :], 0.0, ("Ys",))
        mset("pool", P1["hTpad"], 0.0, ("hTpad",))
        Y = P1["Y"]
        pendB = []
        pendC = []
        tcount = [0]

        def capture(fn):
            n0 = len(fw.ops)
            fn()
            lst = fw.ops[n0:]
            del fw.ops[n0:]
            return lst

        def step(front=None, back=None, tail=None):
            lists = []
            if pendC:
                c = pendC.pop(0)
                lists.append(capture(c))
            if pendB:
                b, c = pendB.pop(0)
                lists.append(capture(b))
                pendC.append(c)
            if front is not None:
                lists.append(capture(lambda: (front(0), front(1))))
                pendB.append((back, tail))
            keyed = []
            for li, lst in enumerate(lists):
                n = len(lst)
                for i, op in enumerate(lst):
                    keyed.append(((i + 0.5) / n, li, i, op))
            keyed.sort(key=lambda t: (t[0], t[1], t[2]))
            fw.ops.extend(op for _, _, _, op in keyed)

        def drain():
            while pendB or pendC:
                step()

        def do_tile(hT_ap, c0, mode, slot, tok0=None, seqA=None):
            par = tcount[0] % 2
            tcount[0] += 1
            step(lambda part: gla_front(hT_ap, c0, mode, par, slot, part),
                 lambda: gla_back(mode, par, slot, tok0=tok0, seqA=seqA),
                 lambda: gla_tail(mode, par, tok0=tok0, seqA=seqA))
        vslot = [0]
        for a in range(2):
            for sti in range(2):
                for i in range(4):
                    lt = sti * 4 + i
                    pt = a * 8 + lt
                    norm_tile(I["xp"][pt * 128:(pt + 1) * 128, :], 128, P1["hT"][:, :, lt * 128:(lt + 1) * 128],
                              C["attn_col"])
                hT512 = P1["hT"][:, :, sti * 512:(sti + 1) * 512]
                proj_qkz(hT512, 512)
                slots = []
                for i in range(4):
                    lt = sti * 4 + i
                    sl_ = vslot[0] % 6
                    vslot[0] += 1
                    slots.append(sl_)
                    proj_vg(P1["hT"][:, :, lt * 128:(lt + 1) * 128], sl_)
                for i in range(4):
                    lt = sti * 4 + i
                    pt = a * 8 + lt
                    do_tile(P1["hT"][:, :, lt * 128:(lt + 1) * 128], i * 128, "p", slots[i], tok0=pt * 128)
            for jl in range(8):
                b = 1 + jl % 2
                for kc in range(8):
                    mm(psum[b][:, 0:512], P1["hT"][:, kc, jl:1024:8], w_in_sb[:, kc, 0:512], ("hT", "w_in"), (f"ps{b}",),
                       start=(kc == 0), stop=(kc == 7))
                cp("act", Y[:, a, :, jl, :], psum[b][:, 0:512].rearrange("p (g c) -> p g c", g=32), (f"ps{b}",), ("Y",))
        drain()
        store(dap(O["glao"], 0, [[128, 128], [16384, 2], [1, 128]]), P1["Sf"][0], ("Sf0",), "glaoP")
        norm_tile(I["xs"], 64, P1["hTs"], C["attn_col"])
        for tq in range(4):
            b = 1 + tq % 2
            for kc in range(8):
                mm(psum[b][0:16, 0:512], P1["hTs"][:, kc, tq:64:4], w_in_sb[:, kc, 0:512], ("hT", "w_in"), (f"ps{b}",),
                   start=(kc == 0), stop=(kc == 7))
            cp("act", P1["Ys"][:, :, 4 + tq, :], psum[b][0:16, 0:512].rearrange("p (g c) -> p g c", g=32), (f"ps{b}",),
               ("Ys",))
        hTp = P1["hT"]
        mset("dve", hTp, 0.0, ("hT",))
        cp("dve", hTp.rearrange("p k (s t) -> p k s t", s=16)[:, :, :, 60:64],
           P1["hTs"].rearrange("p k (s t) -> p k s t", s=16), ("hT",), ("hT",))
        for st_ in range(2):
            proj_qkz(hTp[:, :, st_ * 512:(st_ + 1) * 512], 512)
            slots = []
            for i in range(4):
                sl_ = vslot[0] % 6
                vslot[0] += 1
                slots.append(sl_)
                proj_vg(hTp[:, :, (st_ * 4 + i) * 128:(st_ * 4 + i + 1) * 128], sl_)
            for i in range(4):
                pst = st_ * 4 + i
                do_tile(hTp[:, :, pst * 128:(pst + 1) * 128], i * 128, "s", slots[i], seqA=2 * pst)
        drain()

    P2 = {}

    def phase1b():
        fw.barrier()
        REST.hi = ARENA_BYTES
        REST.reset(P1["P1_KEEP"])
        A = REST.alloc
        Y, Ys = P1["Y"], P1["Ys"]
        Ublk = A([128, 32, MCOL], BF16)
        Wreg_off = REST.top
        W = [A([128, 16, MCOL], F32) for _ in range(2)]
        Sprev = [A([128, 16, MCOL], BF16) for _ in range(2)]
        Sa = A([128, 2, 16, 16], F32); Sb = A([128, 2, 16, 16], F32); Sst = A([128, 2, 16, 16], F32)
        s0 = A([128, 2, 16, 16], F32); s0p = A([128, 2, 16, 16], F32); s0full = A([128, 2, 2048], F32)
        s0tok = s0full[0:16]
        tA = [A([128, 4, 16, 16], F32) for _ in range(4)]
        hA = [A([128, 16, 16], F32) for _ in range(4)]
        wglu = A([128, 4, 512], BF16)
        for kc in range(4):
            dma(wglu[:, kc, :], I["w_glu"][kc * 128:(kc + 1) * 128, :], (), ("wglu",), "G:wglu", eng="pool")
        dma(s0tok[:, 0, :], I["s5r"], (), ("s0tok",), "G:s0")
        dma(s0tok[:, 1, :], I["s5i"], (), ("s0tok",), "G:s0")
        cnt = 0
        for a in range(2):
            for gb in range(4):
                bk = cnt % 2
                pb = psb(bk)
                for gi in range(8):
                    g = gb * 8 + gi
                    tr(pb[:, gi * 128:(gi + 1) * 128], Y[:, a, g].rearrange("p j c -> p (j c)"), C["ident_b"],
                       ("Y", "ident_b"), (f"ps{bk}",))
                cp("act" if cnt % 2 == 0 else "dve", Ublk[:, gb * 8:(gb + 1) * 8, a * 128:(a + 1) * 128],
                   pb.rearrange("p (g m) -> p g m", g=8), (f"ps{bk}",), (f"Ub_{a}_{gb}",))
                cnt += 1
        pb = psb(0)
        for g in range(32):
            tr(pb[:, g * 16:(g + 1) * 16], Ys[:, g].rearrange("p j c -> p (j c)"), C["ident_b"][0:16, 0:16],
               ("Ys", "ident_b"), ("ps0",))
        cp("act", Ublk[:, :, 256:MCOL], pb[:, 0:512].rearrange("p (g m) -> p g m", g=32), ("ps0",), ("Ub_s",))
        fw.add("dve", None, tuple(f"Ub_{a}_{gb}" for a in range(2) for gb in range(4)) + ("Ub_s",), ("Ublk",))
        for ri in range(2):
            for gp in range(16):
                tr(psum[1][:, ri * 256 + gp * 16:ri * 256 + (gp + 1) * 16], s0tok[:, ri, gp * 128:(gp + 1) * 128],
                   C["ident_f"][0:16, 0:16], ("s0tok", "ident_f"), ("ps1",))
        cp("dve", s0.rearrange("p r g s -> p (r g s)"), psum[1][:, 0:512], ("ps1",), ("s0",))
        l4r = bc(C["Lm4"][:, 0, :], 2, 16); l4i = bc(C["Lm4"][:, 1, :], 2, 16)
        tt("dve", hA[0], s0[:, 0], l4r, ALU.mult, ("s0", "Lm4"), ("hA0",))
        tt("dve", hA[1], s0[:, 1], l4i, ALU.mult, ("s0", "Lm4"), ("hA1",))
        tt("dve", s0p[:, 0], hA[0], hA[1], ALU.subtract, ("hA0", "hA1"), ("s0p",))
        tt("dve", hA[0], s0[:, 0], l4i, ALU.mult, ("s0", "Lm4"), ("hA0",))
        tt("dve", hA[1], s0[:, 1], l4r, ALU.mult, ("s0", "Lm4"), ("hA1",))
        tt("dve", s0p[:, 1], hA[0], hA[1], ALU.add, ("hA0", "hA1"), ("s0p",))
        cnt = 0
        for gp in range(16):
            for ri in range(2):
                bk = 2 + cnt % 6
                cnt += 1
                for g2 in range(2):
                    g = 2 * gp + g2
                    mm(psum[bk][64 * g2:64 * g2 + 64, 0:MCOL], T["BL"][:, g, ri, :], Ublk[:, g, :], ("BL", "Ublk"),
                       (f"ps{bk}",))
                cp("act" if cnt % 2 == 0 else "dve", W[ri][:, gp, :], psum[bk][:, 0:MCOL], (f"ps{bk}",),
                   (f"Wc{ri}_{gp}",))
        for ri in range(2):
            fw.add("dve", None, tuple(f"Wc{ri}_{gp}" for gp in range(16)), (f"W{ri}_0", f"W{ri}s"))
        if stop_after == "p1b_W":
            dbg("d_W", W[0].rearrange("p g m -> p (g m)"), ("W0_0", "W0s"))
            dbg("d_Ublk", Ublk.rearrange("p g m -> p (g m)"), ("Ublk",))
            return
        Wv = [W[ri][:, :, 0:256].rearrange("p g (c k) -> p g c k", c=16) for ri in range(2)]
        l8r = bc(C["L8"][:, 0, :], 2, 16); l8i = bc(C["L8"][:, 1, :], 2, 16)

        def wk(ri, k):
            return f"W{ri}_{k}" if k > 0 else f"W{ri}_0"
        for ri in range(2):
            fw.add("dve", None, (f"W{ri}_0",), tuple(f"W{ri}_{k}" for k in range(1, 16)))
        for k in range(1, 16):
            pr, pi_ = Wv[0][:, :, :, k - 1], Wv[1][:, :, :, k - 1]
            kr = (wk(0, k - 1), wk(1, k - 1), "L8")
            tt("dve", hA[0], pr, l8r, ALU.mult, kr, ("hA0",))
            tt("dve", hA[1], pi_, l8i, ALU.mult, kr, ("hA1",))
            tt("dve", hA[2], pi_, l8r, ALU.mult, kr, ("hA2",))
            tt("dve", hA[3], pr, l8i, ALU.mult, kr, ("hA3",))
            tt("dve", Wv[0][:, :, :, k], Wv[0][:, :, :, k], hA[0], ALU.add, ("hA0", wk(0, k)), (wk(0, k),))
            tt("dve", Wv[1][:, :, :, k], Wv[1][:, :, :, k], hA[2], ALU.add, ("hA2", wk(1, k)), (wk(1, k),))
            tt("dve", Wv[0][:, :, :, k], Wv[0][:, :, :, k], hA[1], ALU.subtract, ("hA1", wk(0, k)), (wk(0, k),))
            tt("dve", Wv[1][:, :, :, k], Wv[1][:, :, :, k], hA[3], ALU.add, ("hA3", wk(1, k)), (wk(1, k),))
        allW = tuple(f"W{ri}_{k}" for ri in range(2) for k in range(16))
        cp("dve", Sa[:, 0], Wv[0][:, :, :, 15], (wk(0, 15),), ("Sa",))
        cp("dve", Sa[:, 1], Wv[1][:, :, :, 15], (wk(1, 15),), ("Sa",))
        cur, nxt, kc_, kn_ = Sa, Sb, "Sa", "Sb"
        for lev, d in enumerate((1, 2, 4, 8)):
            lr = bc(C["LK"][:, 0, lev, :], 2, 16 - d); li = bc(C["LK"][:, 1, lev, :], 2, 16 - d)
            cp("dve", nxt[:, 0], cur[:, 0], (kc_,), (kn_,))
            cp("dve", nxt[:, 1], cur[:, 1], (kc_,), (kn_,))
            sr, si = cur[:, 0, :, 0:16 - d], cur[:, 1, :, 0:16 - d]
            h0, h1, h2_, h3 = (hA[i][:, :, 0:16 - d] for i in range(4))
            tt("dve", h0, sr, lr, ALU.mult, (kc_, "LK"), ("hA0",))
            tt("dve", nxt[:, 0, :, d:16], nxt[:, 0, :, d:16], h0, ALU.add, ("hA0", kn_), (kn_,))
            tt("dve", h1, si, li, ALU.mult, (kc_, "LK"), ("hA1",))
            tt("dve", nxt[:, 0, :, d:16], nxt[:, 0, :, d:16], h1, ALU.subtract, ("hA1", kn_), (kn_,))
            tt("dve", h2_, si, lr, ALU.mult, (kc_, "LK"), ("hA2",))
            tt("dve", nxt[:, 1, :, d:16], nxt[:, 1, :, d:16], h2_, ALU.add, ("hA2", kn_), (kn_,))
            tt("dve", h3, sr, li, ALU.mult, (kc_, "LK"), ("hA3",))
            tt("dve", nxt[:, 1, :, d:16], nxt[:, 1, :, d:16], h3, ALU.add, ("hA3", kn_), (kn_,))
            cur, nxt, kc_, kn_ = nxt, cur, kn_, kc_
        Send, ke_ = cur, kc_
        mset("dve", Sst, 0.0, ("Sst",))
        cp("dve", Sst[:, :, :, 1:16], Send[:, :, :, 0:15], (ke_,), ("Sst",))
        cp("dve", C["fin"][:, :, :, 0], Send[:, :, :, 15], (ke_,), ("fin",))
        l8rs = bc(C["L8"][:, 0, :], 2, 16); l8is = bc(C["L8"][:, 1, :], 2, 16)
        tt("dve", hA[0], s0p[:, 0], l8rs, ALU.mult, ("s0p", "L8"), ("hA0",))
        tt("dve", hA[1], s0p[:, 1], l8is, ALU.mult, ("s0p", "L8"), ("hA1",))
        tt("dve", hA[0], hA[0], hA[1], ALU.subtract, ("hA0", "hA1"), ("hA0",))
        tt("dve", C["fin"][:, 0, :, 1:17], hA[0], W[0][:, :, 256:MCOL], ALU.add, ("hA0", "W0s"), ("fin",))
        tt("dve", hA[0], s0p[:, 0], l8is, ALU.mult, ("s0p", "L8"), ("hA0",))
        tt("dve", hA[1], s0p[:, 1], l8rs, ALU.mult, ("s0p", "L8"), ("hA1",))
        tt("dve", hA[0], hA[0], hA[1], ALU.add, ("hA0", "hA1"), ("hA0",))
        tt("dve", C["fin"][:, 1, :, 1:17], hA[0], W[1][:, :, 256:MCOL], ALU.add, ("hA0", "W1s"), ("fin",))
        for q in range(4):
            gs = slice(4 * q, 4 * q + 4)
            TPr = bc(C["TPW"][:, 0, gs, :], 2, 16); TPi = bc(C["TPW"][:, 1, gs, :], 2, 16)
            SR = bc(Sst[:, 0, gs, :], 3, 16); SI = bc(Sst[:, 1, gs, :], 3, 16)
            spr = Sprev[0][:, gs, 0:256].rearrange("p g (c k) -> p g c k", c=16)
            spi = Sprev[1][:, gs, 0:256].rearrange("p g (c k) -> p g c k", c=16)
            tt("dve", tA[0], TPr, SR, ALU.mult, ("TPW", "Sst"), ("tA0",))
            tt("dve", tA[1], TPi, SI, ALU.mult, ("TPW", "Sst"), ("tA1",))
            tt("dve", tA[2], TPr, SI, ALU.mult, ("TPW", "Sst"), ("tA2",))
            tt("dve", tA[3], TPi, SR, ALU.mult, ("TPW", "Sst"), ("tA3",))
            tt("dve", tA[0], tA[0], tA[1], ALU.subtract, ("tA0", "tA1"), ("tA0",))
            tt("dve", tA[2], tA[2], tA[3], ALU.add, ("tA2", "tA3"), ("tA2",))
            tt("dve", spr[:, :, :, 1:16], tA[0][:, :, :, 1:16], Wv[0][:, gs, :, 0:15], ALU.add, ("tA0",) + allW, ("Sp0",))
            tt("dve", spi[:, :, :, 1:16], tA[2][:, :, :, 1:16], Wv[1][:, gs, :, 0:15], ALU.add, ("tA2",) + allW, ("Sp1",))
            cp("dve", spr[:, :, :, 0], tA[0][:, :, :, 0], ("tA0",), ("Sp0",))
            cp("dve", spi[:, :, :, 0], tA[2][:, :, :, 0], ("tA2",), ("Sp1",))
        cp("dve", Sprev[0][:, :, 256:MCOL], s0p[:, 0], ("s0p",), ("Sp0",))
        cp("dve", Sprev[1][:, :, 256:MCOL], s0p[:, 1], ("s0p",), ("Sp1",))
        finT = s0full[0:17]
        for ri in range(2):
            for q4 in range(4):
                bk = 2 + q4
                for gq in range(4):
                    gp = q4 * 4 + gq
                    tr(psum[bk][0:17, gq * 128:(gq + 1) * 128], C["fin"][:, ri, gp, :], C["ident_f"], ("fin", "ident_f"),
                       (f"ps{bk}",))
                cp("act", finT[:, ri, q4 * 512:(q4 + 1) * 512], psum[bk][0:17, 0:512], (f"ps{bk}",), ("s0tok",))
            store(O["s5o_r" if ri == 0 else "s5o_i"], finT[:, ri, :], ("s0tok",), "s5o")
        if stop_after == "p1b_S":
            dbg("d_Sp", Sprev[0].rearrange("p g m -> p (g m)"), ("Sp0", "Sp1"))
            return
        Yg = arena_t[:, REST.lo:REST.lo + 32 * MCOL * 2].bitcast(BF16).rearrange("p (g m) -> p g m", g=32)
        fw.add("dve", None, (), ("Y", "Ys") + tuple(f"Yg{g}" for g in range(32)))
        for g in range(32):
            gp = g // 2
            bk = 2 + g % 6
            o = psum[bk][:, 0:MCOL]
            mm(o, T["T0"][:, g, :], Ublk[:, g, :], ("T0", "Ublk"), (f"ps{bk}",), start=True, stop=False)
            mm(o, T["CL"][:, g, 0, :], Sprev[0][:, gp, :], ("CL", "Sp0"), (f"ps{bk}",), start=False, stop=False)
            mm(o, T["CL"][:, g, 1, :], Sprev[1][:, gp, :], ("CL", "Sp1"), (f"ps{bk}",), start=False, stop=True)
            cp("act" if g % 2 == 0 else "dve", Yg[:, g, :], o, (f"ps{bk}",), (f"Yg{g}",))
        fw.add("dve", None, tuple(f"Yg{g}" for g in range(32)), ("Y", "Ys"))
        if stop_after == "p1b_Y":
            dbg("d_Yg", Yg.rearrange("p g m -> p (g m)"), ("Y",))
            return
        Z = arena_t[:, Wreg_off:Wreg_off + 2 * 8 * 512 * 2].bitcast(BF16).rearrange("p (a i n) -> p a i n", a=2, i=8)
        Zs = arena_t[:, Wreg_off + 16384:Wreg_off + 16384 + 8 * 512 * 2].bitcast(BF16).rearrange(
            "p (i n) -> p i n", i=8)[0:16]
        ZK = allW + ("W0s", "W1s")
        ZF = tuple(f"Zc_{a}_{gb}" for a in range(2) for gb in range(4)) + tuple(f"Zs_{gb}" for gb in range(4))
        fw.add("dve", None, (), ZK + ZF)
        cnt = 0
        for a in range(2):
            for gb in range(4):
                bk = cnt % 2
                pb = psb(bk)
                for gi in range(8):
                    g = gb * 8 + gi
                    tr(pb[:, gi * 128:(gi + 1) * 128], Yg[:, g, a * 128:(a + 1) * 128], C["ident_b"], ("Y", "ident_b"),
                       (f"ps{bk}",))
                cp("act" if cnt % 2 == 0 else "dve",
                   Z[:, a, :, gb * 128:(gb + 1) * 128].rearrange("p i (g c) -> p i g c", g=8),
                   pb.rearrange("p (g i c) -> p i g c", g=8, i=8), (f"ps{bk}",), (f"Zc_{a}_{gb}",))
                cnt += 1
        for gb in range(4):
            bk = cnt % 2
            pb = psb(bk)
            for gi in range(8):
                g = gb * 8 + gi
                tr(pb[0:16, gi * 128:(gi + 1) * 128], Yg[:, g, 256:MCOL], C["ident_b"], ("Y", "ident_b"), (f"ps{bk}",))
            cp("act" if cnt % 2 == 0 else "dve", Zs[:, :, gb * 128:(gb + 1) * 128].rearrange("p i (g c) -> p i g c", g=8),
               pb[0:16].rearrange("p (g i c) -> p i g c", g=8, i=8), (f"ps{bk}",), (f"Zs_{gb}",))
            cnt += 1
        fw.add("dve", None, ZF, ZK)
        y5T = Ublk.rearrange("p g m -> p (g m)")[:, 0:4 * (TP_ + TS_)].rearrange("p (q t) -> p q t", q=4)
        fw.add("dve", None, (), ("Ublk",) + tuple(f"y5c_{a}_{q}" for a in range(2) for q in range(4)) + ("y5c_s",))
        for a in range(2):
            for q in range(4):
                bk = cnt % 2
                pb = psb(bk)
                for il in range(8):
                    tr(pb[:, il * 128:(il + 1) * 128], Z[:, a, il, q * 128:(q + 1) * 128], C["ident_b"], ZK + ("ident_b",),
                       (f"ps{bk}",))
                cp("act" if cnt % 2 == 0 else "dve", y5T[:, q, a * 1024:(a + 1) * 1024].rearrange("p (m i) -> p i m", i=8),
                   pb.rearrange("p (i m) -> p i m", i=8), (f"ps{bk}",), (f"y5c_{a}_{q}",))
                cnt += 1
        bk = cnt % 2
        pb = psb(bk)
        for q in range(4):
            for tq in range(4):
                tr(pb[:, (q * 4 + tq) * 16:(q * 4 + tq + 1) * 16], Zs[:, 4 + tq, q * 128:(q + 1) * 128],
                   C["ident_b"][0:16, 0:16], ZK + ("ident_b",), (f"ps{bk}",))
        cp("act", y5T[:, :, TP_:TP_ + TS_].rearrange("p q (s t) -> p q t s", t=4),
           pb[:, 0:256].rearrange("p (q t s) -> p q t s", q=4, t=4), (f"ps{bk}",), ("y5c_s",))
        fw.add("dve", None, tuple(f"y5c_{a}_{q}" for a in range(2) for q in range(4)) + ("y5c_s",), ("Ublk",))
        if stop_after == "p1b_T":
            dbg("d_y5T", y5T.rearrange("p q t -> p (q t)"), ("Ublk",))
            return
        prefetch_p2_weights()
        G_ = {}
        GB = 256
        G_["y5"] = A([128, 4, GB], BF16); G_["sg"] = A([128, GB], F32); G_["y5g"] = A([128, 4, GB], F32)
        G_["sq"] = A([128, 4, GB], BF16); G_["rstd"] = A([128, GB], F32); G_["mixS"] = A([128, 4, GB], BF16)
        blocks = [(c0, GB) for c0 in range(0, TP_, GB)] + [(TP_, TS_)]
        for (c0, n) in blocks:
            act(G_["y5"][:, :, 0:n], y5T[:, :, c0:c0 + n], AF.Gelu_apprx_tanh, ("Ublk",), ("y5",))
            for qo in range(4):
                bk = 2 + qo % 2
                for qi in range(4):
                    mm(psum[bk][:, 0:n], wglu[:, qi, qo * 128:(qo + 1) * 128], G_["y5"][:, qi, 0:n], ("wglu", "y5"),
                       (f"ps{bk}",), start=(qi == 0), stop=(qi == 3))
                act(G_["sg"][:, 0:n], psum[bk][:, 0:n], AF.Sigmoid, (f"ps{bk}", "bglu_col"), ("sg",),
                    bias=C["bglu_col"][:, qo:qo + 1])
                tt("dve", G_["y5g"][:, qo, 0:n], G_["y5"][:, qo, 0:n], G_["sg"][:, 0:n], ALU.mult, ("y5", "sg"),
                   (f"y5g{qo}",))
                act(G_["sq"][:, qo, 0:n], G_["y5g"][:, qo, 0:n], AF.Square, (f"y5g{qo}",), (f"sq{qo}",))
            for qo in range(4):
                mm(psum[4][:, 0:n], C["ones_b"], G_["sq"][:, qo, 0:n], ("ones_b", f"sq{qo}"), ("ps4",), start=(qo == 0),
                   stop=(qo == 3))
            ts("dve", G_["rstd"][:, 0:n], psum[4][:, 0:n], 1.0 / 512, ALU.mult, ("ps4",), ("rstd",), s2=EPS, op1=ALU.add)
            act(G_["rstd"][:, 0:n], G_["rstd"][:, 0:n], AF.Sqrt, ("rstd",), ("rstd",))
            recip(G_["rstd"][:, 0:n], G_["rstd"][:, 0:n], ("rstd",), ("rstd",))
            for qo in range(4):
                stt(G_["mixS"][:, qo, 0:n], G_["y5g"][:, qo, 0:n], C["s5n_col"][:, qo:qo + 1], G_["rstd"][:, 0:n], ALU.mult,
                    ALU.mult, (f"y5g{qo}", "rstd", "s5n_col"), ("mixS",))
            dma(mix_d[:, 0:4, c0:c0 + n], G_["mixS"][:, :, 0:n], ("mixS",), ("mix_d",), "mixw")

    PH = Arena(arena_t, 14 * 1024, ARENA_BYTES)
    w_o = PH.alloc([128, 8, D], BF16)
    w_up = PH.alloc([128, 8, 2 * DFF], BF16)
    w_dn = PH.alloc([128, NF, D], BF16)
    LATE_KC = (3, 4, 5)

    def p2_weight_dmas(late):
        if not late:
            for kc in range(8):
                dma(w_o[:, kc, :], I["w_o"][kc * 128:(kc + 1) * 128, :], (), ("w_o",), "G:w_o", eng="pool",
                    max_dma_last_dim=4096)
        for kc in range(8):
            if (kc in LATE_KC) != late:
                continue
            dma(w_up[:, kc, :], I["w_up"][kc * 128:(kc + 1) * 128, :], (), (f"wuk{kc}",), f"G:wuk{kc}", eng="pool",
                max_dma_last_dim=4096)
        if not late:
            for f in range(NF):
                dma(w_dn[:, f, :], I["w_down"][f * 128:(f + 1) * 128, :], (), ("w_dn",), "G:w_dn", eng="pool",
                    max_dma_last_dim=4096)

    def prefetch_p2_weights():
        fw.barrier(engs=("pool",))
        p2_weight_dmas(late=False)

    def phase2():
        fw.barrier()
        A = PH.alloc
        NTT = 256
        xbs = [A([128, D], F32) for _ in range(3)]
        mt = A([128, 8, NTT], BF16)
        hb = A([128, D], BF16)
        h2T = A([128, 8, NTT], BF16)
        aext = [A([128, NTT + 8], F32) for _ in range(2)]
        ACC_OFF = (PH.top + 63) // 64 * 64
        acc = [A([128, NTT], F32) for _ in range(2)]
        junk2 = arena_t[:, ACC_OFF:ACC_OFF + 2 * NTT * 4].bitcast(BF16)
        sil = [A([128, NTT], BF16) for _ in range(2)]
        gT = A([128, NF, NTT], BF16)
        halo2 = [A([128, NF, 2], F32) for _ in range(2)]
        atail = arena_t[:, TPW_OFF:TPW_OFF + NF * 34 * 4].bitcast(F32).rearrange("p (f n) -> p f n", f=NF)
        fragA = arena_t[:, TPW_OFF + 2992:TPW_OFF + 2992 + 10 * 128].bitcast(F32).rearrange("p (f n) -> p f n", f=10)
        ctop = (CONST.top + 63) // 64 * 64
        fragB = arena_t[:, ctop:ctop + 8 * 128].bitcast(F32).rearrange("p (f n) -> p f n", f=8)
        assert ctop + 8 * 128 <= 14 * 1024 and 2992 + 10 * 128 <= 4352
        fragC = A([128, 4, 32], F32)
        conv0T = [fragA[:, f, :] for f in range(10)] + [fragB[:, f, :] for f in range(8)] + [fragC[:, f, :] for f in range(4)]
        xbs.append(A([128, D], F32))
        st2 = A([128, 16], F32)
        gT_off_bytes = None
        c0tok = gT.rearrange("p f n -> p (f n)").bitcast(F32)[0:32, 0:DFF]
        tailT = gT.rearrange("p f n -> p (f n)").bitcast(F32)[0:34, 0:DFF]
        p2_weight_dmas(late=True)
        GTK = tuple(f"gT{f}" for f in range(NF))
        dma(c0tok, I["sconv"], (), GTK, "c0")
        for f in range(NF):
            bk = 5 + f // 16
            tr(psum[bk][:, (f % 16) * 32:(f % 16 + 1) * 32], c0tok[:, f * 128:(f + 1) * 128], C["ident_f"][0:32, 0:32],
               GTK + ("ident_f",), (f"ps{bk}",))
        cp("dve", fragA, psum[5][:, 0:320].rearrange("p (f n) -> p f n", f=10), ("ps5",), ("conv0T",))
        cp("dve", fragB[:, 0:6, :], psum[5][:, 320:512].rearrange("p (f n) -> p f n", f=6), ("ps5",), ("conv0T",))
        cp("dve", fragB[:, 6:8, :], psum[6][:, 0:64].rearrange("p (f n) -> p f n", f=2), ("ps6",), ("conv0T",))
        cp("dve", fragC, psum[6][:, 64:192].rearrange("p (f n) -> p f n", f=4), ("ps6",), ("conv0T",))
        for hp_ in range(2):
            mset("dve", halo2[hp_], 0.0, tuple(f"halo{hp_}_{f}" for f in range(NF)))

        def rms_rstd(src, n, slot, xk):
            ss = st2[0:n, slot * 2:slot * 2 + 1]
            rs = st2[0:n, slot * 2 + 1:slot * 2 + 2]
            k = f"st2_{slot}"
            if slot == 0:
                junk, jk = hb[0:n], "hb"
            else:
                junk, jk = junk2[0:n, :], "acc0"
            fw.add("act", lambda e: e.activation(out=junk, in_=src, func=AF.Square, accum_out=ss), (xk,),
                   (jk, k) if slot == 0 else ("acc0", "acc1", k))
            ts("dve", rs, ss, 1.0 / D, ALU.mult, (k,), (k,), s2=EPS, op1=ALU.add)
            tt("pool", rs, rs, C["mhalf"][0:n], ALU.pow, (k, "mhalf"), (k,))
            return rs, k

        tiles = [(t * NTT, NTT, "p") for t in range(TP_ // NTT)] + [(TP_, TS_, "s")]
        xslot = {}
        xctr2 = [0]

        def tinfo(ti):
            tok0, NT, kind = tiles[ti]
            nsub = (NT + 127) // 128
            return tok0, NT, kind, [(sb, min(128, NT - sb * 128)) for sb in range(nsub)]

        def pro(ti, sb):
            tok0, NT, kind, subs = tinfo(ti)
            n = subs[sb][1]
            k_ = xctr2[0] % 4
            xctr2[0] += 1
            xslot[(ti, sb)] = k_
            xb, xk = xbs[k_], f"xb{k_}"
            src = I["xp"][tok0 + sb * 128:tok0 + sb * 128 + n, :] if kind == "p" else I["xs"]
            dma(xb[0:n, :], src, (), (xk,), f"xb{k_}")
            if sb == 0:
                dma(mt[:, :, 0:NT], mix_d[:, :, tok0:tok0 + NT], ("mix_d",), ("mt",), "mt")
            cs = slice(sb * 128, sb * 128 + n)
            for half in range(2):
                for kc in range(8):
                    mm(psum[half][0:n, 0:512], mt[:, kc, cs], w_o[:, kc, half * 512:(half + 1) * 512], ("mt", "w_o"),
                       (f"ps{half}",), start=(kc == 0), stop=(kc == 7))
                tt("dve", xb[0:n, half * 512:(half + 1) * 512], psum[half][0:n, 0:512],
                   xb[0:n, half * 512:(half + 1) * 512], ALU.add, (f"ps{half}", xk), (xk,))
            rs, k = rms_rstd(xb[0:n, :], n, 0, xk)
            xsrc = xb[0:n, :]
            fw.add("act", lambda e, xsrc=xsrc, rs=rs, n=n: e.activation(out=hb[0:n], in_=xsrc, func=AF.Copy, scale=rs),
                   (xk, k), ("hb",))

            def part_b():
                pb = psb(7)
                for kc in range(8):
                    tr(pb[:, kc * 128:kc * 128 + n], hb[0:n, kc * 128:(kc + 1) * 128], C["ident_b"][0:n, 0:n],
                       ("hb", "ident_b"), ("ps7",))
                tt("dve", h2T[:, :, cs], pb.rearrange("p (k n) -> p k n", k=8)[:, :, 0:n], bc(C["ffn_col"], 2, n),
                   ALU.mult, ("ps7", "ffn_col"), ("h2T",))
            return part_b

        def epi(ti, sb):
            tok0, NT, kind, subs = tinfo(ti)
            n = subs[sb][1]
            k_ = xslot[(ti, sb)]
            xb, xk = xbs[k_], f"xb{k_}"
            cs = slice(sb * 128, sb * 128 + n)
            for half in range(2):
                for f in range(NF):
                    mm(psum[half][0:n, 0:512], gT[:, f, cs], w_dn[:, f, half * 512:(half + 1) * 512], (f"gT{f}", "w_dn"),
                       (f"ps{half}",), start=(f == 0), stop=(f == NF - 1))
                tt("dve", xb[0:n, half * 512:(half + 1) * 512], psum[half][0:n, 0:512],
                   xb[0:n, half * 512:(half + 1) * 512], ALU.add, (f"ps{half}", xk), (xk,))
            rs, k = rms_rstd(xb[0:n, :], n, 1, xk)
            xsrc = xb[0:n, :]
            fw.add("act", lambda e, xsrc=xsrc, rs=rs: e.activation(out=xsrc, in_=xsrc, func=AF.Copy, scale=rs),
                   (xk, k), (xk,))
            tt("dve", xsrc, xsrc, C["fin_row"][0:n], ALU.mult, (xk, "fin_row"), (xk,))
            dst = O["yp"][tok0 + sb * 128:tok0 + sb * 128 + n, :] if kind == "p" else O["ys"]
            store(dst, xsrc, (xk,), f"y{k_}")

        def ffn_up(ti):
            tok0, NT, kind, subs = tinfo(ti)
            cw = C["cw_col"]
            last_prompt = (ti == len(tiles) - 2)

            def views(f):
                sl = f % 2
                bank = 2 + f % 5
                ae, ac, si = aext[sl], acc[sl], sil[sl]
                psA = psum[bank][:, 0:NT]
                psB = psum[bank][:, 256:256 + NT]
                if kind == "p":
                    v2, v1, v0 = ae[:, 2:2 + NT], ae[:, 1:1 + NT], ae[:, 0:NT]
                    acv, pav = ac[:, 0:NT], psA
                else:
                    ae3 = ae[:, 0:96].rearrange("p (s w) -> p s w", w=6)
                    v2, v1, v0 = ae3[:, :, 2:6], ae3[:, :, 1:5], ae3[:, :, 0:4]
                    acv = ac[:, 0:NT].rearrange("p (s t) -> p s t", t=4)
                    pav = psA.rearrange("p (s t) -> p s t", t=4)
                return sl, bank, ae, ac, si, psA, psB, v2, v1, v0, acv, pav

            def stA(f):
                sl, bank, ae, ac, si, psA, psB, v2, v1, v0, acv, pav = views(f)
                ak, ck, pk = f"aext{sl}", f"acc{sl}", f"ps{bank}"
                for (c0p, off) in ((0, 0), (256, DFF)):
                    for kc in range(8):
                        mm(psum[bank][:, c0p:c0p + NT], w_up[:, kc, off + f * 128:off + (f + 1) * 128], h2T[:, kc, 0:NT],
                           (f"wuk{kc}", "h2T"), (pk,), start=(kc == 0), stop=(kc == 7))
                if kind == "p":
                    cp("dve", ae[:, 0:2], halo2[ti % 2][:, f, :], (f"halo{ti % 2}_{f}",), (ak + "h",))
                    fw.add("act", lambda e, o_=ae[:, 2:2 + NT], i_=psA: e.copy(out=o_, in_=i_), (pk,), (ak,), weak=(pk,))
                    cp("act", halo2[(ti + 1) % 2][:, f, :], ae[:, NT:NT + 2], (ak,), (f"halo{(ti + 1) % 2}_{f}",))
                    if last_prompt:
                        cp("act", atail[:, f, 0:2], ae[:, NT:NT + 2], (ak,), (f"atail{f}",))
                else:
                    ae3 = ae[:, 0:96].rearrange("p (s w) -> p s w", w=6)
                    cp("dve", ae3[:, :, 0:2], conv0T[f].rearrange("p (s w) -> p s w", w=2), ("conv0T",), (ak + "h",))
                    cp("act", ae3[:, :, 2:6], pav, (pk,), (ak,))
                    cp("act", atail[:, f, 2:34].rearrange("p (s w) -> p s w", w=2), ae3[:, :, 4:6], (ak,), (f"atail{f}",))
                fw.add("act", lambda e, acv=acv, pav=pav, f=f: e.activation(
                    out=acv, in_=pav, func=AF.Identity, scale=cw[:, 2, f:f + 1], bias=C["cb_col"][:, f:f + 1]),
                    (pk, "cw_col", "cb_col"), (ck,), weak=(pk,))

            def stB(f):
                sl, bank, ae, ac, si, psA, psB, v2, v1, v0, acv, pav = views(f)
                ak, ck = f"aext{sl}", f"acc{sl}"
                stt(acv, v1, cw[:, 1, f:f + 1], acv, ALU.mult, ALU.add, (ak, ak + "h", ck, "cw_col"), (ck,))
                stt(acv, v0, cw[:, 0, f:f + 1], acv, ALU.mult, ALU.add, (ak, ak + "h", ck, "cw_col"), (ck,))

            def stC(f):
                sl, bank, ae, ac, si, psA, psB, v2, v1, v0, acv, pav = views(f)
                act(si[:, 0:NT], ac[:, 0:NT], AF.Silu, (f"acc{sl}",), (f"sil{sl}",))

            def stD(f):
                sl, bank, ae, ac, si, psA, psB, v2, v1, v0, acv, pav = views(f)
                tt("dve", gT[:, f, 0:NT], si[:, 0:NT], psB, ALU.mult, (f"sil{sl}", f"ps{bank}"), (f"gT{f}",))

            for step in range(NF + 3):
                for (stg, lag) in ((stD, 3), (stC, 2), (stB, 1), (stA, 0)):
                    f = step - lag
                    if 0 <= f < NF:
                        stg(f)

        for sb, _ in tinfo(0)[3]:
            pro(0, sb)()
        for ti in range(len(tiles)):
            ffn_up(ti)
            cur = tinfo(ti)[3]
            nxt = tinfo(ti + 1)[3] if ti + 1 < len(tiles) else []
            for j in range(max(len(cur), len(nxt))):
                pbf = pro(ti + 1, j) if j < len(nxt) else None
                if j < len(cur):
                    epi(ti, j)
                if pbf is not None:
                    pbf()
        for f in range(NF):
            bk = 3 + (f // 4) % 4
            tr(psum[bk][0:34, (f % 4) * 128:(f % 4 + 1) * 128], atail[:, f, :], C["ident_f"], (f"atail{f}", "ident_f"),
               (f"ps{bk}",))
            if f % 4 == 3 or f == NF - 1:
                f0 = f - f % 4
                nn = (f - f0 + 1) * 128
                cp("act", tailT[:, f0 * 128:f0 * 128 + nn], psum[bk][0:34, 0:nn], (f"ps{bk}",), GTK)
        store(O["convo"], tailT, GTK, "convo")

    def dbg_rows(name, ap, r0, n):
        store(O[name][r0:r0 + n, :], ap, ("xb",), name)

    load_consts()
    if stop_after != "consts":
        load_w_in()
        phase0()
    if stop_after is None or stop_after.startswith("p1") or stop_after.startswith("g"):
        phase1a()
        if "d_Y" in DEBUG:
            dbg("d_Y", P1["Y"].rearrange("p a g j c -> p (a g j c)"), ("Y",))
            dbg("d_Ys", P1["Ys"].rearrange("p g j c -> p (g j c)"), ("Ys",))
        if stop_after is None or stop_after.startswith("p1b") or stop_after.startswith("p2"):
            phase1b()
        if "d_mix" in DEBUG:
            fw.add("sp", None, ("mix_d",), ("OUT_mixd",))
            out_chans.append("o_mixd")
        if stop_after is None or stop_after.startswith("p2"):
            phase2()

    fw.add("sp", None, tuple("OUT_" + c[2:] for c in out_chans), ())
    fw.finalize()
    fw.simulate()
    block = st.enter_context(nc.Block())
    fw.emit(nc, st, block)
    st.close()
    return nc


def make_in_maps(inputs):
    f = lambda a: np.ascontiguousarray(np.asarray(a, dtype=np.float32))
    shared = {
        "attn_norm": f(inputs["attn_norm"][0]), "w_in": f(inputs["w_in"][0]),
        "A_re": f(inputs["s5_A_re"][0]), "A_im": f(inputs["s5_A_im"][0]),
        "B_re": f(inputs["s5_B_re"][0]), "B_im": f(inputs["s5_B_im"][0]),
        "C_re": f(inputs["s5_C_re"][0]), "C_im": f(inputs["s5_C_im"][0]),
        "Dp": f(inputs["s5_D"][0]), "log_step": f(inputs["s5_log_step"][0]),
        "w_glu": f(inputs["w_glu"][0]), "b_glu": f(inputs["b_glu"][0]),
        "s5_out_norm": f(inputs["s5_out_norm"][0]), "w_gate_up": f(inputs["w_gate_up"][0]),
        "b_gate": f(inputs["b_gate"][0]), "gla_out_norm": f(inputs["gla_out_norm"][0]),
        "w_o": f(inputs["w_o"][0]), "ffn_norm": f(inputs["ffn_norm"][0]), "w_up": f(inputs["w_up"][0]),
        "conv_w": f(inputs["conv_w"][0]), "conv_b": f(inputs["conv_b"][0]), "w_down": f(inputs["w_down"][0]),
        "final_norm": f(inputs["final_norm"]),
    }
    maps = []
    for c in range(NCORES):
        m = dict(shared)
        sl = slice(NSEQ * c, NSEQ * (c + 1))
        m["xp"] = f(inputs["x_prompt"][c])
        m["xs"] = f(inputs["x_sample"][sl]).reshape(TS_, D)
        m["s5r"] = f(inputs["state_s5_re"][0, sl]).reshape(NSEQ, 2048)
        m["s5i"] = f(inputs["state_s5_im"][0, sl]).reshape(NSEQ, 2048)
        m["sgla"] = f(inputs["state_gla"][0, sl])
        m["sconv"] = f(inputs["state_conv"][0, sl]).reshape(2 * NSEQ, DFF)
        maps.append(m)
    return maps


_NC_CACHE = {}


def kernel(**inputs):
    if "nc" not in _NC_CACHE:
        _NC_CACHE["nc"] = build()
    nc = _NC_CACHE["nc"]
    in_maps = make_in_maps(inputs)
    res = run_bass_kernel_spmd(nc, in_maps, core_ids=list(range(NCORES)))
    R = res.results
    yp = np.stack([R[c]["yp"] for c in range(NCORES)]).reshape(8, 2048, D)
    ys = np.concatenate([R[c]["ys"].reshape(NSEQ, 4, D) for c in range(NCORES)], 0)
    p_re = np.stack([R[c]["s5o_r"][0].reshape(32, 64) for c in range(NCORES)])[None]
    p_im = np.stack([R[c]["s5o_i"][0].reshape(32, 64) for c in range(NCORES)])[None]
    s_re = np.concatenate([R[c]["s5o_r"][1:].reshape(NSEQ, 32, 64) for c in range(NCORES)], 0)[None]
    s_im = np.concatenate([R[c]["s5o_i"][1:].reshape(NSEQ, 32, 64) for c in range(NCORES)], 0)[None]
    p_gla = np.stack([R[c]["glao"][0] for c in range(NCORES)])[None]
    s_gla = np.concatenate([R[c]["glao"][1:] for c in range(NCORES)], 0)[None]
    p_conv = np.stack([R[c]["convo"][0:2] for c in range(NCORES)])[None]
    s_conv = np.concatenate([R[c]["convo"][2:].reshape(NSEQ, 2, DFF) for c in range(NCORES)], 0)[None]
    out = (yp, ys, p_re, p_im, p_gla, p_conv, s_re, s_im, s_gla, s_conv)
    return tuple(np.ascontiguousarray(o, dtype=np.float32) for o in out)
```

```python
import contextlib
import math
import numpy as np
import concourse.bass as bass
import concourse.mybir as mybir
from concourse.bass_utils import run_bass_kernel_spmd

F32 = mybir.dt.float32
BF16 = mybir.dt.bfloat16
I32 = mybir.dt.int32
U8 = mybir.dt.uint8
AF = mybir.ActivationFunctionType
ALU = mybir.AluOpType
ESZ = {F32: 4, BF16: 2, I32: 4, U8: 1}

NCORES = 8
D = 1024
TP_ = 2048
TS_ = 64
NSEQ = 16
INC = 2064
DFF = 2816
NF = 22
EPS = 1e-6
MCOL = 272
TWO_PI = 2.0 * math.pi
MAGIC = 12582912.0

DEBUG = {}
SAME_ENGINE_WAR = True


class Op:
    __slots__ = ("eng", "fn", "R", "W", "chan", "deps", "inc", "val", "waits", "barrier", "weak")

    def __init__(self, eng, fn, R, W, chan=None, barrier=False, weak=()):
        self.eng, self.fn, self.R, self.W, self.chan = eng, fn, tuple(R), tuple(W), chan
        self.weak = tuple(weak)
        self.deps = set()
        self.inc = False
        self.val = 0
        self.waits = []
        self.barrier = barrier


class FW:
    ENGS = ("pe", "act", "dve", "pool", "sp")

    def __init__(self):
        self.ops = []

    def add(self, eng, fn, R=(), W=(), chan=None, weak=()):
        self.ops.append(Op(eng, fn, R, W, chan, weak=weak))

    def barrier(self, engs=None):
        for e in (engs or self.ENGS):
            self.ops.append(Op(e, None, (), (), None, barrier=True))

    def finalize(self):
        ops = self.ops
        lastw = {}
        readers = {}
        last_on = {}
        bar_start = None
        i = 0
        n = len(ops)
        while i < n:
            op = ops[i]
            if op.barrier:
                j = i
                snap = dict(last_on)
                while j < n and ops[j].barrier:
                    ops[j].deps = set(snap.values())
                    j += 1
                for k in range(i, j):
                    last_on[("e", ops[k].eng)] = k
                i = j
                continue
            deps = set()
            for k in op.R:
                if k in lastw:
                    deps.add(lastw[k])
            for k in op.W:
                if k in lastw:
                    deps.add(lastw[k])
                for r in readers.get(k, {}).values():
                    deps.add(r)
            deps.discard(i)
            op.deps = deps
            rk = ("c", op.chan) if op.chan is not None else ("e", op.eng)
            for k in op.R:
                if k not in op.weak:
                    readers.setdefault(k, {})[rk] = i
            for k in op.W:
                lastw[k] = i
                readers[k] = {}
            if op.chan is not None:
                last_on[("c", op.chan)] = i
            else:
                last_on[("e", op.eng)] = i
            i += 1
        need = [set() for _ in ops]
        for i, op in enumerate(ops):
            for d in op.deps:
                dop = ops[d]
                if dop.chan is None and dop.eng == op.eng and op.chan is None:
                    if op.eng in ("pe", "sp"):
                        continue
                    if dop.fn is None:
                        continue
                    hazard = (set(dop.W) & (set(op.R) | set(op.W)))
                    if SAME_ENGINE_WAR:
                        hazard = hazard or (set(dop.R) & set(op.W))
                    if not hazard and not op.barrier:
                        continue
                if dop.chan is not None and dop.chan == op.chan:
                    continue
                need[i].add(d)
                dop.inc = True
        for op in ops:
            if op.chan is not None:
                op.inc = True
        cnt = {}
        for op in ops:
            if op.inc:
                key = ("c", op.chan) if op.chan is not None else ("e", op.eng)
                cnt[key] = cnt.get(key, 0) + (16 if op.chan is not None else 1)
                op.val = cnt[key]
        self.sem_keys = sorted(cnt.keys(), key=str)
        for op in ops:
            if op.inc and op.chan is not None and str(op.chan).startswith("G:"):
                op.val = cnt[("c", op.chan)]
        waited = {e: {} for e in self.ENGS}
        for i, op in enumerate(ops):
            best = {}
            for d in need[i]:
                dop = ops[d]
                key = ("c", dop.chan) if dop.chan is not None else ("e", dop.eng)
                if dop.val > best.get(key, 0):
                    best[key] = dop.val
            w = waited[op.eng]
            for key, v in best.items():
                if v > w.get(key, 0):
                    w[key] = v
                    op.waits.append((key, v))

    def simulate(self):
        per = {e: [op for op in self.ops if op.eng == e] for e in self.ENGS}
        pos = {e: 0 for e in self.ENGS}
        sem = {}
        progress = True
        while progress:
            progress = False
            for e in self.ENGS:
                lst = per[e]
                while pos[e] < len(lst):
                    op = lst[pos[e]]
                    if all(sem.get(k, 0) >= v for k, v in op.waits):
                        if op.inc:
                            key = ("c", op.chan) if op.chan is not None else ("e", op.eng)
                            sem[key] = sem.get(key, 0) + (16 if op.chan is not None else 1)
                        pos[e] += 1
                        progress = True
                    else:
                        break
        stuck = {e: pos[e] for e in self.ENGS if pos[e] < len(per[e])}
        if stuck:
            for e, p in stuck.items():
                op = per[e][p]
                print("DEADLOCK", e, p, "waits", op.waits, "R", op.R, "W", op.W,
                      "have", {k: sem.get(k, 0) for k, _ in op.waits})
            raise RuntimeError("deadlock in sync plan")
        return True

    def emit(self, nc, st, block):
        sems = {}
        for key in self.sem_keys:
            sems[key] = st.enter_context(nc.semaphore("s_" + "_".join(str(x) for x in key)))
        per = {e: [op for op in self.ops if op.eng == e] for e in self.ENGS}

        def run(engobj, lst):
            for op in lst:
                for key, v in op.waits:
                    engobj.wait_ge(sems[key], v)
                if op.fn is None:
                    if op.inc:
                        engobj.nop().then_inc(sems[("e", op.eng)], 1)
                    continue
                ins = op.fn(engobj)
                if op.inc:
                    key = ("c", op.chan) if op.chan is not None else ("e", op.eng)
                    ins.then_inc(sems[key], 16 if op.chan is not None else 1)

        @block.tensor
        def _(e):
            run(e, per["pe"])

        @block.scalar
        def _(e):
            run(e, per["act"])

        @block.vector
        def _(e):
            run(e, per["dve"])

        @block.gpsimd
        def _(e):
            run(e, per["pool"])

        @block.sync
        def _(e):
            run(e, per["sp"])


class Arena:
    def __init__(self, u8ap, lo, hi):
        self.ap, self.lo, self.hi, self.top = u8ap, lo, hi, lo

    def reset(self, top=None):
        self.top = self.lo if top is None else top

    def alloc(self, shape, dtype):
        free = 1
        for s in shape[1:]:
            free *= s
        nb = free * ESZ[dtype]
        off = (self.top + 63) // 64 * 64
        assert off + nb <= self.hi, f"arena overflow {off + nb} > {self.hi}"
        self.top = off + nb
        v = self.ap[:, off:off + nb].bitcast(dtype)
        if len(shape) > 2:
            names = " ".join(f"d{i}" for i in range(len(shape) - 1))
            kw = {f"d{i}": shape[i + 1] for i in range(len(shape) - 1)}
            v = v.rearrange(f"p ({names}) -> p {names}", **kw)
        if shape[0] < 128:
            v = v[0:shape[0]]
        return v


def bc(ap, axis, n):
    l = [list(t) for t in ap.ap]
    l.insert(axis, [0, n])
    return bass.AP(ap.tensor, ap.offset, l)


def dap(t, offset, dims):
    return bass.AP(t.tensor, offset, [list(d) for d in dims])


def build(stop_after=None):
    nc = bass.Bass("TRN2", target_bir_lowering=False)
    fw = FW()

    def din(name, shape, dt=F32):
        return nc.dram_tensor(name, list(shape), dt, kind="ExternalInput").ap()

    def dout(name, shape, dt=F32):
        return nc.dram_tensor(name, list(shape), dt, kind="ExternalOutput").ap()

    I = {}
    I["xp"] = din("xp", [TP_, D])
    I["xs"] = din("xs", [TS_, D])
    I["s5r"] = din("s5r", [NSEQ, 2048])
    I["s5i"] = din("s5i", [NSEQ, 2048])
    I["sgla"] = din("sgla", [NSEQ, 4, 64, 128])
    I["sconv"] = din("sconv", [2 * NSEQ, DFF])
    for nm, shp in [("attn_norm", [D]), ("w_in", [D, INC]), ("A_re", [32, 64]), ("A_im", [32, 64]),
                    ("B_re", [32, 64, 16]), ("B_im", [32, 64, 16]), ("C_re", [32, 16, 64]),
                    ("C_im", [32, 16, 64]), ("Dp", [32, 16]), ("log_step", [32]), ("w_glu", [512, 512]),
                    ("b_glu", [512]), ("s5_out_norm", [512]), ("w_gate_up", [16, 256]), ("b_gate", [256]),
                    ("gla_out_norm", [128]), ("w_o", [D, D]), ("ffn_norm", [D]), ("w_up", [D, 2 * DFF]),
                    ("conv_w", [3, DFF]), ("conv_b", [DFF]), ("w_down", [DFF, D]), ("final_norm", [D])]:
        I[nm] = din(nm, shp)
    O = {}
    O["yp"] = dout("yp", [TP_, D])
    O["ys"] = dout("ys", [TS_, D])
    O["s5o_r"] = dout("s5o_r", [17, 2048])
    O["s5o_i"] = dout("s5o_i", [17, 2048])
    O["glao"] = dout("glao", [17, 4, 64, 128])
    O["convo"] = dout("convo", [34, DFF])
    for k, (shp, dts) in DEBUG.items():
        O[k] = dout(k, shp, BF16 if dts == "bf16" else F32)
    if "d_mix" in DEBUG:
        mix_d = O["d_mix"]
    else:
        mix_d = nc.dram_tensor("mix_d", [128, 8, TP_ + TS_], BF16, kind="Internal").ap()

    st = contextlib.ExitStack()
    ARENA_BYTES = 206 * 1024
    arena_t = st.enter_context(nc.sbuf_tensor("arena", [128, ARENA_BYTES], U8))
    psum = [st.enter_context(nc.psum_tensor(f"ps{i}", [128, 512], F32)) for i in range(8)]
    CONST = Arena(arena_t, 0, 14 * 1024)
    TAB = Arena(arena_t, 14 * 1024, 46 * 1024)
    REST = Arena(arena_t, 46 * 1024, ARENA_BYTES)

    def psb(i, n=1024):
        return psum[i][:, 0:n // 2].bitcast(BF16)

    dma_ctr = [0]

    def dma(out, in_, R, W, chan, eng="sp", **kw):
        def fn(e, out=out, in_=in_, kw=kw):
            return e.dma_start(out=out, in_=in_, **kw)
        fw.add(eng, fn, R, W, chan=chan)

    def mm(out, lhsT, rhs, R, W, start=True, stop=True):
        def fn(e):
            return e.matmul(out, lhsT, rhs, start=start, stop=stop)
        fw.add("pe", fn, R, W)

    def tr(out, in_, ident, R, W):
        def fn(e):
            return e.transpose(out=out, in_=in_, identity=ident)
        fw.add("pe", fn, R, W)

    def act(out, in_, func, R, W, **kw):
        def fn(e):
            return e.activation(out=out, in_=in_, func=func, **kw)
        fw.add("act", fn, R, W)

    def tt(eng, out, in0, in1, op, R, W):
        def fn(e):
            return e.tensor_tensor(out=out, in0=in0, in1=in1, op=op)
        fw.add(eng, fn, R, W)

    def ts(eng, out, in0, s1, op0, R, W, s2=None, op1=None, accum_out=None):
        def fn(e):
            if op1 is None:
                return e.tensor_scalar(out=out, in0=in0, scalar1=s1, scalar2=None, op0=op0)
            return e.tensor_scalar(out=out, in0=in0, scalar1=s1, scalar2=s2, op0=op0, op1=op1)
        fw.add(eng, fn, R, W)

    def stt(out, in0, scalar, in1, op0, op1, R, W):
        def fn(e):
            return e.scalar_tensor_tensor(out=out, in0=in0, scalar=scalar, in1=in1, op0=op0, op1=op1)
        fw.add("dve", fn, R, W)

    def cp(eng, out, in_, R, W):
        if eng == "act":
            def fn(e):
                return e.copy(out=out, in_=in_)
        else:
            def fn(e):
                return e.tensor_copy(out=out, in_=in_)
        fw.add(eng, fn, R, W)

    def mset(eng, ap, val, W):
        def fn(e):
            return e.memset(ap, val)
        fw.add(eng, fn, (), W)

    def recip(out, in_, R, W):
        def fn(e):
            return e.reciprocal(out=out, in_=in_)
        fw.add("dve", fn, R, W)

    def iota(out, pattern, base, cm, W):
        def fn(e):
            return e.iota(out, pattern=pattern, base=base, channel_multiplier=cm)
        fw.add("pool", fn, (), W)

    out_chans = []

    def store(out, in_, R, name):
        ch = "o_" + name
        if ch not in out_chans:
            out_chans.append(ch)
        dma(out, in_, R, ("OUT_" + name,), ch)

    def dbg(name, ap, R):
        if name in DEBUG:
            store(O[name], ap, R, name)

    C = {}
    C["ident_f"] = CONST.alloc([128, 128], F32)
    C["ident_b"] = CONST.alloc([128, 128], BF16)
    C["tri_f"] = CONST.alloc([128, 128], F32)
    C["ones_b"] = CONST.alloc([128, 128], BF16)
    C["gnorm_row"] = CONST.alloc([128, 128], F32)
    C["fin_row"] = CONST.alloc([128, 1024], F32)
    C["attn_col"] = CONST.alloc([128, 8], F32)
    C["ffn_col"] = CONST.alloc([128, 8], F32)
    C["s5n_col"] = CONST.alloc([128, 4], F32)
    C["bglu_col"] = CONST.alloc([128, 4], F32)
    C["cw_col"] = CONST.alloc([128, 3, NF], F32)
    C["cb_col"] = CONST.alloc([128, NF], F32)
    C["wgate"] = CONST.alloc([16, 256], BF16)
    C["bgate"] = CONST.alloc([1, 256], BF16)
    C["ones_row"] = CONST.alloc([1, 128], BF16)
    C["padmask"] = CONST.alloc([128, 1], F32)
    C["eps_col"] = CONST.alloc([128, 1], F32)
    C["L8"] = CONST.alloc([128, 2, 16], F32)
    C["LK"] = CONST.alloc([128, 2, 4, 16], F32)
    TPW_OFF = (CONST.top + 63) // 64 * 64
    C["TPW"] = CONST.alloc([128, 2, 16, 16], F32)
    C["Lm4"] = CONST.alloc([128, 2, 16], F32)
    C["fin"] = CONST.alloc([128, 2, 16, 17], F32)
    C["itmp"] = st.enter_context(nc.sbuf_tensor("iota_i32", [128, 128], I32))[:, :]
    C["pmtmp"] = CONST.alloc([128, 8], F32)

    def load_consts():
        tmpi = C["itmp"]
        iota(tmpi, [[1, 128]], 0, -1, ("itmp",))
        cp("pool", C["ident_f"], tmpi, ("itmp",), ("c_tmpf",))
        ts("dve", C["ident_b"], C["ident_f"], 0.0, ALU.is_equal, ("c_tmpf",), ("ident_b",))
        ts("dve", C["tri_f"], C["ident_f"], 0.0, ALU.is_ge, ("c_tmpf", "ident_b"), ("tri_f",))
        ts("dve", C["ident_f"], C["ident_f"], 0.0, ALU.is_equal, ("tri_f", "ident_b", "c_tmpf"), ("ident_f",))
        mset("dve", C["tri_f"][0:64, 64:128], 0.0, ("tri_f",))
        mset("dve", C["ones_b"], 1.0, ("ones_b",))
        mset("dve", C["ones_row"], 1.0, ("ones_row",))
        mset("dve", C["padmask"], 0.0, ("padmask",))
        mset("dve", C["eps_col"], EPS, ("eps_col",))
        C["mhalf"] = C["pmtmp"][:, 4:5]
        mset("dve", C["mhalf"], -0.5, ("mhalf",))
        iota(tmpi[:, 0:1], [[0, 1]], 0, 1, ("itmp",))
        cp("pool", C["padmask"], tmpi[:, 0:1], ("itmp",), ("padmask",))
        pm = C["pmtmp"]
        ts("dve", pm[:, 1:2], C["padmask"], 60.0, ALU.is_ge, ("padmask",), ("pmtmp",))
        ts("dve", pm[:, 2:3], C["padmask"], 64.0, ALU.is_ge, ("padmask",), ("pmtmp",))
        ts("dve", pm[:, 3:4], C["padmask"], 124.0, ALU.is_ge, ("padmask",), ("pmtmp",))
        tt("dve", pm[:, 1:2], pm[:, 1:2], pm[:, 2:3], ALU.subtract, ("pmtmp",), ("pmtmp",))
        tt("dve", C["padmask"], pm[:, 1:2], pm[:, 3:4], ALU.add, ("pmtmp",), ("padmask",))
        dma(C["gnorm_row"], dap(I["gla_out_norm"], 0, [[0, 128], [1, 128]]), (), ("gnorm_row",), "G:cst")
        dma(C["fin_row"], dap(I["final_norm"], 0, [[0, 128], [1, 1024]]), (), ("fin_row",), "G:cst")
        dma(C["attn_col"], dap(I["attn_norm"], 0, [[1, 128], [128, 8]]), (), ("attn_col",), "G:cst",
            allow_slow_non_contiguous=True)
        dma(C["ffn_col"], dap(I["ffn_norm"], 0, [[1, 128], [128, 8]]), (), ("ffn_col",), "G:cst",
            allow_slow_non_contiguous=True)
        dma(C["s5n_col"], dap(I["s5_out_norm"], 0, [[1, 128], [128, 4]]), (), ("s5n_col",), "G:cst",
            allow_slow_non_contiguous=True)
        dma(C["bglu_col"], dap(I["b_glu"], 0, [[1, 128], [128, 4]]), (), ("bglu_col",), "G:cst",
            allow_slow_non_contiguous=True)
        dma(C["cw_col"], dap(I["conv_w"], 0, [[1, 128], [DFF, 3], [128, NF]]), (), ("cw_col",), "G:cst",
            allow_slow_non_contiguous=True)
        dma(C["cb_col"], dap(I["conv_b"], 0, [[1, 128], [128, NF]]), (), ("cb_col",), "G:cst",
            allow_slow_non_contiguous=True)
        dma(C["wgate"], I["w_gate_up"], (), ("wgate",), "G:cstp", eng="pool")
        dma(C["bgate"], dap(I["b_gate"], 0, [[0, 1], [1, 256]]), (), ("bgate",), "G:cstp", eng="pool")

    CK = ("ident_f", "ident_b", "tri_f")

    T = {}
    T["BL"] = TAB.alloc([128, 32, 2, 64], BF16)
    T["T0"] = TAB.alloc([128, 32, 128], BF16)
    T["CL"] = TAB.alloc([128, 32, 2, 128], BF16)

    def phase0():
        R0 = Arena(arena_t, REST.lo, REST.hi)

        def A(shape, dt=F32):
            return R0.alloc(shape, dt)
        LSrow = A([128, 32]); Bnr = A([128, 16, 16]); Bni = A([128, 16, 16])
        P0_KEEP = R0.top
        Acr = A([128, 16]); Aci = A([128, 16]); dtc = A([128, 16])
        arc = A([128, 16]); thc = A([128, 16])
        dma(Acr, dap(I["A_re"], 0, [[1, 128], [128, 16]]), (), ("Acr",), "G:p0a", allow_slow_non_contiguous=True)
        dma(Aci, dap(I["A_im"], 0, [[1, 128], [128, 16]]), (), ("Aci",), "G:p0a", allow_slow_non_contiguous=True)
        dma(LSrow, dap(I["log_step"], 0, [[0, 128], [1, 32]]), (), ("LSrow",), "G:p0a")
        act(LSrow, LSrow, AF.Exp, ("LSrow",), ("LSrow",))
        cp("dve", dtc[0:64, :], LSrow[0:64, 0:32:2], ("LSrow",), ("dtc",))
        cp("dve", dtc[64:128, :], LSrow[64:128, 1:32:2], ("LSrow",), ("dtc",))
        tt("dve", arc, Acr, dtc, ALU.mult, ("Acr", "dtc"), ("arc",))
        tt("dve", thc, Aci, dtc, ALU.mult, ("Aci", "dtc"), ("thc",))
        NN = 31
        nlist = list(range(-7, 9)) + [8 * k for k in range(2, 17)]

        def nidx(n):
            return n + 7 if n <= 8 else 15 + (n // 8 - 1)
        NV = A([128, NN, 16]); argE = A([128, NN, 16]); argS = A([128, NN, 16]); tmpk = A([128, NN, 16])
        rr = A([128, NN, 16]); LPr = A([128, NN, 16]); LPi = A([128, NN, 16]); Ecol = A([128, NN, 16])
        for j, n in enumerate(nlist):
            mset("dve", NV[:, j, :], float(n), ("NV",))
        tt("dve", argE, NV, bc(arc, 1, NN), ALU.mult, ("NV", "arc"), ("argE",))
        tt("dve", argS, NV, bc(thc, 1, NN), ALU.mult, ("NV", "thc"), ("argS",))
        act(Ecol, argE, AF.Exp, ("argE",), ("Ecol",))

        def sincos(eng, out_s, out_c, arg, tmp, rtile, etile, kin, kout_s, kout_c, tkey, rkey, ekey):
            for shift, outt, kout in ((0.0, out_s, kout_s), (math.pi / 2, out_c, kout_c)):
                C1 = 6.28125
                C2 = TWO_PI - C1
                ts(eng, outt, arg, shift, ALU.add, (kin,), (kout,))
                ts(eng, tmp, outt, 1.0 / TWO_PI, ALU.mult, (kout,), (tkey,), s2=MAGIC, op1=ALU.add)
                ts(eng, tmp, tmp, -MAGIC, ALU.add, (tkey,), (tkey,))
                ts(eng, rtile, tmp, -C1, ALU.mult, (tkey,), (rkey,))
                tt(eng, rtile, rtile, outt, ALU.add, (rkey, kout), (rkey,))
                ts(eng, tmp, tmp, -C2, ALU.mult, (tkey,), (tkey,))
                tt(eng, rtile, rtile, tmp, ALU.add, (rkey, tkey), (rkey,))
                ts(eng, rtile, rtile, 3.1415925, ALU.min, (rkey,), (rkey,), s2=-3.1415925, op1=ALU.max)
                act(outt, rtile, AF.Sin, (rkey,), (kout,))
                tt(eng, outt, outt, etile, ALU.mult, (kout, ekey), (kout,))

        sincos("dve", LPi, LPr, argS, tmpk, rr, Ecol, "argS", "LPi", "LPr", "tmpk", "rr", "Ecol")
        for ri, LP in ((0, LPr), (1, LPi)):
            key = "LPr" if ri == 0 else "LPi"
            cp("dve", C["L8"][:, ri, :], LP[:, nidx(8), :], (key,), ("L8",))
            cp("dve", C["Lm4"][:, ri, :], LP[:, nidx(-4), :], (key,), ("Lm4",))
            cp("dve", C["LK"][:, ri, 0, :], LP[:, nidx(128), :], (key,), ("LK",))
            cp("dve", C["TPW"][:, ri, :, 0], LP[:, nidx(0), :], (key,), ("TPW",))
            cp("dve", C["TPW"][:, ri, :, 1], LP[:, nidx(8), :], (key,), ("TPW",))
            for k in range(2, 16):
                cp("dve", C["TPW"][:, ri, :, k], LP[:, nidx(8 * k), :], (key,), ("TPW",))
        sq1 = A([128, 16]); sq2 = A([128, 16])
        for k in range(1, 4):
            pr = C["LK"][:, 0, k - 1, :]; pi_ = C["LK"][:, 1, k - 1, :]
            tt("dve", sq1, pr, pr, ALU.mult, ("LK",), ("sq1",))
            tt("dve", sq2, pi_, pi_, ALU.mult, ("LK",), ("sq2",))
            tt("dve", C["LK"][:, 0, k, :], sq1, sq2, ALU.subtract, ("sq1", "sq2"), ("LK",))
            tt("dve", sq1, pr, pi_, ALU.mult, ("LK",), ("sq1",))
            ts("dve", C["LK"][:, 1, k, :], sq1, 2.0, ALU.mult, ("sq1",), ("LK",))
        den = A([128, 16]); crc = A([128, 16]); cic = A([128, 16]); nrc = A([128, 16]); t16 = A([128, 16])
        l1r = LPr[:, nidx(1), :]; l1i = LPi[:, nidx(1), :]
        tt("dve", den, Acr, Acr, ALU.mult, ("Acr",), ("den",))
        tt("dve", t16, Aci, Aci, ALU.mult, ("Aci",), ("t16",))
        tt("dve", den, den, t16, ALU.add, ("den", "t16"), ("den",))
        recip(den, den, ("den",), ("den",))
        ts("dve", nrc, l1r, -1.0, ALU.add, ("LPr",), ("nrc",))
        tt("dve", crc, nrc, Acr, ALU.mult, ("nrc", "Acr"), ("crc",))
        tt("dve", t16, l1i, Aci, ALU.mult, ("LPi", "Aci"), ("t16",))
        tt("dve", crc, crc, t16, ALU.add, ("crc", "t16"), ("crc",))
        tt("dve", crc, crc, den, ALU.mult, ("crc", "den"), ("crc",))
        tt("dve", cic, l1i, Acr, ALU.mult, ("LPi", "Acr"), ("cic",))
        tt("dve", t16, nrc, Aci, ALU.mult, ("nrc", "Aci"), ("t16",))
        tt("dve", cic, cic, t16, ALU.subtract, ("cic", "t16"), ("cic",))
        tt("dve", cic, cic, den, ALU.mult, ("cic", "den"), ("cic",))
        if stop_after == "p0col":
            dbg("d_dtc", dtc, ("dtc",)); dbg("d_thc", thc, ("thc",)); dbg("d_arc", arc, ("arc",))
            dbg("d_argS", argS.rearrange("p n g -> p (n g)"), ("argS",))
            dbg("d_Ecol", Ecol.rearrange("p n g -> p (n g)"), ("Ecol",))
            dbg("d_LPr", LPr.rearrange("p n g -> p (n g)"), ("LPr",))
            dbg("d_LPi", LPi.rearrange("p n g -> p (n g)"), ("LPi",))
            dbg("d_rr", rr.rearrange("p n g -> p (n g)"), ("rr",))
            dbg("d_TPW", C["TPW"].rearrange("p r g k -> p (r g k)"), ("TPW",))
            dbg("d_LK", C["LK"].rearrange("p r k g -> p (r k g)"), ("LK",))
            return
        dma(Bnr, dap(I["B_re"], 0, [[16, 128], [2048, 16], [1, 16]]), (), ("Bnr",), "G:p0b")
        dma(Bni, dap(I["B_im"], 0, [[16, 128], [2048, 16], [1, 16]]), (), ("Bni",), "G:p0b")
        Bbr = A([128, 16, 16]); Bbi = A([128, 16, 16]); t256 = A([128, 16, 16])
        crb = bc(crc, 2, 16); cib = bc(cic, 2, 16)
        tt("dve", Bbr, Bnr, crb, ALU.mult, ("Bnr", "crc"), ("Bbr",))
        tt("dve", t256, Bni, cib, ALU.mult, ("Bni", "cic"), ("t256",))
        tt("dve", Bbr, Bbr, t256, ALU.subtract, ("Bbr", "t256"), ("Bbr",))
        tt("dve", Bbi, Bni, crb, ALU.mult, ("Bni", "crc"), ("Bbi",))
        tt("dve", t256, Bnr, cib, ALU.mult, ("Bnr", "cic"), ("t256",))
        tt("dve", Bbi, Bbi, t256, ALU.add, ("Bbi", "t256"), ("Bbi",))
        if stop_after == "p0Bb":
            dbg("d_x", Bbr.rearrange("p g c -> p (g c)"), ("Bbr", "Bbi"))
            return
        Cn2r = A([128, 4, 128]); Cn2i = A([128, 4, 128])
        for h in range(2):
            dma(Cn2r[:, :, 64 * h:64 * h + 64], dap(I["C_re"], 0, [[64, 128], [8192, 4], [1, 64]]), (), ("Cn2r",), "G:p0b")
            dma(Cn2i[:, :, 64 * h:64 * h + 64], dap(I["C_im"], 0, [[64, 128], [8192, 4], [1, 64]]), (), ("Cn2i",), "G:p0b")
        Ccr = A([128, 32, 16]); Cci = A([128, 32, 16])
        for (src, dstt, pk, sk, dk) in ((Cn2r, Ccr, 0, "Cn2r", "Ccr"), (Cn2i, Cci, 1, "Cn2i", "Cci")):
            for q in range(4):
                mm(psum[pk][:, q * 128:(q + 1) * 128], src[:, q, :], C["ident_f"], (sk, "ident_f"), (f"ps{pk}",))
            cp("dve", dstt.rearrange("p g c -> p (g c)"), psum[pk][:, 0:512], (f"ps{pk}",), (dk,))
        if stop_after == "p0Cc":
            dbg("d_x", Ccr.rearrange("p g c -> p (g c)")[:, 0:256], ("Ccr", "Cci"))
            return
        mset("dve", T["CL"], 0.0, ("CL",))
        X_r = A([128, 16, 8, 16]); X_i = A([128, 16, 8, 16]); Y_r = A([128, 16, 8, 16]); Y_i = A([128, 16, 8, 16])
        t1 = A([128, 16, 8, 16]); t2 = A([128, 16, 8, 16])

        def lp(LP, n0, step, half):
            v = LP[64 * half:64 * half + 64]
            l = [list(t) for t in v.ap]
            nstr, gstr = l[1][0], l[2][0]
            return bass.AP(v.tensor, v.offset + nidx(n0) * nstr, [l[0], [gstr, 16], [step * nstr, 8], [0, 16]])

        Cown_r = A([128, 16, 16]); Cown_i = A([128, 16, 16])
        for half in range(2):
            sl = slice(64 * half, 64 * half + 64)
            cp("dve", Cown_r[sl], Ccr[sl, half:32:2, :], ("Ccr",), ("Cown_r",))
            cp("dve", Cown_i[sl], Cci[sl, half:32:2, :], ("Cci",), ("Cown_i",))

        def lpf(LP, n0, step):
            nstr, gstr = LP.ap[1][0], LP.ap[2][0]
            return bass.AP(LP.tensor, LP.offset + nidx(n0) * nstr, [list(LP.ap[0]), [gstr, 16], [step * nstr, 8], [0, 16]])
        Cr_b = bc(Cown_r, 2, 8); Ci_b = bc(Cown_i, 2, 8)
        for (n0, dst_r, dst_i, kr, ki) in ((1, None, None, None, None), (0, Y_r, Y_i, "Y_r", "Y_i")):
            Lr = lpf(LPr, n0, 1); Li = lpf(LPi, n0, 1)
            tt("dve", t1, Cr_b, Lr, ALU.mult, ("Cown_r", "LPr"), ("t1",))
            tt("dve", t2, Ci_b, Li, ALU.mult, ("Cown_i", "LPi"), ("t2",))
            if dst_r is None:
                for half in range(2):
                    sl = slice(64 * half, 64 * half + 64)
                    clr = T["CL"][sl, half:32:2, 0, :].rearrange("p g (i c) -> p g i c", i=8)
                    tt("dve", clr, t1[sl], t2[sl], ALU.subtract, ("t1", "t2"), ("CL",))
            else:
                tt("dve", dst_r, t1, t2, ALU.subtract, ("t1", "t2"), (kr,))
            tt("dve", t1, Cr_b, Li, ALU.mult, ("Cown_r", "LPi"), ("t1",))
            tt("dve", t2, Ci_b, Lr, ALU.mult, ("Cown_i", "LPr"), ("t2",))
            tt("dve", t1, t1, t2, ALU.add, ("t1", "t2"), ("t1",))
            if dst_r is None:
                for half in range(2):
                    sl = slice(64 * half, 64 * half + 64)
                    cli = T["CL"][sl, half:32:2, 1, :].rearrange("p g (i c) -> p g i c", i=8)
                    ts("dve", cli, t1[sl], -1.0, ALU.mult, ("t1",), ("CL",))
            else:
                ts("dve", dst_i, t1, -1.0, ALU.mult, ("t1",), (ki,))
        if stop_after == "p0CL":
            dbg("d_CL", T["CL"].rearrange("p g r n -> p (g r n)"), ("CL", "Y_r", "Y_i"))
            return
        Lr = bass.AP(LPr.tensor, LPr.offset + nidx(0) * LPr.ap[1][0],
                     [list(LPr.ap[0]), [LPr.ap[2][0], 16], [-LPr.ap[1][0], 8], [0, 16]])
        Li = bass.AP(LPi.tensor, LPi.offset + nidx(0) * LPi.ap[1][0],
                     [list(LPi.ap[0]), [LPi.ap[2][0], 16], [-LPi.ap[1][0], 8], [0, 16]])
        Bbr_b = bc(Bbr, 2, 8); Bbi_b = bc(Bbi, 2, 8)
        tt("dve", t1, Lr, Bbr_b, ALU.mult, ("LPr", "Bbr"), ("t1",))
        tt("dve", t2, Li, Bbi_b, ALU.mult, ("LPi", "Bbi"), ("t2",))
        tt("dve", X_r, t1, t2, ALU.subtract, ("t1", "t2"), ("X_r",))
        tt("dve", t1, Lr, Bbi_b, ALU.mult, ("LPr", "Bbi"), ("t1",))
        tt("dve", t2, Li, Bbr_b, ALU.mult, ("LPi", "Bbr"), ("t2",))
        tt("dve", X_i, t1, t2, ALU.add, ("t1", "t2"), ("X_i",))
        if stop_after == "p0X":
            dbg("d_x", X_r.rearrange("p g j c -> p (g j c)")[:, 0:256], ("X_r", "X_i"))
            return
        mask8 = A([128, 128]); Drow = A([128, 32, 16]); Dterm = A([128, 16, 8, 16]); tmpT = A([128, 16, 128])
        iota(C["itmp"], [[16, 8], [0, 16]], 15, -1, ("itmp",))
        cp("dve", mask8, C["itmp"], ("itmp",), ("mask8",))
        ts("dve", mask8, mask8, 0.0, ALU.is_ge, ("mask8",), ("mask8",))
        dma(Drow, dap(I["Dp"], 0, [[0, 128], [16, 32], [1, 16]]), (), ("Drow",), "G:p0b")
        Xm_r = A([128, 16, 128]); Xm_i = A([128, 16, 128])
        for rnd in range(2):
            tt("dve", Dterm, bc(C["ident_f"].rearrange("p (i c) -> p i c", i=8), 1, 16),
               bc(Drow[:, rnd * 16:(rnd + 1) * 16, :], 2, 8), ALU.mult, ("ident_f", "Drow"), ("Dterm",))
            for (Xm, Xs, kx, kxm) in ((Xm_r, X_r, "X_r", "Xm_r"), (Xm_i, X_i, "X_i", "Xm_i")):
                mset("dve", Xm, 0.0, (kxm,))
                for half in range(2):
                    sl = slice(64 * half, 64 * half + 64)
                    cp("dve", Xm[sl, half:16:2, :],
                       Xs[sl, rnd * 8:(rnd + 1) * 8].rearrange("p g j c -> p g (j c)"), (kx,), (kxm,))
            for gg in range(16):
                g = rnd * 16 + gg
                gp = g // 2
                bank = psum[gg // 4]
                o = bank[:, (gg % 4) * 128:(gg % 4 + 1) * 128]
                mm(o, Xm_r[:, gg, :], Y_r[:, gp].rearrange("p i c -> p (i c)"),
                   ("Xm_r", "Y_r"), (f"ps{gg // 4}",), start=True, stop=False)
                mm(o, Xm_i[:, gg, :], Y_i[:, gp].rearrange("p i c -> p (i c)"),
                   ("Xm_i", "Y_i"), (f"ps{gg // 4}",), start=False, stop=True)
            for b4 in range(4):
                tt("dve", tmpT[:, b4 * 4:(b4 + 1) * 4, :], psum[b4][:, 0:512].rearrange("p (g n) -> p g n", g=4),
                   bc(mask8, 1, 4), ALU.mult, (f"ps{b4}", "mask8"), ("tmpT",))
            tt("dve", T["T0"][:, rnd * 16:(rnd + 1) * 16, :], tmpT,
               Dterm.rearrange("p g i c -> p g (i c)"), ALU.add,
               ("tmpT", "Dterm"), ("T0",))
        BLc = [X_r, X_i]
        nstr, gstr = LPr.ap[1][0], LPr.ap[2][0]
        LrB = bass.AP(LPr.tensor, LPr.offset + nidx(7) * nstr, [list(LPr.ap[0]), [gstr, 16], [-nstr, 8], [0, 16]])
        LiB = bass.AP(LPi.tensor, LPi.offset + nidx(7) * nstr, [list(LPi.ap[0]), [gstr, 16], [-nstr, 8], [0, 16]])
        tt("dve", t1, LrB, Bbr_b, ALU.mult, ("LPr", "Bbr"), ("t1",))
        tt("dve", t2, LiB, Bbi_b, ALU.mult, ("LPi", "Bbi"), ("t2",))
        tt("dve", BLc[0], t1, t2, ALU.subtract, ("t1", "t2"), ("X_r",))
        tt("dve", t1, LrB, Bbi_b, ALU.mult, ("LPr", "Bbi"), ("t1",))
        tt("dve", t2, LiB, Bbr_b, ALU.mult, ("LPi", "Bbr"), ("t2",))
        tt("dve", BLc[1], t1, t2, ALU.add, ("t1", "t2"), ("X_i",))
        cnt = 0
        for ri in range(2):
            for q4 in range(4):
                bk = 4 + cnt % 4
                cnt += 1
                for gq in range(4):
                    gp = q4 * 4 + gq
                    tr(psum[bk][:, gq * 128:(gq + 1) * 128], BLc[ri][:, gp].rearrange("p j c -> p (j c)"), C["ident_f"],
                       ("X_r" if ri == 0 else "X_i", "ident_f"), (f"ps{bk}",))
                cp("act" if cnt % 2 == 0 else "dve", T["BL"][:, q4 * 8:(q4 + 1) * 8, ri, :],
                   psum[bk][:, 0:512].rearrange("p (g q) -> p g q", g=8), (f"ps{bk}",), ("BL",))
        if "d_BL" in DEBUG:
            dbg("d_BL", T["BL"].rearrange("p g r q -> p (g r q)"), ("BL",))
            dbg("d_T0", T["T0"].rearrange("p g n -> p (g n)"), ("T0",))
            dbg("d_CL", T["CL"].rearrange("p g r n -> p (g r n)"), ("CL",))
            dbg("d_TPW", C["TPW"].rearrange("p r g k -> p (r g k)"), ("TPW",))
            dbg("d_LK", C["LK"].rearrange("p r k g -> p (r k g)"), ("LK",))


    W_IN_BYTES = 8 * INC * 2
    w_in_sb = arena_t[:, ARENA_BYTES - W_IN_BYTES:ARENA_BYTES].bitcast(BF16).rearrange("p (k n) -> p k n", k=8)
    REST.hi = ARENA_BYTES - W_IN_BYTES - 64

    def load_w_in():
        for kc in range(8):
            dma(w_in_sb[:, kc, :], I["w_in"][kc * 128:(kc + 1) * 128, :], (), ("w_in",), "G:w_in", eng="pool",
                max_dma_last_dim=4096)

    P1 = {}

    def alloc_p1():
        REST.reset()
        A = REST.alloc
        P1["Y"] = A([128, 2, 32, 8, 16], BF16)
        P1["Ys"] = A([16, 32, 8, 16], BF16)
        P1["P1_KEEP"] = REST.top
        P1["hT"] = A([128, 8, 1024], BF16)
        P1["hTs"] = A([128, 8, 64], BF16)
        P1["hTpad"] = A([128, 8, 128], BF16)
        P1["x"] = [A([128, 1024], F32) for _ in range(2)]
        P1["hb"] = [A([128, 1024], BF16) for _ in range(2)]
        P1["st"] = A([128, 16], F32)
        P1["qf"] = A([128, 2, 512], F32)
        P1["kf"] = A([128, 2, 512], F32)
        P1["zT"] = A([16, 512], BF16)
        P1["vtok"] = [A([128, 512], BF16) for _ in range(6)]
        P1["gsil"] = [A([128, 512], BF16) for _ in range(6)]
        P1["lnv"] = A([128, 256], F32)
        P1["Epl"] = [A([128, 2, 128], F32) for _ in range(2)]
        P1["Emi"] = A([128, 2, 128], F32)
        P1["qeT"] = A([128, 2, 128], BF16)
        P1["QA"] = [A([128, 2, 128], BF16) for _ in range(2)]
        P1["QB"] = [A([128, 2, 128], BF16) for _ in range(2)]
        P1["keT"] = A([128, 2, 128], BF16)
        P1["ketok"] = A([128, 256], BF16)
        P1["attm"] = [A([128, 4, 128], BF16) for _ in range(2)]
        P1["Psb"] = [A([128, 2, 2, 128], F32) for _ in range(2)]
        P1["Sf"] = [A([128, 2, 128], F32) for _ in range(2)]
        P1["Sblk"] = [A([128, 2, 2, 128], BF16) for _ in range(2)]
        P1["Fst"] = [A([128, 2, 128], F32) for _ in range(2)]
        P1["stmp"] = A([128, 2, 128], F32)
        P1["osq"] = A([128, 4, 128], F32)
        P1["on"] = [A([128, 512], BF16) for _ in range(2)]
        P1["mst"] = A([128, 4, 128], BF16)
        P1["P1_TOP"] = REST.top

    xctr = [0]

    def norm_tile(x_dram_rows, nrows, hT_dst, gain_col, xkey_pref="x"):
        slot = xctr[0] % 2
        xctr[0] += 1
        xb = P1["x"][slot][0:nrows]
        hb = P1["hb"][slot][0:nrows]
        xk, hk, sk = f"x{slot}", f"hb{slot}", f"st{slot}"
        ss = P1["st"][0:nrows, slot * 4:slot * 4 + 1]
        rs = P1["st"][0:nrows, slot * 4 + 1:slot * 4 + 2]
        dma(xb, x_dram_rows, (), (xk,), f"x{slot}")
        fw.add("act", lambda e: e.activation(out=hb, in_=xb, func=AF.Square, accum_out=ss), (xk,), (hk, sk))
        act(rs, ss, AF.Ln, (sk,), (sk,), scale=1.0 / D, bias=C["eps_col"][0:nrows])
        act(rs, rs, AF.Exp, (sk,), (sk,), scale=-0.5)
        fw.add("act", lambda e: e.activation(out=hb, in_=xb, func=AF.Copy, scale=rs), (xk, sk), (hk,))
        pb = psb(0)
        for kc in range(8):
            tr(pb[0:128, kc * 128:kc * 128 + nrows], hb[:, kc * 128:(kc + 1) * 128], C["ident_b"][0:nrows, 0:nrows],
               (hk, "ident_b"), ("ps0a", "ps0b"))
        pv = pb.rearrange("p (k n) -> p k n", k=8)[:, :, 0:nrows]
        tt("dve", hT_dst, pv, bc(gain_col, 2, nrows), ALU.mult, ("ps0a", "ps0b", "attn_col"), ("hT",))

    def proj_qkz(hT_ap, n, tagR=("hT",)):
        idx = 0
        for (dst, col0, key) in ((P1["qf"], 512, "qf"), (P1["kf"], 768, "kf")):
            for c2 in range(2):
                b = 1 + idx % 2
                idx += 1
                for kc in range(8):
                    mm(psum[b][:, 0:n], w_in_sb[:, kc, col0 + c2 * 128:col0 + (c2 + 1) * 128], hT_ap[:, kc, :],
                       ("w_in",) + tagR, (f"ps{b}",), start=(kc == 0), stop=(kc == 7))
                cp("act", dst[:, c2, 0:n], psum[b][:, 0:n], (f"ps{b}",), (key,))
        b = 1 + idx % 2
        for kc in range(8):
            mm(psum[b][0:16, 0:n], w_in_sb[:, kc, 2048:2064], hT_ap[:, kc, :], ("w_in",) + tagR, (f"ps{b}",),
               start=(kc == 0), stop=(kc == 7))
        cp("dve", P1["zT"][:, 0:n], psum[b][0:16, 0:n], (f"ps{b}",), ("zT",))

    def proj_vg(hT_ap, slot):
        vtok, gsil = P1["vtok"][slot], P1["gsil"][slot]
        kv, kg = f"vtok{slot}", f"gsil{slot}"
        for (b, col0) in ((3, 1024), (4, 1536)):
            for kc in range(8):
                mm(psum[b][:, 0:512], hT_ap[:, kc, :], w_in_sb[:, kc, col0:col0 + 512], ("hT", "w_in"), (f"ps{b}",),
                   start=(kc == 0), stop=(kc == 7))
        cp("dve", vtok, psum[3][:, 0:512], ("ps3",), (kv,))
        act(gsil, psum[4][:, 0:512], AF.Silu, ("ps4",), (kg,))
        tt("dve", gsil.rearrange("p (h v) -> p h v", h=4), gsil.rearrange("p (h v) -> p h v", h=4),
           bc(C["gnorm_row"], 1, 4), ALU.mult, (kg, "gnorm_row"), (kg,))

    def gla_front(hT_ap, c0, mode, par, slot, part):
        vtok, gsil, Epl = P1["vtok"][slot], P1["gsil"][slot], P1["Epl"][par]
        QA, QB, attm, Psb = P1["QA"][par], P1["QB"][par], P1["attm"][par], P1["Psb"][par]
        kv, kg, ke, kqa, kqb, kat, kp = (f"vtok{slot}", f"gsil{slot}", f"Epl{par}", f"QA{par}", f"QB{par}", f"attm{par}",
                                         f"Psb{par}")
        if part == 1:
            return gla_front2(c0, mode, par, slot)
        mm(psum[5][:, 0:256], P1["zT"][:, c0:c0 + 128], C["wgate"], ("zT", "wgate"), ("ps5a",), start=True, stop=False)
        mm(psum[5][:, 0:256], C["ones_row"], C["bgate"], ("ones_row", "bgate"), ("ps5a",), start=False, stop=True)
        act(P1["lnv"], psum[5][:, 0:256], AF.Exp, ("ps5a",), ("lnv",), scale=-1.0)
        act(P1["lnv"], P1["lnv"], AF.Ln, ("lnv",), ("lnv",), bias=1.0)
        if mode == "s":
            ts("dve", P1["lnv"], P1["lnv"], C["padmask"], ALU.mult, ("lnv", "padmask"), ("lnv",))
        for hp in range(2):
            mm(psum[5][:, 256 + hp * 128:256 + (hp + 1) * 128], P1["lnv"][:, hp * 128:(hp + 1) * 128], C["tri_f"],
               ("lnv", "tri_f"), ("ps5b",))
        cT = psum[5][:, 256:512].rearrange("p (h t) -> p h t", h=2)
        act(Epl, cT, AF.Exp, ("ps5b",), (ke,), scale=-1.0 / 16)
        act(P1["Emi"], cT, AF.Exp, ("ps5b",), ("Emi",), scale=1.0 / 16)

    def gla_front2(c0, mode, par, slot):
        vtok, gsil, Epl = P1["vtok"][slot], P1["gsil"][slot], P1["Epl"][par]
        QA, QB, attm, Psb = P1["QA"][par], P1["QB"][par], P1["attm"][par], P1["Psb"][par]
        kv, kg, ke, kqa, kqb, kat, kp = (f"vtok{slot}", f"gsil{slot}", f"Epl{par}", f"QA{par}", f"QB{par}", f"attm{par}",
                                         f"Psb{par}")
        stt(P1["qeT"], P1["qf"][:, :, c0:c0 + 128], 0.125, Epl, ALU.mult, ALU.mult, ("qf", ke), ("qeT",))
        tt("dve", P1["keT"], P1["kf"][:, :, c0:c0 + 128], P1["Emi"], ALU.mult, ("kf", "Emi"), ("keT",))
        cp("act", QA[:, :, 0:64], P1["qeT"][:, :, 0:64], ("qeT",), (kqa,))
        cp("act", QB[:, :, 64:128], P1["qeT"][:, :, 64:128], ("qeT",), (kqb,))
        pb = psb(0)
        for hp in range(2):
            tr(pb[:, hp * 128:(hp + 1) * 128], P1["keT"][:, hp, :], C["ident_b"], ("keT", "ident_b"), ("ps0a",))
        cp("act", P1["ketok"], pb[:, 0:256], ("ps0a",), ("ketok",))
        for h in range(4):
            rows = slice(64 * (h % 2), 64 * (h % 2) + 64)
            bk = 6 if h % 2 == 0 else 3
            mm(psum[bk][:, (h // 2) * 128:(h // 2 + 1) * 128], P1["keT"][rows, h // 2, :], P1["qeT"][rows, h // 2, :],
               ("keT", "qeT"), (f"ps{bk}",))
        for h2 in range(2):
            bk = 6 if h2 == 0 else 3
            tt("dve", attm[:, h2:4:2, :], psum[bk][:, 0:256].rearrange("p (h i) -> p h i", h=2),
               bc(C["tri_f"], 1, 2), ALU.mult, (f"ps{bk}", "tri_f"), (f"{kat}_{h2}",))
        for X in range(2):
            trow = slice(64 * X, 64 * X + 64)
            for hp in range(2):
                mm(psum[1 + X][:, hp * 256:(hp + 1) * 256], P1["ketok"][trow, hp * 128:(hp + 1) * 128],
                   vtok[trow, hp * 256:(hp + 1) * 256], ("ketok", kv), (f"ps{1 + X}",))
            for h2 in range(2):
                rows = slice(64 * h2, 64 * h2 + 64)
                pv = psum[1 + X][rows, 0:512].rearrange("p (hp b v) -> p hp b v", hp=2, b=2)[:, :, h2, :]
                cp("act", Psb[rows, X], pv, (f"ps{1 + X}",), (f"{kp}_{X}{h2}",))

    def gla_back(mode, par, slot, tok0=None, seqA=None):
        vtok, gsil, Epl = P1["vtok"][slot], P1["gsil"][slot], P1["Epl"][par]
        QA, QB, attm, Psb = P1["QA"][par], P1["QB"][par], P1["attm"][par], P1["Psb"][par]
        kv, kg, ke, kqa, kqb, kat, kp = (f"vtok{slot}", f"gsil{slot}", f"Epl{par}", f"QA{par}", f"QB{par}", f"attm{par}",
                                         f"Psb{par}")
        on, kon = P1["on"][par], f"on{par}"
        Sf, Sblk = P1["Sf"], P1["Sblk"]
        if mode == "s":
            for X in range(2):
                dma(Sf[X], dap(I["sgla"], (seqA + X) * 32768, [[128, 128], [16384, 2], [1, 128]]), (), (f"Sf{X}",),
                    f"sf{X}")
                for h2 in range(2):
                    rows = slice(64 * h2, 64 * h2 + 64)
                    cp("act", Sblk[X][rows, :, h2, :], Sf[X][rows], (f"Sf{X}",), (f"Sblk{X}",))

        def upd(src_f, X, eL_col, dst_f, dst_blk, dkey_f, dkey_b, skey):
            tt("dve", P1["stmp"], Psb[:, X], src_f, ALU.add, (f"{kp}_{X}0", f"{kp}_{X}1", skey), ("stmp",))
            tt("dve", dst_f, P1["stmp"], bc(Epl[:, :, eL_col], 2, 128), ALU.mult, ("stmp", ke), (dkey_f,))
            if dst_blk is not None:
                for h2 in range(2):
                    rows = slice(64 * h2, 64 * h2 + 64)
                    cp("act", dst_blk[rows, :, h2, :], dst_f[rows], (dkey_f,), (dkey_b,))
        if mode == "p":
            upd(Sf[0], 0, 63, Sf[1], Sblk[1], "Sf1", "Sblk1", "Sf0")
        else:
            upd(Sf[0], 0, 63, P1["Fst"][0], None, "Fst0", None, "Sf0")
            upd(Sf[1], 1, 127, P1["Fst"][1], None, "Fst1", None, "Sf1")
        for hp in range(2):
            o_pair = psum[7][:, hp * 256:(hp + 1) * 256]
            mm(o_pair, QA[:, hp, :], Sblk[0][:, hp].rearrange("p a v -> p (a v)"), (kqa, "Sblk0"), ("ps7",),
               start=True, stop=False)
            mm(o_pair, QB[:, hp, :], Sblk[1][:, hp].rearrange("p a v -> p (a v)"), (kqb, "Sblk1"), ("ps7",),
               start=False, stop=False)
            for h in (2 * hp, 2 * hp + 1):
                mm(psum[7][:, h * 128:(h + 1) * 128], attm[:, h, :], vtok[:, h * 128:(h + 1) * 128],
                   (f"{kat}_{h % 2}", kv), ("ps7",), start=False, stop=(h == 2 * hp + 1))
        if mode == "p":
            upd(Sf[1], 1, 127, Sf[0], Sblk[0], "Sf0", "Sblk0", "Sf1")
        o4 = psum[7][:, 0:512].rearrange("p (h v) -> p h v", h=4)
        act(P1["osq"], o4, AF.Square, ("ps7",), ("osq",))
        ost = P1["st"][:, 8:12]
        fw.add("dve", lambda e: e.tensor_reduce(out=ost, in_=P1["osq"], axis=mybir.AxisListType.X, op=ALU.add),
               ("osq",), ("ost",))
        act(ost, ost, AF.Ln, ("ost", "eps_col"), ("ost",), scale=1.0 / 128, bias=C["eps_col"])
        act(ost, ost, AF.Exp, ("ost",), ("ost",), scale=-0.5)
        tt("dve", P1["osq"], o4, bc(ost, 2, 128), ALU.mult, ("ps7", "ost"), ("osq",))
        tt("dve", on, P1["osq"].rearrange("p h v -> p (h v)"), gsil, ALU.mult, ("osq", kg), (kon,))
        if mode == "s":
            for X in range(2):
                store(dap(O["glao"], (1 + seqA + X) * 32768, [[128, 128], [16384, 2], [1, 128]]), P1["Fst"][X],
                      (f"Fst{X}",), f"glao{X}")

    def gla_tail(mode, par, tok0=None, seqA=None):
        on, kon = P1["on"][par], f"on{par}"
        pb = psb(0)
        for h in range(4):
            tr(pb[:, 512 + h * 128:512 + (h + 1) * 128], on[:, h * 128:(h + 1) * 128], C["ident_b"],
               (kon, "ident_b"), ("ps0b",))
        cp("act", P1["mst"].rearrange("p h t -> p (h t)"), pb[:, 512:1024], ("ps0b",), ("mst",))
        if mode == "p":
            dma(mix_d[:, 4:8, tok0:tok0 + 128], P1["mst"], ("mst",), ("mix_d",), "mixw")
        else:
            for X in range(2):
                t0 = TP_ + (seqA + X) * 4
                dma(mix_d[:, 4:8, t0:t0 + 4], P1["mst"][:, :, 64 * X + 60:64 * X + 64], ("mst",), ("mix_d",), "mixw")

    def phase1a():
        fw.barrier()
        alloc_p1()
        for par in range(2):
            mset("pool", P1["QA"][par], 0.0, (f"QA{par}",))
            mset("pool", P1["QB"][par], 0.0, (f"QB{par}",))
        for X in range(2):
            mset("pool", P1["Sblk"][X], 0.0, (f"Sblk{X}",))
        mset("pool", P1["Sf"][0], 0.0, ("Sf0",))
        mset("pool", P1["Ys"][:, :, 0:4, :], 0.0, ("Ys",))
        mset("pool", P1["hTpad"], 0.0, ("hTpad",))
        Y = P1["Y"]
        pendB = []
        pendC = []
        tcount = [0]

        def capture(fn):
            n0 = len(fw.ops)
            fn()
            lst = fw.ops[n0:]
            del fw.ops[n0:]
            return lst

        def step(front=None, back=None, tail=None):
            lists = []
            if pendC:
                c = pendC.pop(0)
                lists.append(capture(c))
            if pendB:
                b, c = pendB.pop(0)
                lists.append(capture(b))
                pendC.append(c)
            if front is not None:
                lists.append(capture(lambda: (front(0), front(1))))
                pendB.append((back, tail))
            keyed = []
            for li, lst in enumerate(lists):
                n = len(lst)
                for i, op in enumerate(lst):
                    keyed.append(((i + 0.5) / n, li, i, op))
            keyed.sort(key=lambda t: (t[0], t[1], t[2]))
            fw.ops.extend(op for _, _, _, op in keyed)

        def drain():
            while pendB or pendC:
                step()

        def do_tile(hT_ap, c0, mode, slot, tok0=None, seqA=None):
            par = tcount[0] % 2
            tcount[0] += 1
            step(lambda part: gla_front(hT_ap, c0, mode, par, slot, part),
                 lambda: gla_back(mode, par, slot, tok0=tok0, seqA=seqA),
                 lambda: gla_tail(mode, par, tok0=tok0, seqA=seqA))
        vslot = [0]
        for a in range(2):
            for sti in range(2):
                for i in range(4):
                    lt = sti * 4 + i
                    pt = a * 8 + lt
                    norm_tile(I["xp"][pt * 128:(pt + 1) * 128, :], 128, P1["hT"][:, :, lt * 128:(lt + 1) * 128],
                              C["attn_col"])
                hT512 = P1["hT"][:, :, sti * 512:(sti + 1) * 512]
                proj_qkz(hT512, 512)
                slots = []
                for i in range(4):
                    lt = sti * 4 + i
                    sl_ = vslot[0] % 6
                    vslot[0] += 1
                    slots.append(sl_)
                    proj_vg(P1["hT"][:, :, lt * 128:(lt + 1) * 128], sl_)
                for i in range(4):
                    lt = sti * 4 + i
                    pt = a * 8 + lt
                    do_tile(P1["hT"][:, :, lt * 128:(lt + 1) * 128], i * 128, "p", slots[i], tok0=pt * 128)
            for jl in range(8):
                b = 1 + jl % 2
                for kc in range(8):
                    mm(psum[b][:, 0:512], P1["hT"][:, kc, jl:1024:8], w_in_sb[:, kc, 0:512], ("hT", "w_in"), (f"ps{b}",),
                       start=(kc == 0), stop=(kc == 7))
                cp("act", Y[:, a, :, jl, :], psum[b][:, 0:512].rearrange("p (g c) -> p g c", g=32), (f"ps{b}",), ("Y",))
        drain()
        store(dap(O["glao"], 0, [[128, 128], [16384, 2], [1, 128]]), P1["Sf"][0], ("Sf0",), "glaoP")
        norm_tile(I["xs"], 64, P1["hTs"], C["attn_col"])
        for tq in range(4):
            b = 1 + tq % 2
            for kc in range(8):
                mm(psum[b][0:16, 0:512], P1["hTs"][:, kc, tq:64:4], w_in_sb[:, kc, 0:512], ("hT", "w_in"), (f"ps{b}",),
                   start=(kc == 0), stop=(kc == 7))
            cp("act", P1["Ys"][:, :, 4 + tq, :], psum[b][0:16, 0:512].rearrange("p (g c) -> p g c", g=32), (f"ps{b}",),
               ("Ys",))
        hTp = P1["hT"]
        mset("dve", hTp, 0.0, ("hT",))
        cp("dve", hTp.rearrange("p k (s t) -> p k s t", s=16)[:, :, :, 60:64],
           P1["hTs"].rearrange("p k (s t) -> p k s t", s=16), ("hT",), ("hT",))
        for st_ in range(2):
            proj_qkz(hTp[:, :, st_ * 512:(st_ + 1) * 512], 512)
            slots = []
            for i in range(4):
                sl_ = vslot[0] % 6
                vslot[0] += 1
                slots.append(sl_)
                proj_vg(hTp[:, :, (st_ * 4 + i) * 128:(st_ * 4 + i + 1) * 128], sl_)
            for i in range(4):
                pst = st_ * 4 + i
                do_tile(hTp[:, :, pst * 128:(pst + 1) * 128], i * 128, "s", slots[i], seqA=2 * pst)
        drain()

    P2 = {}

    def phase1b():
        fw.barrier()
        REST.hi = ARENA_BYTES
        REST.reset(P1["P1_KEEP"])
        A = REST.alloc
        Y, Ys = P1["Y"], P1["Ys"]
        Ublk = A([128, 32, MCOL], BF16)
        Wreg_off = REST.top
        W = [A([128, 16, MCOL], F32) for _ in range(2)]
        Sprev = [A([128, 16, MCOL], BF16) for _ in range(2)]
        Sa = A([128, 2, 16, 16], F32); Sb = A([128, 2, 16, 16], F32); Sst = A([128, 2, 16, 16], F32)
        s0 = A([128, 2, 16, 16], F32); s0p = A([128, 2, 16, 16], F32); s0full = A([128, 2, 2048], F32)
        s0tok = s0full[0:16]
        tA = [A([128, 4, 16, 16], F32) for _ in range(4)]
        hA = [A([128, 16, 16], F32) for _ in range(4)]
        wglu = A([128, 4, 512], BF16)
        for kc in range(4):
            dma(wglu[:, kc, :], I["w_glu"][kc * 128:(kc + 1) * 128, :], (), ("wglu",), "G:wglu", eng="pool")
        dma(s0tok[:, 0, :], I["s5r"], (), ("s0tok",), "G:s0")
        dma(s0tok[:, 1, :], I["s5i"], (), ("s0tok",), "G:s0")
        cnt = 0
        for a in range(2):
            for gb in range(4):
                bk = cnt % 2
                pb = psb(bk)
                for gi in range(8):
                    g = gb * 8 + gi
                    tr(pb[:, gi * 128:(gi + 1) * 128], Y[:, a, g].rearrange("p j c -> p (j c)"), C["ident_b"],
                       ("Y", "ident_b"), (f"ps{bk}",))
                cp("act" if cnt % 2 == 0 else "dve", Ublk[:, gb * 8:(gb + 1) * 8, a * 128:(a + 1) * 128],
                   pb.rearrange("p (g m) -> p g m", g=8), (f"ps{bk}",), (f"Ub_{a}_{gb}",))
                cnt += 1
        pb = psb(0)
        for g in range(32):
            tr(pb[:, g * 16:(g + 1) * 16], Ys[:, g].rearrange("p j c -> p (j c)"), C["ident_b"][0:16, 0:16],
               ("Ys", "ident_b"), ("ps0",))
        cp("act", Ublk[:, :, 256:MCOL], pb[:, 0:512].rearrange("p (g m) -> p g m", g=32), ("ps0",), ("Ub_s",))
        fw.add("dve", None, tuple(f"Ub_{a}_{gb}" for a in range(2) for gb in range(4)) + ("Ub_s",), ("Ublk",))
        for ri in range(2):
            for gp in range(16):
                tr(psum[1][:, ri * 256 + gp * 16:ri * 256 + (gp + 1) * 16], s0tok[:, ri, gp * 128:(gp + 1) * 128],
                   C["ident_f"][0:16, 0:16], ("s0tok", "ident_f"), ("ps1",))
        cp("dve", s0.rearrange("p r g s -> p (r g s)"), psum[1][:, 0:512], ("ps1",), ("s0",))
        l4r = bc(C["Lm4"][:, 0, :], 2, 16); l4i = bc(C["Lm4"][:, 1, :], 2, 16)
        tt("dve", hA[0], s0[:, 0], l4r, ALU.mult, ("s0", "Lm4"), ("hA0",))
        tt("dve", hA[1], s0[:, 1], l4i, ALU.mult, ("s0", "Lm4"), ("hA1",))
        tt("dve", s0p[:, 0], hA[0], hA[1], ALU.subtract, ("hA0", "hA1"), ("s0p",))
        tt("dve", hA[0], s0[:, 0], l4i, ALU.mult, ("s0", "Lm4"), ("hA0",))
        tt("dve", hA[1], s0[:, 1], l4r, ALU.mult, ("s0", "Lm4"), ("hA1",))
        tt("dve", s0p[:, 1], hA[0], hA[1], ALU.add, ("hA0", "hA1"), ("s0p",))
        cnt = 0
        for gp in range(16):
            for ri in range(2):
                bk = 2 + cnt % 6
                cnt += 1
                for g2 in range(2):
                    g = 2 * gp + g2
                    mm(psum[bk][64 * g2:64 * g2 + 64, 0:MCOL], T["BL"][:, g, ri, :], Ublk[:, g, :], ("BL", "Ublk"),
                       (f"ps{bk}",))
                cp("act" if cnt % 2 == 0 else "dve", W[ri][:, gp, :], psum[bk][:, 0:MCOL], (f"ps{bk}",),
                   (f"Wc{ri}_{gp}",))
        for ri in range(2):
            fw.add("dve", None, tuple(f"Wc{ri}_{gp}" for gp in range(16)), (f"W{ri}_0", f"W{ri}s"))
        if stop_after == "p1b_W":
            dbg("d_W", W[0].rearrange("p g m -> p (g m)"), ("W0_0", "W0s"))
            dbg("d_Ublk", Ublk.rearrange("p g m -> p (g m)"), ("Ublk",))
            return
        Wv = [W[ri][:, :, 0:256].rearrange("p g (c k) -> p g c k", c=16) for ri in range(2)]
        l8r = bc(C["L8"][:, 0, :], 2, 16); l8i = bc(C["L8"][:, 1, :], 2, 16)

        def wk(ri, k):
            return f"W{ri}_{k}" if k > 0 else f"W{ri}_0"
        for ri in range(2):
            fw.add("dve", None, (f"W{ri}_0",), tuple(f"W{ri}_{k}" for k in range(1, 16)))
        for k in range(1, 16):
            pr, pi_ = Wv[0][:, :, :, k - 1], Wv[1][:, :, :, k - 1]
            kr = (wk(0, k - 1), wk(1, k - 1), "L8")
            tt("dve", hA[0], pr, l8r, ALU.mult, kr, ("hA0",))
            tt("dve", hA[1], pi_, l8i, ALU.mult, kr, ("hA1",))
            tt("dve", hA[2], pi_, l8r, ALU.mult, kr, ("hA2",))
            tt("dve", hA[3], pr, l8i, ALU.mult, kr, ("hA3",))
            tt("dve", Wv[0][:, :, :, k], Wv[0][:, :, :, k], hA[0], ALU.add, ("hA0", wk(0, k)), (wk(0, k),))
            tt("dve", Wv[1][:, :, :, k], Wv[1][:, :, :, k], hA[2], ALU.add, ("hA2", wk(1, k)), (wk(1, k),))
            tt("dve", Wv[0][:, :, :, k], Wv[0][:, :, :, k], hA[1], ALU.subtract, ("hA1", wk(0, k)), (wk(0, k),))
            tt("dve", Wv[1][:, :, :, k], Wv[1][:, :, :, k], hA[3], ALU.add, ("hA3", wk(1, k)), (wk(1, k),))
        allW = tuple(f"W{ri}_{k}" for ri in range(2) for k in range(16))
        cp("dve", Sa[:, 0], Wv[0][:, :, :, 15], (wk(0, 15),), ("Sa",))
        cp("dve", Sa[:, 1], Wv[1][:, :, :, 15], (wk(1, 15),), ("Sa",))
        cur, nxt, kc_, kn_ = Sa, Sb, "Sa", "Sb"
        for lev, d in enumerate((1, 2, 4, 8)):
            lr = bc(C["LK"][:, 0, lev, :], 2, 16 - d); li = bc(C["LK"][:, 1, lev, :], 2, 16 - d)
            cp("dve", nxt[:, 0], cur[:, 0], (kc_,), (kn_,))
            cp("dve", nxt[:, 1], cur[:, 1], (kc_,), (kn_,))
            sr, si = cur[:, 0, :, 0:16 - d], cur[:, 1, :, 0:16 - d]
            h0, h1, h2_, h3 = (hA[i][:, :, 0:16 - d] for i in range(4))
            tt("dve", h0, sr, lr, ALU.mult, (kc_, "LK"), ("hA0",))
            tt("dve", nxt[:, 0, :, d:16], nxt[:, 0, :, d:16], h0, ALU.add, ("hA0", kn_), (kn_,))
            tt("dve", h1, si, li, ALU.mult, (kc_, "LK"), ("hA1",))
            tt("dve", nxt[:, 0, :, d:16], nxt[:, 0, :, d:16], h1, ALU.subtract, ("hA1", kn_), (kn_,))
            tt("dve", h2_, si, lr, ALU.mult, (kc_, "LK"), ("hA2",))
            tt("dve", nxt[:, 1, :, d:16], nxt[:, 1, :, d:16], h2_, ALU.add, ("hA2", kn_), (kn_,))
            tt("dve", h3, sr, li, ALU.mult, (kc_, "LK"), ("hA3",))
            tt("dve", nxt[:, 1, :, d:16], nxt[:, 1, :, d:16], h3, ALU.add, ("hA3", kn_), (kn_,))
            cur, nxt, kc_, kn_ = nxt, cur, kn_, kc_
        Send, ke_ = cur, kc_
        mset("dve", Sst, 0.0, ("Sst",))
        cp("dve", Sst[:, :, :, 1:16], Send[:, :, :, 0:15], (ke_,), ("Sst",))
        cp("dve", C["fin"][:, :, :, 0], Send[:, :, :, 15], (ke_,), ("fin",))
        l8rs = bc(C["L8"][:, 0, :], 2, 16); l8is = bc(C["L8"][:, 1, :], 2, 16)
        tt("dve", hA[0], s0p[:, 0], l8rs, ALU.mult, ("s0p", "L8"), ("hA0",))
        tt("dve", hA[1], s0p[:, 1], l8is, ALU.mult, ("s0p", "L8"), ("hA1",))
        tt("dve", hA[0], hA[0], hA[1], ALU.subtract, ("hA0", "hA1"), ("hA0",))
        tt("dve", C["fin"][:, 0, :, 1:17], hA[0], W[0][:, :, 256:MCOL], ALU.add, ("hA0", "W0s"), ("fin",))
        tt("dve", hA[0], s0p[:, 0], l8is, ALU.mult, ("s0p", "L8"), ("hA0",))
        tt("dve", hA[1], s0p[:, 1], l8rs, ALU.mult, ("s0p", "L8"), ("hA1",))
        tt("dve", hA[0], hA[0], hA[1], ALU.add, ("hA0", "hA1"), ("hA0",))
        tt("dve", C["fin"][:, 1, :, 1:17], hA[0], W[1][:, :, 256:MCOL], ALU.add, ("hA0", "W1s"), ("fin",))
        for q in range(4):
            gs = slice(4 * q, 4 * q + 4)
            TPr = bc(C["TPW"][:, 0, gs, :], 2, 16); TPi = bc(C["TPW"][:, 1, gs, :], 2, 16)
            SR = bc(Sst[:, 0, gs, :], 3, 16); SI = bc(Sst[:, 1, gs, :], 3, 16)
            spr = Sprev[0][:, gs, 0:256].rearrange("p g (c k) -> p g c k", c=16)
            spi = Sprev[1][:, gs, 0:256].rearrange("p g (c k) -> p g c k", c=16)
            tt("dve", tA[0], TPr, SR, ALU.mult, ("TPW", "Sst"), ("tA0",))
            tt("dve", tA[1], TPi, SI, ALU.mult, ("TPW", "Sst"), ("tA1",))
            tt("dve", tA[2], TPr, SI, ALU.mult, ("TPW", "Sst"), ("tA2",))
            tt("dve", tA[3], TPi, SR, ALU.mult, ("TPW", "Sst"), ("tA3",))
            tt("dve", tA[0], tA[0], tA[1], ALU.subtract, ("tA0", "tA1"), ("tA0",))
            tt("dve", tA[2], tA[2], tA[3], ALU.add, ("tA2", "tA3"), ("tA2",))
            tt("dve", spr[:, :, :, 1:16], tA[0][:, :, :, 1:16], Wv[0][:, gs, :, 0:15], ALU.add, ("tA0",) + allW, ("Sp0",))
            tt("dve", spi[:, :, :, 1:16], tA[2][:, :, :, 1:16], Wv[1][:, gs, :, 0:15], ALU.add, ("tA2",) + allW, ("Sp1",))
            cp("dve", spr[:, :, :, 0], tA[0][:, :, :, 0], ("tA0",), ("Sp0",))
            cp("dve", spi[:, :, :, 0], tA[2][:, :, :, 0], ("tA2",), ("Sp1",))
        cp("dve", Sprev[0][:, :, 256:MCOL], s0p[:, 0], ("s0p",), ("Sp0",))
        cp("dve", Sprev[1][:, :, 256:MCOL], s0p[:, 1], ("s0p",), ("Sp1",))
        finT = s0full[0:17]
        for ri in range(2):
            for q4 in range(4):
                bk = 2 + q4
                for gq in range(4):
                    gp = q4 * 4 + gq
                    tr(psum[bk][0:17, gq * 128:(gq + 1) * 128], C["fin"][:, ri, gp, :], C["ident_f"], ("fin", "ident_f"),
                       (f"ps{bk}",))
                cp("act", finT[:, ri, q4 * 512:(q4 + 1) * 512], psum[bk][0:17, 0:512], (f"ps{bk}",), ("s0tok",))
            store(O["s5o_r" if ri == 0 else "s5o_i"], finT[:, ri, :], ("s0tok",), "s5o")
        if stop_after == "p1b_S":
            dbg("d_Sp", Sprev[0].rearrange("p g m -> p (g m)"), ("Sp0", "Sp1"))
            return
        Yg = arena_t[:, REST.lo:REST.lo + 32 * MCOL * 2].bitcast(BF16).rearrange("p (g m) -> p g m", g=32)
        fw.add("dve", None, (), ("Y", "Ys") + tuple(f"Yg{g}" for g in range(32)))
        for g in range(32):
            gp = g // 2
            bk = 2 + g % 6
            o = psum[bk][:, 0:MCOL]
            mm(o, T["T0"][:, g, :], Ublk[:, g, :], ("T0", "Ublk"), (f"ps{bk}",), start=True, stop=False)
            mm(o, T["CL"][:, g, 0, :], Sprev[0][:, gp, :], ("CL", "Sp0"), (f"ps{bk}",), start=False, stop=False)
            mm(o, T["CL"][:, g, 1, :], Sprev[1][:, gp, :], ("CL", "Sp1"), (f"ps{bk}",), start=False, stop=True)
            cp("act" if g % 2 == 0 else "dve", Yg[:, g, :], o, (f"ps{bk}",), (f"Yg{g}",))
        fw.add("dve", None, tuple(f"Yg{g}" for g in range(32)), ("Y", "Ys"))
        if stop_after == "p1b_Y":
            dbg("d_Yg", Yg.rearrange("p g m -> p (g m)"), ("Y",))
            return
        Z = arena_t[:, Wreg_off:Wreg_off + 2 * 8 * 512 * 2].bitcast(BF16).rearrange("p (a i n) -> p a i n", a=2, i=8)
        Zs = arena_t[:, Wreg_off + 16384:Wreg_off + 16384 + 8 * 512 * 2].bitcast(BF16).rearrange(
            "p (i n) -> p i n", i=8)[0:16]
        ZK = allW + ("W0s", "W1s")
        ZF = tuple(f"Zc_{a}_{gb}" for a in range(2) for gb in range(4)) + tuple(f"Zs_{gb}" for gb in range(4))
        fw.add("dve", None, (), ZK + ZF)
        cnt = 0
        for a in range(2):
            for gb in range(4):
                bk = cnt % 2
                pb = psb(bk)
                for gi in range(8):
                    g = gb * 8 + gi
                    tr(pb[:, gi * 128:(gi + 1) * 128], Yg[:, g, a * 128:(a + 1) * 128], C["ident_b"], ("Y", "ident_b"),
                       (f"ps{bk}",))
                cp("act" if cnt % 2 == 0 else "dve",
                   Z[:, a, :, gb * 128:(gb + 1) * 128].rearrange("p i (g c) -> p i g c", g=8),
                   pb.rearrange("p (g i c) -> p i g c", g=8, i=8), (f"ps{bk}",), (f"Zc_{a}_{gb}",))
                cnt += 1
        for gb in range(4):
            bk = cnt % 2
            pb = psb(bk)
            for gi in range(8):
                g = gb * 8 + gi
                tr(pb[0:16, gi * 128:(gi + 1) * 128], Yg[:, g, 256:MCOL], C["ident_b"], ("Y", "ident_b"), (f"ps{bk}",))
            cp("act" if cnt % 2 == 0 else "dve", Zs[:, :, gb * 128:(gb + 1) * 128].rearrange("p i (g c) -> p i g c", g=8),
               pb[0:16].rearrange("p (g i c) -> p i g c", g=8, i=8), (f"ps{bk}",), (f"Zs_{gb}",))
            cnt += 1
        fw.add("dve", None, ZF, ZK)
        y5T = Ublk.rearrange("p g m -> p (g m)")[:, 0:4 * (TP_ + TS_)].rearrange("p (q t) -> p q t", q=4)
        fw.add("dve", None, (), ("Ublk",) + tuple(f"y5c_{a}_{q}" for a in range(2) for q in range(4)) + ("y5c_s",))
        for a in range(2):
            for q in range(4):
                bk = cnt % 2
                pb = psb(bk)
                for il in range(8):
                    tr(pb[:, il * 128:(il + 1) * 128], Z[:, a, il, q * 128:(q + 1) * 128], C["ident_b"], ZK + ("ident_b",),
                       (f"ps{bk}",))
                cp("act" if cnt % 2 == 0 else "dve", y5T[:, q, a * 1024:(a + 1) * 1024].rearrange("p (m i) -> p i m", i=8),
                   pb.rearrange("p (i m) -> p i m", i=8), (f"ps{bk}",), (f"y5c_{a}_{q}",))
                cnt += 1
        bk = cnt % 2
        pb = psb(bk)
        for q in range(4):
            for tq in range(4):
                tr(pb[:, (q * 4 + tq) * 16:(q * 4 + tq + 1) * 16], Zs[:, 4 + tq, q * 128:(q + 1) * 128],
                   C["ident_b"][0:16, 0:16], ZK + ("ident_b",), (f"ps{bk}",))
        cp("act", y5T[:, :, TP_:TP_ + TS_].rearrange("p q (s t) -> p q t s", t=4),
           pb[:, 0:256].rearrange("p (q t s) -> p q t s", q=4, t=4), (f"ps{bk}",), ("y5c_s",))
        fw.add("dve", None, tuple(f"y5c_{a}_{q}" for a in range(2) for q in range(4)) + ("y5c_s",), ("Ublk",))
        if stop_after == "p1b_T":
            dbg("d_y5T", y5T.rearrange("p q t -> p (q t)"), ("Ublk",))
            return
        prefetch_p2_weights()
        G_ = {}
        GB = 256
        G_["y5"] = A([128, 4, GB], BF16); G_["sg"] = A([128, GB], F32); G_["y5g"] = A([128, 4, GB], F32)
        G_["sq"] = A([128, 4, GB], BF16); G_["rstd"] = A([128, GB], F32); G_["mixS"] = A([128, 4, GB], BF16)
        blocks = [(c0, GB) for c0 in range(0, TP_, GB)] + [(TP_, TS_)]
        for (c0, n) in blocks:
            act(G_["y5"][:, :, 0:n], y5T[:, :, c0:c0 + n], AF.Gelu_apprx_tanh, ("Ublk",), ("y5",))
            for qo in range(4):
                bk = 2 + qo % 2
                for qi in range(4):
                    mm(psum[bk][:, 0:n], wglu[:, qi, qo * 128:(qo + 1) * 128], G_["y5"][:, qi, 0:n], ("wglu", "y5"),
                       (f"ps{bk}",), start=(qi == 0), stop=(qi == 3))
                act(G_["sg"][:, 0:n], psum[bk][:, 0:n], AF.Sigmoid, (f"ps{bk}", "bglu_col"), ("sg",),
                    bias=C["bglu_col"][:, qo:qo + 1])
                tt("dve", G_["y5g"][:, qo, 0:n], G_["y5"][:, qo, 0:n], G_["sg"][:, 0:n], ALU.mult, ("y5", "sg"),
                   (f"y5g{qo}",))
                act(G_["sq"][:, qo, 0:n], G_["y5g"][:, qo, 0:n], AF.Square, (f"y5g{qo}",), (f"sq{qo}",))
            for qo in range(4):
                mm(psum[4][:, 0:n], C["ones_b"], G_["sq"][:, qo, 0:n], ("ones_b", f"sq{qo}"), ("ps4",), start=(qo == 0),
                   stop=(qo == 3))
            ts("dve", G_["rstd"][:, 0:n], psum[4][:, 0:n], 1.0 / 512, ALU.mult, ("ps4",), ("rstd",), s2=EPS, op1=ALU.add)
            act(G_["rstd"][:, 0:n], G_["rstd"][:, 0:n], AF.Sqrt, ("rstd",), ("rstd",))
            recip(G_["rstd"][:, 0:n], G_["rstd"][:, 0:n], ("rstd",), ("rstd",))
            for qo in range(4):
                stt(G_["mixS"][:, qo, 0:n], G_["y5g"][:, qo, 0:n], C["s5n_col"][:, qo:qo + 1], G_["rstd"][:, 0:n], ALU.mult,
                    ALU.mult, (f"y5g{qo}", "rstd", "s5n_col"), ("mixS",))
            dma(mix_d[:, 0:4, c0:c0 + n], G_["mixS"][:, :, 0:n], ("mixS",), ("mix_d",), "mixw")

    PH = Arena(arena_t, 14 * 1024, ARENA_BYTES)
    w_o = PH.alloc([128, 8, D], BF16)
    w_up = PH.alloc([128, 8, 2 * DFF], BF16)
    w_dn = PH.alloc([128, NF, D], BF16)
    LATE_KC = (3, 4, 5)

    def p2_weight_dmas(late):
        if not late:
            for kc in range(8):
                dma(w_o[:, kc, :], I["w_o"][kc * 128:(kc + 1) * 128, :], (), ("w_o",), "G:w_o", eng="pool",
                    max_dma_last_dim=4096)
        for kc in range(8):
            if (kc in LATE_KC) != late:
                continue
            dma(w_up[:, kc, :], I["w_up"][kc * 128:(kc + 1) * 128, :], (), (f"wuk{kc}",), f"G:wuk{kc}", eng="pool",
                max_dma_last_dim=4096)
        if not late:
            for f in range(NF):
                dma(w_dn[:, f, :], I["w_down"][f * 128:(f + 1) * 128, :], (), ("w_dn",), "G:w_dn", eng="pool",
                    max_dma_last_dim=4096)

    def prefetch_p2_weights():
        fw.barrier(engs=("pool",))
        p2_weight_dmas(late=False)

    def phase2():
        fw.barrier()
        A = PH.alloc
        NTT = 256
        xbs = [A([128, D], F32) for _ in range(3)]
        mt = A([128, 8, NTT], BF16)
        hb = A([128, D], BF16)
        h2T = A([128, 8, NTT], BF16)
        aext = [A([128, NTT + 8], F32) for _ in range(2)]
        ACC_OFF = (PH.top + 63) // 64 * 64
        acc = [A([128, NTT], F32) for _ in range(2)]
        junk2 = arena_t[:, ACC_OFF:ACC_OFF + 2 * NTT * 4].bitcast(BF16)
        sil = [A([128, NTT], BF16) for _ in range(2)]
        gT = A([128, NF, NTT], BF16)
        halo2 = [A([128, NF, 2], F32) for _ in range(2)]
        atail = arena_t[:, TPW_OFF:TPW_OFF + NF * 34 * 4].bitcast(F32).rearrange("p (f n) -> p f n", f=NF)
        fragA = arena_t[:, TPW_OFF + 2992:TPW_OFF + 2992 + 10 * 128].bitcast(F32).rearrange("p (f n) -> p f n", f=10)
        ctop = (CONST.top + 63) // 64 * 64
        fragB = arena_t[:, ctop:ctop + 8 * 128].bitcast(F32).rearrange("p (f n) -> p f n", f=8)
        assert ctop + 8 * 128 <= 14 * 1024 and 2992 + 10 * 128 <= 4352
        fragC = A([128, 4, 32], F32)
        conv0T = [fragA[:, f, :] for f in range(10)] + [fragB[:, f, :] for f in range(8)] + [fragC[:, f, :] for f in range(4)]
        xbs.append(A([128, D], F32))
        st2 = A([128, 16], F32)
        gT_off_bytes = None
        c0tok = gT.rearrange("p f n -> p (f n)").bitcast(F32)[0:32, 0:DFF]
        tailT = gT.rearrange("p f n -> p (f n)").bitcast(F32)[0:34, 0:DFF]
        p2_weight_dmas(late=True)
        GTK = tuple(f"gT{f}" for f in range(NF))
        dma(c0tok, I["sconv"], (), GTK, "c0")
        for f in range(NF):
            bk = 5 + f // 16
            tr(psum[bk][:, (f % 16) * 32:(f % 16 + 1) * 32], c0tok[:, f * 128:(f + 1) * 128], C["ident_f"][0:32, 0:32],
               GTK + ("ident_f",), (f"ps{bk}",))
        cp("dve", fragA, psum[5][:, 0:320].rearrange("p (f n) -> p f n", f=10), ("ps5",), ("conv0T",))
        cp("dve", fragB[:, 0:6, :], psum[5][:, 320:512].rearrange("p (f n) -> p f n", f=6), ("ps5",), ("conv0T",))
        cp("dve", fragB[:, 6:8, :], psum[6][:, 0:64].rearrange("p (f n) -> p f n", f=2), ("ps6",), ("conv0T",))
        cp("dve", fragC, psum[6][:, 64:192].rearrange("p (f n) -> p f n", f=4), ("ps6",), ("conv0T",))
        for hp_ in range(2):
            mset("dve", halo2[hp_], 0.0, tuple(f"halo{hp_}_{f}" for f in range(NF)))

        def rms_rstd(src, n, slot, xk):
            ss = st2[0:n, slot * 2:slot * 2 + 1]
            rs = st2[0:n, slot * 2 + 1:slot * 2 + 2]
            k = f"st2_{slot}"
            if slot == 0:
                junk, jk = hb[0:n], "hb"
            else:
                junk, jk = junk2[0:n, :], "acc0"
            fw.add("act", lambda e: e.activation(out=junk, in_=src, func=AF.Square, accum_out=ss), (xk,),
                   (jk, k) if slot == 0 else ("acc0", "acc1", k))
            ts("dve", rs, ss, 1.0 / D, ALU.mult, (k,), (k,), s2=EPS, op1=ALU.add)
            tt("pool", rs, rs, C["mhalf"][0:n], ALU.pow, (k, "mhalf"), (k,))
            return rs, k

        tiles = [(t * NTT, NTT, "p") for t in range(TP_ // NTT)] + [(TP_, TS_, "s")]
        xslot = {}
        xctr2 = [0]

        def tinfo(ti):
            tok0, NT, kind = tiles[ti]
            nsub = (NT + 127) // 128
            return tok0, NT, kind, [(sb, min(128, NT - sb * 128)) for sb in range(nsub)]

        def pro(ti, sb):
            tok0, NT, kind, subs = tinfo(ti)
            n = subs[sb][1]
            k_ = xctr2[0] % 4
            xctr2[0] += 1
            xslot[(ti, sb)] = k_
            xb, xk = xbs[k_], f"xb{k_}"
            src = I["xp"][tok0 + sb * 128:tok0 + sb * 128 + n, :] if kind == "p" else I["xs"]
            dma(xb[0:n, :], src, (), (xk,), f"xb{k_}")
            if sb == 0:
                dma(mt[:, :, 0:NT], mix_d[:, :, tok0:tok0 + NT], ("mix_d",), ("mt",), "mt")
            cs = slice(sb * 128, sb * 128 + n)
            for half in range(2):
                for kc in range(8):
                    mm(psum[half][0:n, 0:512], mt[:, kc, cs], w_o[:, kc, half * 512:(half + 1) * 512], ("mt", "w_o"),
                       (f"ps{half}",), start=(kc == 0), stop=(kc == 7))
                tt("dve", xb[0:n, half * 512:(half + 1) * 512], psum[half][0:n, 0:512],
                   xb[0:n, half * 512:(half + 1) * 512], ALU.add, (f"ps{half}", xk), (xk,))
            rs, k = rms_rstd(xb[0:n, :], n, 0, xk)
            xsrc = xb[0:n, :]
            fw.add("act", lambda e, xsrc=xsrc, rs=rs, n=n: e.activation(out=hb[0:n], in_=xsrc, func=AF.Copy, scale=rs),
                   (xk, k), ("hb",))

            def part_b():
                pb = psb(7)
                for kc in range(8):
                    tr(pb[:, kc * 128:kc * 128 + n], hb[0:n, kc * 128:(kc + 1) * 128], C["ident_b"][0:n, 0:n],
                       ("hb", "ident_b"), ("ps7",))
                tt("dve", h2T[:, :, cs], pb.rearrange("p (k n) -> p k n", k=8)[:, :, 0:n], bc(C["ffn_col"], 2, n),
                   ALU.mult, ("ps7", "ffn_col"), ("h2T",))
            return part_b

        def epi(ti, sb):
            tok0, NT, kind, subs = tinfo(ti)
            n = subs[sb][1]
            k_ = xslot[(ti, sb)]
            xb, xk = xbs[k_], f"xb{k_}"
            cs = slice(sb * 128, sb * 128 + n)
            for half in range(2):
                for f in range(NF):
                    mm(psum[half][0:n, 0:512], gT[:, f, cs], w_dn[:, f, half * 512:(half + 1) * 512], (f"gT{f}", "w_dn"),
                       (f"ps{half}",), start=(f == 0), stop=(f == NF - 1))
                tt("dve", xb[0:n, half * 512:(half + 1) * 512], psum[half][0:n, 0:512],
                   xb[0:n, half * 512:(half + 1) * 512], ALU.add, (f"ps{half}", xk), (xk,))
            rs, k = rms_rstd(xb[0:n, :], n, 1, xk)
            xsrc = xb[0:n, :]
            fw.add("act", lambda e, xsrc=xsrc, rs=rs: e.activation(out=xsrc, in_=xsrc, func=AF.Copy, scale=rs),
                   (xk, k), (xk,))
            tt("dve", xsrc, xsrc, C["fin_row"][0:n], ALU.mult, (xk, "fin_row"), (xk,))
            dst = O["yp"][tok0 + sb * 128:tok0 + sb * 128 + n, :] if kind == "p" else O["ys"]
            store(dst, xsrc, (xk,), f"y{k_}")

        def ffn_up(ti):
            tok0, NT, kind, subs = tinfo(ti)
            cw = C["cw_col"]
            last_prompt = (ti == len(tiles) - 2)

            def views(f):
                sl = f % 2
                bank = 2 + f % 5
                ae, ac, si = aext[sl], acc[sl], sil[sl]
                psA = psum[bank][:, 0:NT]
                psB = psum[bank][:, 256:256 + NT]
                if kind == "p":
                    v2, v1, v0 = ae[:, 2:2 + NT], ae[:, 1:1 + NT], ae[:, 0:NT]
                    acv, pav = ac[:, 0:NT], psA
                else:
                    ae3 = ae[:, 0:96].rearrange("p (s w) -> p s w", w=6)
                    v2, v1, v0 = ae3[:, :, 2:6], ae3[:, :, 1:5], ae3[:, :, 0:4]
                    acv = ac[:, 0:NT].rearrange("p (s t) -> p s t", t=4)
                    pav = psA.rearrange("p (s t) -> p s t", t=4)
                return sl, bank, ae, ac, si, psA, psB, v2, v1, v0, acv, pav

            def stA(f):
                sl, bank, ae, ac, si, psA, psB, v2, v1, v0, acv, pav = views(f)
                ak, ck, pk = f"aext{sl}", f"acc{sl}", f"ps{bank}"
                for (c0p, off) in ((0, 0), (256, DFF)):
                    for kc in range(8):
                        mm(psum[bank][:, c0p:c0p + NT], w_up[:, kc, off + f * 128:off + (f + 1) * 128], h2T[:, kc, 0:NT],
                           (f"wuk{kc}", "h2T"), (pk,), start=(kc == 0), stop=(kc == 7))
                if kind == "p":
                    cp("dve", ae[:, 0:2], halo2[ti % 2][:, f, :], (f"halo{ti % 2}_{f}",), (ak + "h",))
                    fw.add("act", lambda e, o_=ae[:, 2:2 + NT], i_=psA: e.copy(out=o_, in_=i_), (pk,), (ak,), weak=(pk,))
                    cp("act", halo2[(ti + 1) % 2][:, f, :], ae[:, NT:NT + 2], (ak,), (f"halo{(ti + 1) % 2}_{f}",))
                    if last_prompt:
                        cp("act", atail[:, f, 0:2], ae[:, NT:NT + 2], (ak,), (f"atail{f}",))
                else:
                    ae3 = ae[:, 0:96].rearrange("p (s w) -> p s w", w=6)
                    cp("dve", ae3[:, :, 0:2], conv0T[f].rearrange("p (s w) -> p s w", w=2), ("conv0T",), (ak + "h",))
                    cp("act", ae3[:, :, 2:6], pav, (pk,), (ak,))
                    cp("act", atail[:, f, 2:34].rearrange("p (s w) -> p s w", w=2), ae3[:, :, 4:6], (ak,), (f"atail{f}",))
                fw.add("act", lambda e, acv=acv, pav=pav, f=f: e.activation(
                    out=acv, in_=pav, func=AF.Identity, scale=cw[:, 2, f:f + 1], bias=C["cb_col"][:, f:f + 1]),
                    (pk, "cw_col", "cb_col"), (ck,), weak=(pk,))

            def stB(f):
                sl, bank, ae, ac, si, psA, psB, v2, v1, v0, acv, pav = views(f)
                ak, ck = f"aext{sl}", f"acc{sl}"
                stt(acv, v1, cw[:, 1, f:f + 1], acv, ALU.mult, ALU.add, (ak, ak + "h", ck, "cw_col"), (ck,))
                stt(acv, v0, cw[:, 0, f:f + 1], acv, ALU.mult, ALU.add, (ak, ak + "h", ck, "cw_col"), (ck,))

            def stC(f):
                sl, bank, ae, ac, si, psA, psB, v2, v1, v0, acv, pav = views(f)
                act(si[:, 0:NT], ac[:, 0:NT], AF.Silu, (f"acc{sl}",), (f"sil{sl}",))

            def stD(f):
                sl, bank, ae, ac, si, psA, psB, v2, v1, v0, acv, pav = views(f)
                tt("dve", gT[:, f, 0:NT], si[:, 0:NT], psB, ALU.mult, (f"sil{sl}", f"ps{bank}"), (f"gT{f}",))

            for step in range(NF + 3):
                for (stg, lag) in ((stD, 3), (stC, 2), (stB, 1), (stA, 0)):
                    f = step - lag
                    if 0 <= f < NF:
                        stg(f)

        for sb, _ in tinfo(0)[3]:
            pro(0, sb)()
        for ti in range(len(tiles)):
            ffn_up(ti)
            cur = tinfo(ti)[3]
            nxt = tinfo(ti + 1)[3] if ti + 1 < len(tiles) else []
            for j in range(max(len(cur), len(nxt))):
                pbf = pro(ti + 1, j) if j < len(nxt) else None
                if j < len(cur):
                    epi(ti, j)
                if pbf is not None:
                    pbf()
        for f in range(NF):
            bk = 3 + (f // 4) % 4
            tr(psum[bk][0:34, (f % 4) * 128:(f % 4 + 1) * 128], atail[:, f, :], C["ident_f"], (f"atail{f}", "ident_f"),
               (f"ps{bk}",))
            if f % 4 == 3 or f == NF - 1:
                f0 = f - f % 4
                nn = (f - f0 + 1) * 128
                cp("act", tailT[:, f0 * 128:f0 * 128 + nn], psum[bk][0:34, 0:nn], (f"ps{bk}",), GTK)
        store(O["convo"], tailT, GTK, "convo")

    def dbg_rows(name, ap, r0, n):
        store(O[name][r0:r0 + n, :], ap, ("xb",), name)

    load_consts()
    if stop_after != "consts":
        load_w_in()
        phase0()
    if stop_after is None or stop_after.startswith("p1") or stop_after.startswith("g"):
        phase1a()
        if "d_Y" in DEBUG:
            dbg("d_Y", P1["Y"].rearrange("p a g j c -> p (a g j c)"), ("Y",))
            dbg("d_Ys", P1["Ys"].rearrange("p g j c -> p (g j c)"), ("Ys",))
        if stop_after is None or stop_after.startswith("p1b") or stop_after.startswith("p2"):
            phase1b()
        if "d_mix" in DEBUG:
            fw.add("sp", None, ("mix_d",), ("OUT_mixd",))
            out_chans.append("o_mixd")
        if stop_after is None or stop_after.startswith("p2"):
            phase2()

    fw.add("sp", None, tuple("OUT_" + c[2:] for c in out_chans), ())
    fw.finalize()
    fw.simulate()
    block = st.enter_context(nc.Block())
    fw.emit(nc, st, block)
    st.close()
    return nc


def make_in_maps(inputs):
    f = lambda a: np.ascontiguousarray(np.asarray(a, dtype=np.float32))
    shared = {
        "attn_norm": f(inputs["attn_norm"][0]), "w_in": f(inputs["w_in"][0]),
        "A_re": f(inputs["s5_A_re"][0]), "A_im": f(inputs["s5_A_im"][0]),
        "B_re": f(inputs["s5_B_re"][0]), "B_im": f(inputs["s5_B_im"][0]),
        "C_re": f(inputs["s5_C_re"][0]), "C_im": f(inputs["s5_C_im"][0]),
        "Dp": f(inputs["s5_D"][0]), "log_step": f(inputs["s5_log_step"][0]),
        "w_glu": f(inputs["w_glu"][0]), "b_glu": f(inputs["b_glu"][0]),
        "s5_out_norm": f(inputs["s5_out_norm"][0]), "w_gate_up": f(inputs["w_gate_up"][0]),
        "b_gate": f(inputs["b_gate"][0]), "gla_out_norm": f(inputs["gla_out_norm"][0]),
        "w_o": f(inputs["w_o"][0]), "ffn_norm": f(inputs["ffn_norm"][0]), "w_up": f(inputs["w_up"][0]),
        "conv_w": f(inputs["conv_w"][0]), "conv_b": f(inputs["conv_b"][0]), "w_down": f(inputs["w_down"][0]),
        "final_norm": f(inputs["final_norm"]),
    }
    maps = []
    for c in range(NCORES):
        m = dict(shared)
        sl = slice(NSEQ * c, NSEQ * (c + 1))
        m["xp"] = f(inputs["x_prompt"][c])
        m["xs"] = f(inputs["x_sample"][sl]).reshape(TS_, D)
        m["s5r"] = f(inputs["state_s5_re"][0, sl]).reshape(NSEQ, 2048)
        m["s5i"] = f(inputs["state_s5_im"][0, sl]).reshape(NSEQ, 2048)
        m["sgla"] = f(inputs["state_gla"][0, sl])
        m["sconv"] = f(inputs["state_conv"][0, sl]).reshape(2 * NSEQ, DFF)
        maps.append(m)
    return maps


_NC_CACHE = {}


def kernel(**inputs):
    if "nc" not in _NC_CACHE:
        _NC_CACHE["nc"] = build()
    nc = _NC_CACHE["nc"]
    in_maps = make_in_maps(inputs)
    res = run_bass_kernel_spmd(nc, in_maps, core_ids=list(range(NCORES)))
    R = res.results
    yp = np.stack([R[c]["yp"] for c in range(NCORES)]).reshape(8, 2048, D)
    ys = np.concatenate([R[c]["ys"].reshape(NSEQ, 4, D) for c in range(NCORES)], 0)
    p_re = np.stack([R[c]["s5o_r"][0].reshape(32, 64) for c in range(NCORES)])[None]
    p_im = np.stack([R[c]["s5o_i"][0].reshape(32, 64) for c in range(NCORES)])[None]
    s_re = np.concatenate([R[c]["s5o_r"][1:].reshape(NSEQ, 32, 64) for c in range(NCORES)], 0)[None]
    s_im = np.concatenate([R[c]["s5o_i"][1:].reshape(NSEQ, 32, 64) for c in range(NCORES)], 0)[None]
    p_gla = np.stack([R[c]["glao"][0] for c in range(NCORES)])[None]
    s_gla = np.concatenate([R[c]["glao"][1:] for c in range(NCORES)], 0)[None]
    p_conv = np.stack([R[c]["convo"][0:2] for c in range(NCORES)])[None]
    s_conv = np.concatenate([R[c]["convo"][2:].reshape(NSEQ, 2, DFF) for c in range(NCORES)], 0)[None]
    out = (yp, ys, p_re, p_im, p_gla, p_conv, s_re, s_im, s_gla, s_conv)
    return tuple(np.ascontiguousarray(o, dtype=np.float32) for o in out)
```
